# Optimizing a Trainium2 kernel written in Bass

```python
import jax, jax.numpy as jnp
from jax import lax
import numpy as np

D_MODEL = 2048
BATCH = 2
SEQ = 16384
DEPTH = 4

CONV_CH = 512
CONV_K = 31
SC_CH = 512
SC_K = 3
N_HEADS = 12
HEAD_DIM = 64
ATT_W = N_HEADS * HEAD_DIM
DIL_GROUPS = ((128, 1), (512, 4), (2048, 16))
HEADS_PER_GROUP = N_HEADS // len(DIL_GROUPS)
ATT_OUT = HEADS_PER_GROUP * HEAD_DIM
Q_BLOCK = 64
N_BRANCH = 3
D_FF = -(-8 * D_MODEL // (3 * 256)) * 256
N_IN = 2 * CONV_CH + 3 * SC_CH + 3 * ATT_W + N_BRANCH * D_MODEL
NORM_EPS = 1e-6
LN_EPS = 1e-5
NEG = -1e30

kernel_name = "hybrid_gated_conformer_shortconv_dilated_alibi_encoder"


def rmsnorm(x, g):
    xf = x.astype(jnp.float32)
    y = xf * lax.rsqrt(jnp.mean(xf * xf, axis=-1, keepdims=True) + NORM_EPS)
    return (y * g.astype(jnp.float32)).astype(x.dtype)


def layernorm(x, g, b):
    xf = x.astype(jnp.float32)
    mu = jnp.mean(xf, axis=-1, keepdims=True)
    var = jnp.mean(jnp.square(xf - mu), axis=-1, keepdims=True)
    y = (xf - mu) * lax.rsqrt(var + LN_EPS)
    return (y * g.astype(jnp.float32) + b.astype(jnp.float32)).astype(x.dtype)


def depthwise_conv(x, w, pad):
    c = x.shape[-1]
    return lax.conv_general_dilated(
        x, w[:, None, :].astype(x.dtype), window_strides=(1,), padding=[(pad, pad)],
        dimension_numbers=("NWC", "WIO", "NWC"), feature_group_count=c)


def alibi_slopes(n):
    return (2.0 ** (-8.0 * np.arange(1, n + 1) / n)).astype(np.float32)


def dilated_window_attention(q, k, v, slopes, dilation, half):
    bt, s, h, dh = q.shape
    L = s // dilation
    nb = -(-L // Q_BLOCK)
    lp = nb * Q_BLOCK
    kb_len = Q_BLOCK + 2 * half

    def split(t):
        return t.reshape(bt, L, dilation, h, dh).transpose(0, 3, 2, 1, 4)

    qs, ks, vs = split(q), split(k), split(v)
    qs = jnp.pad(qs, ((0, 0), (0, 0), (0, 0), (0, lp - L), (0, 0)))
    pad_k = ((0, 0), (0, 0), (0, 0), (half, lp - L + half), (0, 0))
    ks = jnp.pad(ks, pad_k)
    vs = jnp.pad(vs, pad_k)
    idx = np.arange(nb)[:, None] * Q_BLOCK + np.arange(kb_len)[None, :]
    kb = jnp.take(ks, idx, axis=3)
    vb = jnp.take(vs, idx, axis=3)
    qb = qs.reshape(bt, h, dilation, nb, Q_BLOCK, dh)
    scores = jnp.einsum("bhrnqd,bhrnkd->bhrnqk", qb.astype(jnp.float32),
                        kb.astype(jnp.float32)) * (HEAD_DIM ** -0.5)
    rel = np.arange(kb_len)[None, :] - half - np.arange(Q_BLOCK)[:, None]
    kpos = idx - half
    valid = (np.abs(rel) <= half)[None] & ((kpos >= 0) & (kpos < L))[:, None, :]
    dist = (dilation * np.abs(rel)).astype(np.float32)
    bias = -slopes[:, None, None] * dist[None]
    scores = jnp.where(valid, scores + bias[:, None, None], NEG)
    lse = jax.nn.logsumexp(scores, axis=-1)
    p = jnp.exp(scores - lse[..., None])
    out = jnp.einsum("bhrnqk,bhrnkd->bhrnqd", p.astype(v.dtype), vb)
    out = out.reshape(bt, h, dilation, lp, dh)[:, :, :, :L]
    out = out.transpose(0, 3, 2, 1, 4).reshape(bt, s, h, dh)
    lse = lse.reshape(bt, h, dilation, lp)[..., :L].transpose(0, 3, 2, 1).reshape(bt, s, h)
    return out, lse


def mixture_dilated_attention(q, k, v):
    slopes_all = alibi_slopes(N_HEADS)
    outs, lses = [], []
    for g, (window, dilation) in enumerate(DIL_GROUPS):
        sl = slice(g * HEADS_PER_GROUP, (g + 1) * HEADS_PER_GROUP)
        half = window // (2 * dilation)
        o, l = dilated_window_attention(q[:, :, sl], k[:, :, sl], v[:, :, sl],
                                        slopes_all[sl], dilation, half)
        outs.append(o)
        lses.append(l)
    w = jax.nn.softmax(jnp.stack(lses, axis=0), axis=0)
    o = jnp.sum(w[..., None] * jnp.stack(outs, axis=0).astype(jnp.float32), axis=0)
    bt, s = q.shape[0], q.shape[1]
    return o.reshape(bt, s, ATT_OUT).astype(q.dtype)


def setup_inputs(seed: int = 0) -> dict:
    key = jax.random.key(seed)
    ks = jax.random.split(key, 20)

    def nrm(k, shape, fan_in):
        return jax.random.normal(k, shape, jnp.float32) * (fan_in ** -0.5)

    def gain(k, shape):
        return 1.0 + 0.02 * jax.random.normal(k, shape, jnp.float32)

    def small(k, shape):
        return 0.02 * jax.random.normal(k, shape, jnp.float32)

    return {
        "x": jax.random.normal(ks[0], (BATCH, SEQ, D_MODEL), jnp.float32),
        "norm1_g": gain(ks[1], (DEPTH, D_MODEL)),
        "w_in": nrm(ks[2], (DEPTH, D_MODEL, N_IN), D_MODEL),
        "b_gate": small(ks[3], (DEPTH, N_BRANCH * D_MODEL)),
        "conv_a_w": nrm(ks[4], (DEPTH, CONV_K, CONV_CH), CONV_K),
        "conv_a_b": small(ks[5], (DEPTH, CONV_CH)),
        "ln_a_g": gain(ks[6], (DEPTH, CONV_CH)),
        "ln_a_b": small(ks[7], (DEPTH, CONV_CH)),
        "w_a_out": nrm(ks[8], (DEPTH, CONV_CH, D_MODEL), CONV_CH),
        "conv_b_w": nrm(ks[9], (DEPTH, SC_K, SC_CH), SC_K),
        "w_b_out": nrm(ks[10], (DEPTH, SC_CH, D_MODEL), SC_CH),
        "w_c_out": nrm(ks[11], (DEPTH, ATT_OUT, D_MODEL), ATT_OUT),
        "w_o": nrm(ks[12], (DEPTH, D_MODEL, D_MODEL), D_MODEL),
        "norm2_g": gain(ks[13], (DEPTH, D_MODEL)),
        "w_ffn_gate": nrm(ks[14], (DEPTH, D_MODEL, D_FF), D_MODEL),
        "w_ffn_up": nrm(ks[15], (DEPTH, D_MODEL, D_FF), D_MODEL),
        "w_ffn_down": nrm(ks[16], (DEPTH, D_FF, D_MODEL), D_FF),
        "final_g": gain(ks[17], (D_MODEL,)),
    }


def reference(x, norm1_g, w_in, b_gate, conv_a_w, conv_a_b, ln_a_g, ln_a_b, w_a_out,
              conv_b_w, w_b_out, w_c_out, w_o, norm2_g, w_ffn_gate, w_ffn_up, w_ffn_down,
              final_g):
    bt, s, _ = x.shape
    cuts = np.cumsum([2 * CONV_CH, 3 * SC_CH, ATT_W, ATT_W, ATT_W]).tolist()
    for l in range(DEPTH):
        h = rmsnorm(x, norm1_g[l])
        z = jnp.einsum("bsd,dn->bsn", h, w_in[l])
        za, zb, zq, zk, zv, zg = jnp.split(z, cuts, axis=-1)

        a = za[..., :CONV_CH] * jax.nn.sigmoid(za[..., CONV_CH:])
        a = depthwise_conv(a, conv_a_w[l], CONV_K // 2) + conv_a_b[l]
        a = jax.nn.silu(layernorm(a, ln_a_g[l], ln_a_b[l]))
        y_a = jnp.einsum("bsc,cd->bsd", a, w_a_out[l])

        gb, gc, u = jnp.split(zb, 3, axis=-1)
        y_b = jnp.einsum("bsc,cd->bsd", gb * depthwise_conv(gc * u, conv_b_w[l], SC_K // 2),
                         w_b_out[l])

        q = zq.reshape(bt, s, N_HEADS, HEAD_DIM)
        k = zk.reshape(bt, s, N_HEADS, HEAD_DIM)
        v = zv.reshape(bt, s, N_HEADS, HEAD_DIM)
        y_c = jnp.einsum("bsc,cd->bsd", mixture_dilated_attention(q, k, v), w_c_out[l])

        g_a, g_b, g_c = jnp.split(jax.nn.sigmoid(zg + b_gate[l]), N_BRANCH, axis=-1)
        m = g_a * y_a + g_b * y_b + g_c * y_c
        x = x + jnp.einsum("bsd,de->bse", m, w_o[l])

        h2 = rmsnorm(x, norm2_g[l])
        f = jax.nn.silu(jnp.einsum("bsd,df->bsf", h2, w_ffn_gate[l])) * \
            jnp.einsum("bsd,df->bsf", h2, w_ffn_up[l])
        x = x + jnp.einsum("bsf,fd->bsd", f, w_ffn_down[l])
    return rmsnorm(x, final_g)
```

```python
import numpy as np
import ml_dtypes
import concourse.bass as bass
import concourse.mybir as mybir
from concourse.bass_utils import run_bass_kernel_spmd

F32 = mybir.dt.float32
BF16 = mybir.dt.bfloat16
AF = mybir.ActivationFunctionType
ALU = mybir.AluOpType

D = 2048
S = 16384
NBATCH = 2
DEPTH = 4
NCORE = 8
NTOK = 4096
HALO = 1024
NLOC = NTOK + 2 * HALO
T = 512
HC = 16
TW = T + 2 * HC
NIN = 11008
DFF = 5632
OQ, OKK, OV, OG = 2560, 3328, 4096, 4864
RS = (1, 4, 16)
N_HEADS = 12
P_N1G, P_N2G, P_FG, P_BG, P_CAW, P_CAB, P_LNG, P_LNB, P_CBW, NPP = 0, 16, 32, 48, 96, 220, 224, 228, 232, 244
NKB = 9

WNAMES = ("w_in", "w_a_out", "w_b_out", "w_c_out", "w_o", "w_ffn_gate", "w_ffn_up", "w_ffn_down")
WSHAPES = {"w_in": (D, NIN), "w_a_out": (512, D), "w_b_out": (512, D), "w_c_out": (256, D),
           "w_o": (D, D), "w_ffn_gate": (D, DFF), "w_ffn_up": (D, DFF), "w_ffn_down": (DFF, D)}


def alibi_slopes(n):
    return (2.0 ** (-8.0 * np.arange(1, n + 1) / n)).astype(np.float64)


class Res:
    __slots__ = ("name", "w", "r", "dsem", "dcount")

    def __init__(self, name):
        self.name = name
        self.w = None
        self.r = {}
        self.dsem = None
        self.dcount = 0


class Eng:
    def __init__(self, name, h, sem, is_pe=False):
        self.name = name
        self.h = h
        self.sem = sem
        self.is_pe = is_pe
        self.n = 0
        self.seen = {}

    def wait(self, tok):
        if tok is None:
            return
        sem, val = tok
        k = id(sem)
        if self.seen.get(k, 0) >= val:
            return
        self.h.wait_ge(sem, val)
        self.seen[k] = val


class Prog:
    def __init__(self, n_layers=1, final=True, dbg_tile=None, stop_after=None, max_main=None, max_pre=None):
        self.L = n_layers
        self.NLOC = NTOK + 2 * HALO * n_layers
        self.NT = self.NLOC // T
        self.max_main = max_main
        self.max_pre = max_pre
        self.final = final
        self.dbg_tile = dbg_tile
        self.stop_after = stop_after
        self.dbg_outs = []
        self.nc = bass.Bass("TRN2", target_bir_lowering=False)
        self._build()

    def op(self, E, fn, rd=(), wr=()):
        own = E.sem
        for r in rd:
            t = r.w
            if t is not None and not (E.is_pe and t[0] is own):
                E.wait(t)
        for w in wr:
            t = w.w
            if t is not None and not (E.is_pe and t[0] is own):
                E.wait(t)
            for t in w.r.values():
                if not (E.is_pe and t[0] is own):
                    E.wait(t)
        ins = fn()
        E.n += 1
        ins.then_inc(E.sem, 1)
        tok = (E.sem, E.n)
        for w in wr:
            w.w = tok
            w.r = {}
        for r in rd:
            r.r[E.name] = tok
        return ins

    def dma(self, E, out_ap, in_ap, dst, src=None):
        if dst.dsem is None:
            dst.dsem = self.new_sem("d_" + dst.name)
        if src is not None and src.w is not None:
            E.wait(src.w)
        if dst.w is not None and dst.w[0] is not dst.dsem:
            E.wait(dst.w)
        for t in dst.r.values():
            E.wait(t)
        ins = E.h.dma_start(out=out_ap, in_=in_ap)
        dst.dcount += 16
        ins.then_inc(dst.dsem, 16)
        tok = (dst.dsem, dst.dcount)
        dst.w = tok
        dst.r = {}
        if src is not None:
            src.r["dma_" + dst.name] = tok
        self.dma_res[dst.name] = dst

    def new_sem(self, name):
        return self.stack.enter_context(self.nc.semaphore(name))

    def barrier(self, reset=(), with_sp=False):
        engs = [self.PE, self.ACT, self.DVE, self.POOL]
        for E in engs + ([self.SP] if with_sp else []):
            for Fg in engs:
                if Fg is not E and Fg.n > 0:
                    E.wait((Fg.sem, Fg.n))
            for r in self.dma_res.values():
                if r.name.startswith("slab") or r.name.startswith("wb_") or r.name.startswith("kvd") or r.name.startswith("xs"):
                    continue
                if r.dcount > 0:
                    E.wait((r.dsem, r.dcount))
        for r in reset:
            r.w = None
            r.r = {}

    def sb(self, name, shape, dt):
        return self.stack.enter_context(self.nc.sbuf_tensor(name, list(shape), dt))

    def carve(self, off, shape, dt):
        nfree = int(np.prod(shape[1:]))
        esz = 4 if dt == F32 else 2
        assert off % 4 == 0
        a = self.ov[:, off // 4: off // 4 + (nfree * esz + 3) // 4]
        if dt != F32:
            a = a.bitcast(dt)
        a = a[:, 0:nfree]
        if len(shape) == 3:
            a = a.rearrange("p (a b) -> p a b", a=shape[1])
        elif len(shape) == 4:
            a = a.rearrange("p (a b c) -> p a b c", a=shape[1], b=shape[2])
        return a

    def dump(self, name, ap, res, tile):
        if self.dbg_tile is None or tile != self.dbg_tile:
            return
        shape = list(ap.shape)
        o = self.nc.dram_tensor("dbg_" + name, shape, ap.dtype, kind="ExternalOutput").ap()
        r = Res("dbg_" + name)
        self.dma(self.POOL, o, ap, r, res)
        self.dbg_outs.append(("dbg_" + name, r))

    def _build(self):
        import contextlib
        nc = self.nc
        L = self.L
        with contextlib.ExitStack() as stack:
            self.stack = stack
            self.dma_res = {}
            NLOC = self.NLOC
            self.xT = nc.dram_tensor("xT", [D, NLOC], F32, kind="ExternalInput").ap()
            self.xs = {0: self.xT}
            self.xs_res = {0: None}
            for l in range(1, L):
                self.xs[l] = nc.dram_tensor(f"xs{l}", [D, NLOC], F32, kind="Internal").ap()
                self.xs_res[l] = Res(f"xs{l}")
            self.vm_d = nc.dram_tensor("vmask", [128, NLOC], F32, kind="ExternalInput").ap()
            self.wf = {}
            self.wb = {}
            self.wb_res = {}
            for l in range(L):
                for wn in WNAMES:
                    sh = list(WSHAPES[wn])
                    self.wf[(l, wn)] = nc.dram_tensor(f"{wn}{l}", sh, F32, kind="ExternalInput").ap()
                    self.wb[(l, wn)] = nc.dram_tensor(f"wb_{wn}{l}", sh, BF16, kind="Internal").ap()
                    if wn == WNAMES[0]:
                        self.wb_layer_res = getattr(self, "wb_layer_res", {})
                        self.wb_layer_res[l] = Res(f"wb_L{l}")
                    self.wb_res[(l, wn)] = self.wb_layer_res[l]
            self.pp_d = nc.dram_tensor("pp", [128, L * NPP], F32, kind="ExternalInput").ap()
            self.kb_d = nc.dram_tensor("kb", [128, self.NT * NKB], F32, kind="ExternalInput").ap()
            self.dist_d = nc.dram_tensor("dist", [128, 256], F32, kind="ExternalInput").ap()
            self.xo_res = Res("xo")
            if not self.final:
                self.xo = nc.dram_tensor("xo", [D, NTOK], F32, kind="ExternalOutput").ap()
            if self.final:
                self.yo = nc.dram_tensor("yo", [D, NTOK], F32, kind="ExternalOutput").ap()
                self.yo_res = Res("yo")
            self.kd2 = [[nc.dram_tensor(f"kd{par}_{g}", [256, RS[g], self.NLOC // RS[g]], BF16, kind="Internal").ap() for g in range(3)] for par in range(2)]
            self.vd2 = [[nc.dram_tensor(f"vd{par}_{g}", [self.NLOC, 256], BF16, kind="Internal").ap() for g in range(3)] for par in range(2)]
            self.kvd_res2 = [Res("kvd0"), Res("kvd1")]
            self.rs_d = [nc.dram_tensor(f"rs{par}", [1, self.NLOC], F32, kind="Internal").ap() for par in range(2)]
            self.rs_res = [Res("kvdrs0"), Res("kvdrs1")]
            self.kst_res = [Res("kst0"), Res("kst1")]
            self.vst_res = [Res("vst0"), Res("vst1")]
            self.n1g_d = nc.dram_tensor("n1g", [128, L * 16], F32, kind="ExternalInput").ap()
            self.ident_d = nc.dram_tensor("ident", [128, 128], F32, kind="ExternalInput").ap()
            self.win_res = Res("win")
            self.slabs = self.sb("slabs", [128, 3 * 8192], BF16)
            self.slab_res = [Res(f"slab{i}") for i in range(3)]
            self.slab_i = 0
            self.xt = self.sb("xt", [128, 16, TW], F32)
            self.xt_res = Res("xt")
            self.hT = self.sb("hT", [128, 16, TW], BF16)
            self.hT_res = [Res(f"hT{c}") for c in range(16)]
            self.pp = self.sb("pp_sb", [128, NPP], F32)
            self.kb = self.sb("kb_sb", [128, self.NT * NKB], F32)
            self.vm = self.sb("vm_sb", [128, TW], F32)
            self.vm_res = Res("vm")
            self.dist = self.sb("dist_sb", [128, 256], F32)
            self.n1g = self.sb("n1g_sb", [128, L * 16], F32)
            self.ident = self.sb("ident_sb", [128, 128], F32)
            self.const_res = Res("consts")
            self.onesD = self.sb("onesD", [128, 128], BF16)
            self.ones5 = self.sb("ones5", [128, 128], BF16)
            self.ones1 = self.sb("ones1", [128, 128], BF16)
            self.rstd = self.sb("rstd", [128, TW], F32)
            self.rstd_res = Res("rstd")
            self.sq = self.sb("sq", [128, 2, TW], BF16)
            self.sq_res = [Res("sq0"), Res("sq1")]
            R13 = 46080
            R2 = 54272
            self.ov = self.sb("ov", [128, (R13 + R2) // 4], F32)
            o = 0
            self.sg = self.carve(o, [128, 4, TW], BF16); o += 4352
            self.gc = self.carve(o, [128, 4, TW], BF16); o += 4352
            self.glu = self.carve(o, [128, 4, TW], BF16); o += 4352
            self.gcu = self.carve(o, [128, 4, TW], BF16); o += 4352
            self.gb = self.carve(o, [128, 4, T], BF16); o += 4096
            self.q = self.carve(o, [128, 6, T], BF16); o += 6144
            self.a1 = self.carve(o, [128, 4, T], F32); o += 8192
            self.abc = self.carve(o, [128, 10, T], BF16); o += 10240
            assert o == R13
            self.asq = self.carve(8704, [128, 4, T], BF16)
            self.diag = self.carve(R13 - 2048, [128, 8, 128], BF16)
            self.diag_res = [Res(f"diag{i}") for i in range(8)]
            self.diag_i = 0
            self.fT = self.carve(0, [128, 44, T], BF16)
            self.yfin = self.carve(0, [128, 16, T], F32)
            self.lnm2 = self.carve(0, [128, T], F32)
            self.lnrs = self.carve(2048, [128, T], F32)
            self.a1b = self.carve(4352, [128, 4, T], BF16)
            o = R13
            self.kw0 = self.carve(o, [128, 2, 640], BF16); o += 2560
            self.kw1 = self.carve(o, [128, 2, 4, 256], BF16); o += 4096
            self.kw2 = self.carve(o, [128, 2, 16, 160], BF16); o += 10240
            self.vw0 = self.carve(o, [128, 5, 256], BF16); o += 2560
            self.vw1 = self.carve(o, [128, 2, 4, 256], BF16); o += 4096
            self.vw2a = self.carve(o, [128, 16, 256], BF16); o += 8192
            self.vw2b = self.carve(o, [128, 16, 256], BF16); o += 8192
            self.sS = self.carve(o, [128, 2, 512], F32); o += 4096
            self.pT = self.carve(o, [128, 2, 512], BF16); o += 2048
            self.numacc = self.carve(o, [128, 2, 512], F32); o += 4096
            self.denacc = self.carve(o, [128, 2, 512], F32); o += 4096
            assert o == R13 + R2
            self.x2 = self.carve(R13, [128, 16, T], F32)
            self.x2_res = Res("x2")
            self.h2 = self.carve(R13 + 32768, [128, 16, T], BF16)
            self.h2_res = [Res(f"h2_{c}") for c in range(16)]
            self.vm2 = self.carve(R13 + 32768 + 16384, [128, T], F32)
            self.vm2_res = Res("vm2")
            o = R13
            self.sgate = self.carve(o, [128, 4, T], F32); o += 8192
            self.tmp = self.carve(o, [128, 2, T], F32); o += 4096
            self.macc = self.carve(o, [128, 4, T], F32); o += 8192
            self.mT = self.carve(o, [128, 16, T], BF16); o += 16384
            o = 0
            self.kst = self.carve(o, [128, 2, 6, T], BF16); o += 12288
            self.vst = self.carve(o, [128, 2, 4, 768], BF16); o += 12288
            self.ps = stack.enter_context(nc.psum_tensor("ps", [128, 4096], F32))
            self.ps_res = [Res(f"ps{b}") for b in range(8)]
            self.bank_i = 0
            self.PE = Eng("pe", nc.tensor, self.new_sem("s_pe"), is_pe=True)
            self.ACT = Eng("act", nc.scalar, self.new_sem("s_act"))
            self.DVE = Eng("dve", nc.vector, self.new_sem("s_dve"))
            self.POOL = Eng("pool", nc.gpsimd, self.new_sem("s_pool"))
            self.SP = Eng("sp", nc.sync, self.new_sem("s_sp"))
            self._emit()
            for r in [self.xo_res] + ([self.yo_res] if self.final else []) + [r for _, r in self.dbg_outs]:
                if r.dcount:
                    self.POOL.wait((r.dsem, r.dcount))
            self.barrier()

    def bank(self, b):
        return self.ps[:, b * 512:(b + 1) * 512]

    def next_bank(self):
        b = self.bank_i
        self.bank_i = (self.bank_i + 1) % 4
        return b

    def _emit(self):
        nc = self.nc
        P = self.POOL
        self.dma(P, self.kb[:], self.kb_d, self.const_res)
        self.dma(P, self.dist[:], self.dist_d, self.const_res)
        self.dma(P, self.n1g[:], self.n1g_d, self.const_res)
        self.dma(P, self.ident[:], self.ident_d, self.const_res)
        cres = Res("ones")
        self.op(self.DVE, lambda: nc.vector.memset(self.onesD[:], 1.0 / D), wr=[cres])
        self.op(self.DVE, lambda: nc.vector.memset(self.ones5[:], 1.0 / 512), wr=[cres])
        self.op(self.DVE, lambda: nc.vector.memset(self.ones1[:], 1.0), wr=[cres])
        self.ones_res = cres
        self.cast_weights(0)
        for l in range(self.L):
            self.layer(l)

    def cast_weights(self, l):
        P = self.POOL
        for wn in WNAMES:
            K, N = WSHAPES[wn]
            src = self.wf[(l, wn)]
            dst = self.wb[(l, wn)]
            res = self.wb_res[(l, wn)]
            rows = 512 if K >= 512 else K
            if l == 0 and wn == "w_in":
                self.wb_kv0_res = Res("wb_kv0")
                for k0 in range(0, K, rows):
                    self.dma(P, dst[k0:k0 + rows, OKK:OV + 768], src[k0:k0 + rows, OKK:OV + 768], self.wb_kv0_res)
                for k0 in range(0, K, rows):
                    self.dma(P, dst[k0:k0 + rows, 0:OKK], src[k0:k0 + rows, 0:OKK], res)
                    self.dma(P, dst[k0:k0 + rows, OV + 768:N], src[k0:k0 + rows, OV + 768:N], res)
                continue
            for k0 in range(0, K, rows):
                self.dma(P, dst[k0:k0 + rows, :], src[k0:k0 + rows, :], res)

    def load_slab(self, parts):
        i = self.slab_i
        self.slab_i = (i + 1) % 3
        sres = self.slab_res[i]
        base = i * 8192
        views = []
        off = 0
        for (wres, w, k0, kc, n0, nw) in parts:
            v = self.slabs[:, base + off: base + off + kc * nw].rearrange("p (c n) -> p c n", c=kc)
            src = w[k0 * 128:(k0 + kc) * 128, n0:n0 + nw].rearrange("(c p) n -> p c n", p=128)
            self.dma(self.SP, v, src, sres, wres)
            views.append(v)
            off += kc * nw
        assert off <= 8192
        return views, sres

    def rmsnorm(self, l, gcol, c0, ncol, out_bf16=True, out_ap=None, out_res=None, mask=False, gsrc=None, xb=None, hb=None, vmb=None, rs_load=None, rs_store=None):
        nc = self.nc
        xt, xt_res = xb if xb is not None else (self.xt, self.xt_res)
        hT, hT_res = hb if hb is not None else (self.hT, self.hT_res)
        vm, vm_res = vmb if vmb is not None else (self.vm, self.vm_res)
        if rs_load is not None:
            self.dma(self.ACT, self.rstd[:, c0:c0 + ncol], rs_load[0].partition_broadcast(128), self.rstd_res, rs_load[1])
        segs = [(c0, ncol)] if ncol <= 512 else [(c0, ncol // 2), (c0 + ncol // 2, ncol - ncol // 2)]
        banks = [self.next_bank() for _ in segs] if rs_load is None else []
        for c in range(16 if rs_load is None else 0):
            sres = self.sq_res[c % 2]
            if c % 2 == 0:
                self.op(self.ACT, lambda c=c: nc.scalar.activation(out=self.sq[:, c % 2, c0:c0 + ncol], in_=xt[:, c, c0:c0 + ncol], func=AF.Square),
                        rd=[xt_res], wr=[sres])
            else:
                self.op(self.DVE, lambda c=c: nc.vector.tensor_tensor(out=self.sq[:, c % 2, c0:c0 + ncol], in0=xt[:, c, c0:c0 + ncol], in1=xt[:, c, c0:c0 + ncol], op=ALU.mult),
                        rd=[xt_res], wr=[sres])
            for (s0, sn), b in zip(segs, banks):
                self.op(self.PE, lambda c=c, s0=s0, sn=sn, b=b: nc.tensor.matmul(self.bank(b)[:, 0:sn], lhsT=self.onesD[:], rhs=self.sq[:, c % 2, s0:s0 + sn], start=(c == 0), stop=(c == 15)),
                        rd=[sres, self.ones_res], wr=[self.ps_res[b]])
        for (s0, sn), b in zip(segs, banks):
            self.op(self.ACT, lambda s0=s0, sn=sn, b=b: nc.scalar.activation(out=self.rstd[:, s0:s0 + sn], in_=self.bank(b)[:, 0:sn], func=AF.Sqrt, bias=self.eps6[:, 0:1], scale=1.0),
                    rd=[self.ps_res[b]], wr=[self.rstd_res])
        if rs_load is None:
            self.op(self.DVE, lambda: nc.vector.reciprocal(out=self.rstd[:, c0:c0 + ncol], in_=self.rstd[:, c0:c0 + ncol]), rd=[self.rstd_res], wr=[self.rstd_res])
        if mask and rs_load is None:
            self.op(self.DVE, lambda: nc.vector.tensor_tensor(out=self.rstd[:, c0:c0 + ncol], in0=self.rstd[:, c0:c0 + ncol], in1=vm[:, c0:c0 + ncol], op=ALU.mult), rd=[self.rstd_res, vm_res], wr=[self.rstd_res])
        if rs_store is not None:
            self.dma(self.POOL, rs_store[0], self.rstd[0:1, c0:c0 + ncol], rs_store[1], self.rstd_res)
        for c in range(16):
            if out_ap is None:
                o = hT[:, c, c0:c0 + ncol]
                ores = hT_res[c]
            else:
                o = out_ap[:, c, :]
                ores = out_res
            g = self.pp[:, gcol + c: gcol + c + 1] if gsrc is None else gsrc[:, c:c + 1]
            self.op(self.DVE, lambda o=o, c=c, g=g: nc.vector.scalar_tensor_tensor(out=o, in0=xt[:, c, c0:c0 + ncol], scalar=g, in1=self.rstd[:, c0:c0 + ncol], op0=ALU.mult, op1=ALU.mult),
                    rd=[xt_res, self.rstd_res, self.const_res], wr=[ores])

    def proj16(self, slabv, sres, ncols, src_fn, src_res, segs, evac):
        nc = self.nc
        for j in range(ncols // 128):
            for si, (s0, sn) in enumerate(segs):
                b = self.next_bank()
                for c in range(16):
                    self.op(self.PE, lambda c=c, j=j, s0=s0, sn=sn, b=b: nc.tensor.matmul(self.bank(b)[:, 0:sn], lhsT=slabv[:, c, j * 128:(j + 1) * 128], rhs=src_fn(c, s0, sn), start=(c == 0), stop=(c == 15)),
                            rd=[sres, src_res[c]], wr=[self.ps_res[b]])
                evac(j, si, b)

    def layer(self, l):
        nc = self.nc
        if not hasattr(self, "eps6"):
            self.eps6 = self.sb("eps6", [128, 2], F32)
            self.eps_res = Res("eps")
            self.op(self.DVE, lambda: nc.vector.memset(self.eps6[:, 0:1], 1e-6), wr=[self.eps_res])
            self.op(self.DVE, lambda: nc.vector.memset(self.eps6[:, 1:2], 1e-5), wr=[self.eps_res])
            self.barrier()
        self.barrier()
        self.dma(self.POOL, self.pp[:], self.pp_d[:, l * NPP:(l + 1) * NPP], self.const_res)
        if l == 0:
            self.prepass(l)
            self.barrier()
        if self.dbg_tile == "pre":
            for g in range(3):
                self.dump(f"kd{g}", self.kd[g], self.kvd_res, "pre")
                self.dump(f"vd{g}", self.vd[g], self.kvd_res, "pre")
        if self.stop_after == "prepass":
            return
        if l + 1 < self.L:
            self.cast_weights(l + 1)
        mt = list(range(2 * (l + 1), self.NT - 2 * (l + 1)))
        if self.max_main is not None:
            mt = mt[:self.max_main]
        for p in mt:
            self.main_tile(l, p)

    def prepass(self, l):
        nc = self.nc
        P = self.POOL
        self.kd, self.vd, self.kvd_res = self.kd2[l % 2], self.vd2[l % 2], self.kvd_res2[l % 2]
        wres = self.wb_kv0_res if l == 0 else self.wb_res[(l, "w_in")]
        w = self.wb[(l, "w_in")]
        wk = self.slabs[:, 0:12288].rearrange("p (c n) -> p c n", c=16)
        wv = self.slabs[:, 12288:24576].rearrange("p (c n) -> p c n", c=16)
        for sr in self.slab_res:
            if sr.w is not None:
                self.SP.wait(sr.w)
            for t in sr.r.values():
                self.SP.wait(t)
        self.dma(self.SP, wk, w[:, OKK:OKK + 768].rearrange("(c p) n -> p c n", p=128), self.slab_res[0], wres)
        self.dma(self.SP, wv, w[:, OV:OV + 768].rearrange("(c p) n -> p c n", p=128), self.slab_res[2], wres)
        self.slab_i = 0
        kst_res = self.kst_res
        vst_res = self.vst_res
        pts = list(range(2 * l, self.NT - 2 * l))
        if self.max_pre is not None:
            pts = pts[:self.max_pre]
        bufs = [((self.xt, self.xt_res), (self.hT, self.hT_res), (self.vm, self.vm_res)),
                ((self.x2, self.x2_res), (self.h2, self.h2_res), (self.vm2, self.vm2_res))]

        def front(i):
            p = pts[i]
            u0 = p * T
            (xb, xr), hb_, (vb, vr) = bufs[i % 2]
            self.dma(self.SP, xb[:, 0:8, 0:T], self.xs[l][0:1024, u0:u0 + T].rearrange("(c p) n -> p c n", p=128), xr, self.xs_res[l])
            self.dma(self.ACT, xb[:, 8:16, 0:T], self.xs[l][1024:2048, u0:u0 + T].rearrange("(c p) n -> p c n", p=128), xr, self.xs_res[l])
            self.dma(self.ACT, vb[:, 0:T], self.vm_d[:, u0:u0 + T], vr)
            self.rmsnorm(l, P_N1G, 0, T, mask=True, xb=(xb, xr), hb=hb_, vmb=(vb, vr), rs_store=(self.rs_d[l % 2][0:1, u0:u0 + T], self.rs_res[l % 2]))
        if pts:
            front(0)
        for i, p in enumerate(pts):
            u0 = p * T
            pb = p % 2
            if i + 1 < len(pts):
                front(i + 1)
            hT, hT_res = bufs[i % 2][1]
            for ch in range(6):
                g = ch // 2
                r = RS[g]
                b = self.next_bank()
                for c in range(16):
                    self.op(self.PE, lambda c=c, ch=ch, b=b: nc.tensor.matmul(self.bank(b), lhsT=wk[:, c, ch * 128:(ch + 1) * 128], rhs=hT[:, c, 0:T], start=(c == 0), stop=(c == 15)),
                            rd=self.slab_res + [hT_res[c]], wr=[self.ps_res[b]])
                o = self.kst[:, pb, ch, :].rearrange("p (r i) -> p r i", r=r)
                i_ = self.bank(b).rearrange("p (i r) -> p r i", r=r)
                E = self.ACT if ch % 2 == 0 else self.DVE
                if E is self.ACT:
                    self.op(E, lambda o=o, i_=i_: nc.scalar.copy(out=o, in_=i_), rd=[self.ps_res[b]], wr=[kst_res[pb]])
                else:
                    self.op(E, lambda o=o, i_=i_: nc.vector.tensor_copy(out=o, in_=i_), rd=[self.ps_res[b]], wr=[kst_res[pb]])
            for blk in range(4):
                for half in range(2):
                    b = self.next_bank()
                    for c in range(16):
                        self.op(self.PE, lambda c=c, blk=blk, half=half, b=b: nc.tensor.matmul(self.bank(b)[:, 0:384], lhsT=hT[:, c, blk * 128:(blk + 1) * 128], rhs=wv[:, c, half * 384:(half + 1) * 384], start=(c == 0), stop=(c == 15)),
                                rd=self.slab_res + [hT_res[c]], wr=[self.ps_res[b]])
                    o = self.vst[:, pb, blk, half * 384:(half + 1) * 384]
                    if half == 0:
                        self.op(self.ACT, lambda o=o, b=b: nc.scalar.copy(out=o, in_=self.bank(b)[:, 0:384]), rd=[self.ps_res[b]], wr=[vst_res[pb]])
                    else:
                        self.op(self.DVE, lambda o=o, b=b: nc.vector.tensor_copy(out=o, in_=self.bank(b)[:, 0:384]), rd=[self.ps_res[b]], wr=[vst_res[pb]])
            for g in range(3):
                r = RS[g]
                for chl in range(2):
                    dst = self.kd[g][chl * 128:(chl + 1) * 128, :, u0 // r:(u0 + T) // r]
                    srcv = self.kst[:, pb, 2 * g + chl, :].rearrange("p (r i) -> p r i", r=r)
                    self.dma(P, dst, srcv, self.kvd_res, kst_res[pb])
                dst = self.vd[g][u0:u0 + T, :].rearrange("(b p) n -> p b n", p=128)
                self.dma(P, dst, self.vst[:, pb, :, g * 256:(g + 1) * 256], self.kvd_res, vst_res[pb])

    def main_tile(self, l, m):
        nc = self.nc
        P = self.POOL
        PE, ACT, DVE = self.PE, self.ACT, self.DVE
        ppo = 0
        t0 = m * T
        self.kd, self.vd, self.kvd_res = self.kd2[l % 2], self.vd2[l % 2], self.kvd_res2[l % 2]
        w_in = self.wb[(l, "w_in")]
        w_in_res = self.wb_res[(l, "w_in")]
        sg_res = [Res(f"sg{j}") for j in range(4)]
        gc_res = [Res(f"gc{j}") for j in range(4)]
        glu_res = [Res(f"glu{j}") for j in range(4)]
        gcu_res = [Res(f"gcu{j}") for j in range(4)]
        gb_res = [Res(f"gb{j}") for j in range(4)]
        q_res = [Res(f"q{j}") for j in range(6)]
        a1_res = [Res(f"a1{j}") for j in range(4)]
        asq_res = [Res(f"asq{j}") for j in range(4)]
        a1b_res = [Res(f"a1b{j}") for j in range(4)]
        abc_res = [Res(f"abc{j}") for j in range(10)]
        win_res = self.win_res
        self.barrier(with_sp=True)
        P = self.SP
        self.dma(self.SP, self.xt[:, 0:8, :], self.xs[l][0:1024, t0 - HC:t0 + T + HC].rearrange("(c p) n -> p c n", p=128), self.xt_res, self.xs_res[l])
        self.dma(self.ACT, self.xt[:, 8:16, :], self.xs[l][1024:2048, t0 - HC:t0 + T + HC].rearrange("(c p) n -> p c n", p=128), self.xt_res, self.xs_res[l])
        self.dma(self.ACT, self.vm[:, :], self.vm_d[:, t0 - HC:t0 + T + HC], self.vm_res)
        i1 = t0 // 4
        i2 = t0 // 16
        self.dma(P, self.kw0[:], self.kd[0][:, 0, t0 - 64:t0 + 576].rearrange("(c p) n -> p c n", p=128), win_res, self.kvd_res)
        for chl in range(2):
            self.dma(P, self.kw1[:, chl], self.kd[1][chl * 128:(chl + 1) * 128, :, i1 - 64:i1 + 192], win_res, self.kvd_res)
            self.dma(P, self.kw2[:, chl], self.kd[2][chl * 128:(chl + 1) * 128, :, i2 - 64:i2 + 96], win_res, self.kvd_res)
        self.dma(P, self.vw0[:], self.vd[0][t0 - 64:t0 + 576, :].rearrange("(j p) n -> p j n", p=128), win_res, self.kvd_res)
        self.dma(P, self.vw1[:], self.vd[1][4 * (i1 - 64):4 * (i1 + 192), :].rearrange("(k p c) n -> p k c n", p=128, c=4), win_res, self.kvd_res)
        self.dma(P, self.vw2a[:], self.vd[2][16 * (i2 - 64):16 * (i2 + 64), :].rearrange("(p c) n -> p c n", c=16), win_res, self.kvd_res)
        self.dma(P, self.vw2b[0:32], self.vd[2][16 * (i2 + 64):16 * (i2 + 96), :].rearrange("(p c) n -> p c n", c=16), win_res, self.kvd_res)
        P = self.POOL
        self.rmsnorm(l, P_N1G, 0, TW, mask=True, rs_load=(self.rs_d[l % 2][0:1, t0 - HC:t0 + T + HC], self.rs_res[l % 2]))
        for E in (self.ACT, self.DVE, self.POOL, self.PE):
            for rr in self.kst_res + self.vst_res:
                for tk in rr.r.values():
                    E.wait(tk)
        self.dump("hT", self.hT[:], self.hT_res[15], m)
        segW = [(0, TW // 2), (TW // 2, TW // 2)]
        segT = [(HC, T)]
        src = lambda c, s0, sn: self.hT[:, c, s0:s0 + sn]

        def slab_in(n0, nw):
            v, r = self.load_slab([(w_in_res, w_in, 0, 16, n0, nw)])
            return v[0], r

        v, r = slab_in(512, 512)
        self.proj16(v, r, 512, src, self.hT_res, segW,
                    lambda j, si, b: self.op(ACT, lambda: nc.scalar.activation(out=self.sg[:, j, segW[si][0]:segW[si][0] + segW[si][1]], in_=self.bank(b)[:, 0:segW[si][1]], func=AF.Sigmoid), rd=[self.ps_res[b]], wr=[sg_res[j]]))
        v, r = slab_in(0, 512)
        self.proj16(v, r, 512, src, self.hT_res, segW,
                    lambda j, si, b: self.op(DVE, lambda: nc.vector.tensor_tensor(out=self.glu[:, j, segW[si][0]:segW[si][0] + segW[si][1]], in0=self.bank(b)[:, 0:segW[si][1]], in1=self.sg[:, j, segW[si][0]:segW[si][0] + segW[si][1]], op=ALU.mult), rd=[self.ps_res[b], sg_res[j]], wr=[glu_res[j]]))
        for j in range(4):
            b = self.next_bank()
            for k in range(31):
                slot = self.diag_i % 8
                self.diag_i += 1
                wcol = self.pp[:, ppo + P_CAW + k * 4 + j: ppo + P_CAW + k * 4 + j + 1]
                self.op(DVE, lambda: nc.vector.tensor_scalar(out=self.diag[:, slot, :], in0=self.ident[:], scalar1=wcol, scalar2=None, op0=ALU.mult), rd=[self.const_res], wr=[self.diag_res[slot]])
                self.op(PE, lambda: nc.tensor.matmul(self.bank(b), lhsT=self.diag[:, slot, :], rhs=self.glu[:, j, k + 1:k + 1 + T], start=(k == 0), stop=(k == 30)), rd=[self.diag_res[slot], glu_res[j]], wr=[self.ps_res[b]])
            bcol = self.pp[:, ppo + P_CAB + j: ppo + P_CAB + j + 1]
            self.op(ACT, lambda: nc.scalar.activation(out=self.a1[:, j, :], in_=self.bank(b), func=AF.Identity, bias=bcol, scale=1.0), rd=[self.ps_res[b], self.const_res], wr=[a1_res[j]])
        v, r = slab_in(1024, 512)
        self.proj16(v, r, 512, src, self.hT_res, segT,
                    lambda j, si, b: self.op(ACT, lambda: nc.scalar.copy(out=self.gb[:, j, :], in_=self.bank(b)), rd=[self.ps_res[b]], wr=[gb_res[j]]))
        bm = self.next_bank()
        bq = self.next_bank()
        for j in range(4):
            self.op(ACT, lambda: nc.scalar.activation(out=self.asq[:, j, :], in_=self.a1[:, j, :], func=AF.Square), rd=[a1_res[j]], wr=[asq_res[j]] + glu_res)
            self.op(ACT, lambda: nc.scalar.copy(out=self.a1b[:, j, :], in_=self.a1[:, j, :]), rd=[a1_res[j]], wr=[a1b_res[j]])
        for j in range(4):
            self.op(PE, lambda: nc.tensor.matmul(self.bank(bm), lhsT=self.ones5[:], rhs=self.a1b[:, j, :], start=(j == 0), stop=(j == 3)), rd=[a1b_res[j], self.ones_res], wr=[self.ps_res[bm]])
        for j in range(4):
            self.op(PE, lambda: nc.tensor.matmul(self.bank(bq), lhsT=self.ones5[:], rhs=self.asq[:, j, :], start=(j == 0), stop=(j == 3)), rd=[asq_res[j], self.ones_res], wr=[self.ps_res[bq]])
        ln_res = Res("lnm2")
        lr_res = Res("lnrs")
        self.op(ACT, lambda: nc.scalar.activation(out=self.lnm2, in_=self.bank(bm), func=AF.Square), rd=[self.ps_res[bm]], wr=[ln_res] + sg_res)
        self.op(DVE, lambda: nc.vector.tensor_tensor(out=self.lnrs, in0=self.bank(bq), in1=self.lnm2, op=ALU.subtract), rd=[self.ps_res[bq], ln_res], wr=[lr_res] + sg_res)
        self.op(ACT, lambda: nc.scalar.activation(out=self.lnrs, in_=self.lnrs, func=AF.Sqrt, bias=self.eps6[:, 1:2], scale=1.0), rd=[lr_res, self.eps_res], wr=[lr_res])
        self.op(DVE, lambda: nc.vector.reciprocal(out=self.lnrs, in_=self.lnrs), rd=[lr_res], wr=[lr_res])
        for j in range(4):
            self.op(DVE, lambda: nc.vector.tensor_tensor(out=self.a1[:, j, :], in0=self.a1[:, j, :], in1=self.bank(bm), op=ALU.subtract), rd=[a1_res[j], self.ps_res[bm]], wr=[a1_res[j]])
            self.op(DVE, lambda: nc.vector.tensor_tensor(out=self.a1[:, j, :], in0=self.a1[:, j, :], in1=self.lnrs, op=ALU.mult), rd=[a1_res[j], lr_res], wr=[a1_res[j]])
            gcol = self.pp[:, ppo + P_LNG + j: ppo + P_LNG + j + 1]
            bcol = self.pp[:, ppo + P_LNB + j: ppo + P_LNB + j + 1]
            self.op(ACT, lambda: nc.scalar.activation(out=self.abc[:, j, :], in_=self.a1[:, j, :], func=AF.Silu, bias=bcol, scale=gcol), rd=[a1_res[j], self.const_res], wr=[abc_res[j]])
        v, r = slab_in(1536, 512)
        self.proj16(v, r, 512, src, self.hT_res, segW,
                    lambda j, si, b: self.op(ACT, lambda: nc.scalar.copy(out=self.gc[:, j, segW[si][0]:segW[si][0] + segW[si][1]], in_=self.bank(b)[:, 0:segW[si][1]]), rd=[self.ps_res[b]], wr=[gc_res[j]] + a1b_res))
        v, r = slab_in(2048, 512)

        def u_evac(j, si, b):
            self.op(ACT, lambda: nc.scalar.copy(out=self.sg[:, j, segW[si][0]:segW[si][0] + segW[si][1]], in_=self.bank(b)[:, 0:segW[si][1]]), rd=[self.ps_res[b]], wr=[sg_res[j], ln_res, lr_res])
            if si == 1:
                self.op(self.POOL, lambda: nc.gpsimd.tensor_tensor(out=self.gcu[:, j, :], in0=self.sg[:, j, :], in1=self.gc[:, j, :], op=ALU.mult), rd=[sg_res[j], gc_res[j]], wr=[gcu_res[j]])
        self.proj16(v, r, 512, src, self.hT_res, segW, u_evac)

        for j in range(4):
            for k in range(3):
                wcol = self.pp[:, ppo + P_CBW + k * 4 + j: ppo + P_CBW + k * 4 + j + 1]
                xin = self.gcu[:, j, k + HC - 1:k + HC - 1 + T]
                if k == 0:
                    self.op(DVE, lambda: nc.vector.tensor_scalar(out=self.a1[:, j, :], in0=xin, scalar1=wcol, scalar2=None, op0=ALU.mult), rd=[gcu_res[j], self.const_res], wr=[a1_res[j]])
                else:
                    self.op(DVE, lambda: nc.vector.scalar_tensor_tensor(out=self.a1[:, j, :], in0=xin, scalar=wcol, in1=self.a1[:, j, :], op0=ALU.mult, op1=ALU.add), rd=[gcu_res[j], self.const_res, a1_res[j]], wr=[a1_res[j]])
            self.op(DVE, lambda: nc.vector.tensor_tensor(out=self.abc[:, 4 + j, :], in0=self.a1[:, j, :], in1=self.gb[:, j, :], op=ALU.mult), rd=[a1_res[j], gb_res[j]], wr=[abc_res[4 + j]])
        def q_evac(ch0):
            def f(j, si, b):
                ch = ch0 + j
                r_ = RS[ch // 2]
                o = self.q[:, ch, :].rearrange("p (r i) -> p r i", r=r_)
                i_ = self.bank(b).rearrange("p (i r) -> p r i", r=r_)
                self.op(ACT, lambda: nc.scalar.copy(out=o, in_=i_), rd=[self.ps_res[b]], wr=[q_res[ch]])
            return f
        v, r = slab_in(OQ, 512)
        self.proj16(v, r, 512, src, self.hT_res, segT, q_evac(0))
        v, r = slab_in(OQ + 512, 256)
        self.proj16(v, r, 256, src, self.hT_res, segT, q_evac(4))
        self.dump("glu", self.glu[:], glu_res[3], m)
        self.dump("gcu", self.gcu[:], gcu_res[3], m)
        self.dump("q", self.q[:], q_res[5], m)
        if self.stop_after == "stage1":
            return

        self.dump("aconv", self.a1[:], a1_res[3], m)
        self.dump("abT", self.abc[:, 0:8, :], abc_res[7], m)
        if self.stop_after == "branches":
            return
        self.attention(m, q_res, win_res, abc_res)
        self.dump("cT", self.abc[:, 8:10, :], abc_res[9], m)
        if self.stop_after == "attn":
            return
        for E in (ACT, DVE, self.POOL):
            for tk in win_res.r.values():
                E.wait(tk)
        sgate_res = [Res(f"sgate{j}") for j in range(4)]
        tmp_res = [Res("tmp0"), Res("tmp1")]
        macc_res = [Res(f"macc{j}") for j in range(4)]
        mT_res = [Res(f"mT{j}") for j in range(16)]
        wo = {k: (self.wb_res[(l, k)], self.wb[(l, k)]) for k in WNAMES}
        ti = 0
        for jb in range(4):
            for br in range(3):
                v, r = slab_in(OG + br * D + jb * 512, 512)

                def gate_evac(j, si, b, br=br, jb=jb):
                    bcol = self.pp[:, ppo + P_BG + br * 16 + jb * 4 + j: ppo + P_BG + br * 16 + jb * 4 + j + 1]
                    self.op(ACT, lambda: nc.scalar.activation(out=self.sgate[:, j, :], in_=self.bank(b), func=AF.Sigmoid, bias=bcol, scale=1.0), rd=[self.ps_res[b], self.const_res], wr=[sgate_res[j]])
                self.proj16(v, r, 512, src, self.hT_res, segT, gate_evac)
                kcs = (4, 4, 2)[br]
                base = (0, 4, 8)[br]
                wbn = ("w_a_out", "w_b_out", "w_c_out")[br]
                vb_, rbr = self.load_slab([(wo[wbn][0], wo[wbn][1], 0, kcs, jb * 512, 512)])
                vbr = {br: vb_[0]}
                for j in range(4):
                    b = self.next_bank()
                    for c in range(kcs):
                        self.op(PE, lambda: nc.tensor.matmul(self.bank(b), lhsT=vbr[br][:, c, j * 128:(j + 1) * 128], rhs=self.abc[:, base + c, :], start=(c == 0), stop=(c == kcs - 1)), rd=[rbr, abc_res[base + c]], wr=[self.ps_res[b]])
                    if br == 0:
                        self.op(DVE, lambda: nc.vector.tensor_tensor(out=self.macc[:, j, :], in0=self.bank(b), in1=self.sgate[:, j, :], op=ALU.mult), rd=[self.ps_res[b], sgate_res[j]], wr=[macc_res[j]])
                    else:
                        tt = ti % 2
                        ti += 1
                        self.op(DVE, lambda: nc.vector.tensor_tensor(out=self.tmp[:, tt, :], in0=self.bank(b), in1=self.sgate[:, j, :], op=ALU.mult), rd=[self.ps_res[b], sgate_res[j]], wr=[tmp_res[tt]])
                        if br == 1:
                            self.op(self.POOL, lambda: nc.gpsimd.tensor_tensor(out=self.macc[:, j, :], in0=self.macc[:, j, :], in1=self.tmp[:, tt, :], op=ALU.add), rd=[macc_res[j], tmp_res[tt]], wr=[macc_res[j]])
                        else:
                            self.op(self.POOL, lambda: nc.gpsimd.tensor_tensor(out=self.mT[:, jb * 4 + j, :], in0=self.macc[:, j, :], in1=self.tmp[:, tt, :], op=ALU.add), rd=[macc_res[j], tmp_res[tt]], wr=[mT_res[jb * 4 + j]])
        self.dump("mT", self.mT[:], mT_res[15], m)
        if self.stop_after == "stage2":
            return
        msrc = lambda c, s0, sn: self.mT[:, c, :]
        for jb in range(4):
            v, r = self.load_slab([(wo["w_o"][0], wo["w_o"][1], 0, 16, jb * 512, 512)])
            self.proj16(v[0], r, 512, msrc, mT_res, [(0, T)],
                        lambda j, si, b, jb=jb: self.op(DVE, lambda: nc.vector.tensor_tensor(out=self.xt[:, jb * 4 + j, HC:HC + T], in0=self.xt[:, jb * 4 + j, HC:HC + T], in1=self.bank(b), op=ALU.add), rd=[self.ps_res[b], self.xt_res], wr=[self.xt_res]))
        self.dump("x1", self.xt[:, :, HC:HC + T], self.xt_res, m)
        if self.stop_after == "wo":
            return
        self.barrier()
        self.rmsnorm(l, P_N2G, HC, T)
        fT_res = [Res(f"fT{j}") for j in range(44)]
        sgate_res = [Res(f"sgateF{j}") for j in range(4)]
        wg_r, wg = wo["w_ffn_gate"]
        wu_r, wu = wo["w_ffn_up"]
        wd_r, wd = wo["w_ffn_down"]
        for fb in range(11):
            v, r = self.load_slab([(wg_r, wg, 0, 16, fb * 512, 512)])
            self.proj16(v[0], r, 512, src, self.hT_res, segT,
                        lambda j, si, b: self.op(ACT, lambda: nc.scalar.activation(out=self.sgate[:, j, :], in_=self.bank(b), func=AF.Silu), rd=[self.ps_res[b]], wr=[sgate_res[j]]))
            v, r = self.load_slab([(wu_r, wu, 0, 16, fb * 512, 512)])
            self.proj16(v[0], r, 512, src, self.hT_res, segT,
                        lambda j, si, b, fb=fb: self.op(DVE, lambda: nc.vector.tensor_tensor(out=self.fT[:, fb * 4 + j, :], in0=self.bank(b), in1=self.sgate[:, j, :], op=ALU.mult), rd=[self.ps_res[b], sgate_res[j]], wr=[fT_res[fb * 4 + j]]))
        for jb in range(4):
            for kq in range(4):
                v, r = self.load_slab([(wd_r, wd, kq * 11, 11, jb * 512, 512)])
                for j in range(4):
                    for c in range(11):
                        self.op(PE, lambda: nc.tensor.matmul(self.bank(j), lhsT=v[0][:, c, j * 128:(j + 1) * 128], rhs=self.fT[:, kq * 11 + c, :], start=(kq == 0 and c == 0), stop=(kq == 3 and c == 10)), rd=[r, fT_res[kq * 11 + c]], wr=[self.ps_res[j]])
            for j in range(4):
                self.op(DVE, lambda: nc.vector.tensor_tensor(out=self.xt[:, jb * 4 + j, HC:HC + T], in0=self.xt[:, jb * 4 + j, HC:HC + T], in1=self.bank(j), op=ALU.add), rd=[self.ps_res[j], self.xt_res], wr=[self.xt_res])
        self.bank_i = 0
        mo = m - 2 * self.L
        if l + 1 < self.L:
            self.dma(P, self.xs[l + 1][:, t0:t0 + T].rearrange("(c p) n -> p c n", p=128), self.xt[:, :, HC:HC + T], self.xs_res[l + 1], self.xt_res)
        elif not self.final:
            self.dma(P, self.xo[:, mo * T:(mo + 1) * T].rearrange("(c p) n -> p c n", p=128), self.xt[:, :, HC:HC + T], self.xo_res, self.xt_res)
        if l + 1 < self.L:
            self.kv_next(l + 1, m)
        if self.final and l == self.L - 1:
            self.barrier()
            yres = Res("yfin")
            self.rmsnorm(l, P_FG, HC, T, out_ap=self.yfin, out_res=yres)
            self.dma(P, self.yo[:, mo * T:(mo + 1) * T].rearrange("(c p) n -> p c n", p=128), self.yfin[:], self.yo_res, yres)

    def kv_next(self, ln, p):
        nc = self.nc
        P = self.POOL
        PE, ACT, DVE = self.PE, self.ACT, self.DVE
        kd, vd, kvres = self.kd2[ln % 2], self.vd2[ln % 2], self.kvd_res2[ln % 2]
        w = self.wb[(ln, "w_in")]
        wres = self.wb_res[(ln, "w_in")]
        u0 = p * T
        pb = p % 2
        self.rmsnorm(ln, 0, HC, T, mask=True, gsrc=self.n1g[:, ln * 16:(ln + 1) * 16], rs_store=(self.rs_d[ln % 2][0:1, u0:u0 + T], self.rs_res[ln % 2]))
        src = lambda c, s0, sn: self.hT[:, c, s0:s0 + sn]

        def k_evac(ch0):
            def f(j, si, b):
                ch = ch0 + j
                r_ = RS[ch // 2]
                o = self.kst[:, pb, ch, :].rearrange("p (r i) -> p r i", r=r_)
                i_ = self.bank(b).rearrange("p (i r) -> p r i", r=r_)
                if j % 2 == 0:
                    self.op(ACT, lambda: nc.scalar.copy(out=o, in_=i_), rd=[self.ps_res[b]], wr=[self.kst_res[pb]])
                else:
                    self.op(DVE, lambda: nc.vector.tensor_copy(out=o, in_=i_), rd=[self.ps_res[b]], wr=[self.kst_res[pb]])
            return f
        v, r = self.load_slab([(wres, w, 0, 16, OKK, 512)])
        self.proj16(v[0], r, 512, src, self.hT_res, [(HC, T)], k_evac(0))
        v, r = self.load_slab([(wres, w, 0, 16, OKK + 512, 256)])
        self.proj16(v[0], r, 256, src, self.hT_res, [(HC, T)], k_evac(4))
        for half in range(2):
            v, r = self.load_slab([(wres, w, 0, 16, OV + half * 384, 384)])
            for blk in range(4):
                b = self.next_bank()
                for c in range(16):
                    self.op(PE, lambda: nc.tensor.matmul(self.bank(b)[:, 0:384], lhsT=self.hT[:, c, HC + blk * 128:HC + (blk + 1) * 128], rhs=v[0][:, c, :], start=(c == 0), stop=(c == 15)),
                            rd=[r, self.hT_res[c]], wr=[self.ps_res[b]])
                o = self.vst[:, pb, blk, half * 384:(half + 1) * 384]
                if blk % 2 == 0:
                    self.op(ACT, lambda: nc.scalar.copy(out=o, in_=self.bank(b)[:, 0:384]), rd=[self.ps_res[b]], wr=[self.vst_res[pb]])
                else:
                    self.op(DVE, lambda: nc.vector.tensor_copy(out=o, in_=self.bank(b)[:, 0:384]), rd=[self.ps_res[b]], wr=[self.vst_res[pb]])
        for g in range(3):
            r_ = RS[g]
            for chl in range(2):
                dst = kd[g][chl * 128:(chl + 1) * 128, :, u0 // r_:(u0 + T) // r_]
                srcv = self.kst[:, pb, 2 * g + chl, :].rearrange("p (r i) -> p r i", r=r_)
                self.dma(P, dst, srcv, kvres, self.kst_res[pb])
            dst = vd[g][u0:u0 + T, :].rearrange("(b p) n -> p b n", p=128)
            self.dma(P, dst, self.vst[:, pb, :, g * 256:(g + 1) * 256], kvres, self.vst_res[pb])

    def attention(self, m, q_res, win_res, abc_res):
        nc = self.nc
        PE, ACT, DVE = self.PE, self.ACT, self.DVE
        slopes = alibi_slopes(N_HEADS)
        num_res = Res("numacc")
        den_res = Res("denacc")
        sS_res = [Res("sS0"), Res("sS1")]
        pT_res = [Res("pT0"), Res("pT1")]
        BN, BD = 6, 7
        sb_i = [0]
        pending = []

        def flush():
            for f in pending:
                f()
            pending.clear()

        def after_block(ops):
            flush()
            pending.extend(ops)

        def score_block(lhs_list, rhs_list, ncols, dist_ap, kbcol, nrows, coef, qres):
            i = sb_i[0] % 2
            sb_i[0] += 1
            b = 4 + i
            col = 0
            for lh, rh in zip(lhs_list, rhs_list):
                n = rh.shape[-1]
                self.op(PE, lambda lh=lh, rh=rh, col=col, n=n: nc.tensor.matmul(self.bank(b)[0:nrows, col:col + n], lhsT=lh, rhs=rh, start=True, stop=True), rd=[win_res, qres], wr=[self.ps_res[b]])
                col += n
            assert col == ncols
            ps_ap = self.bank(b)[0:nrows, 0:ncols]
            ss = self.sS[0:nrows, i, 0:ncols]
            if len(dist_ap.shape) == 3:
                ps_ap = ps_ap.rearrange("p (a b) -> p a b", a=dist_ap.shape[1])
                ss3 = ss.rearrange("p (a b) -> p a b", a=dist_ap.shape[1])
            else:
                ss3 = ss
            self.op(DVE, lambda: nc.vector.scalar_tensor_tensor(out=ss3, in0=dist_ap, scalar=float(coef), in1=ps_ap, op0=ALU.mult, op1=ALU.add), rd=[self.ps_res[b], self.const_res], wr=[sS_res[i]])
            self.op(ACT, lambda: nc.scalar.activation(out=self.pT[0:nrows, i, 0:ncols], in_=ss, func=AF.Exp, bias=self.kb[0:nrows, kbcol:kbcol + 1], scale=0.125), rd=[sS_res[i], self.const_res], wr=[pT_res[i]])
            return i

        for g in range(3):
            r = RS[g]
            for hl in range(4):
                h = 4 * g + hl
                pair, par = hl // 2, hl % 2
                rows = slice(par * 64, par * 64 + 64)
                qch = 2 * g + pair
                coef = 8.0 * slopes[h] * r
                kb0 = m * NKB
                nb, db = self.bank(BN), self.bank(BD)
                if g == 0:
                    for j in range(5):
                        qc0, qc1 = max(0, 128 * (j - 1)), min(512, 128 * (j + 1))
                        d0 = 128 if j == 0 else 0
                        n = qc1 - qc0
                        i = score_block([self.kw0[rows, pair, 128 * j:128 * j + 128]], [self.q[rows, qch, qc0:qc1]], n, self.dist[:, d0:d0 + n], kb0 + j, 128, coef, q_res[qch])
                        if hl == 0 and j == 1:
                            self.dump("sS", self.sS[:, i, 0:256], sS_res[i], m)
                            self.dump("pT", self.pT[:, i, 0:256], pT_res[i], m)
                            self.dump("kw0", self.kw0[:], win_res, m)
                            self.dump("vw0", self.vw0[:], win_res, m)
                            self.dump("vw1", self.vw1[:], win_res, m)
                            self.dump("vw2a", self.vw2a[:], win_res, m)
                            self.dump("vw2b", self.vw2b[0:32], win_res, m)
                        ops = []
                        for qb in range(qc0 // 128, qc1 // 128):
                            pcol = qb * 128 - qc0
                            first = (j == 0)
                            ops.append(lambda qb=qb, pcol=pcol, first=first, j=j, i=i, pair=pair: self.op(PE, lambda: nc.tensor.matmul(nb[:, qb * 128:(qb + 1) * 128], lhsT=self.vw0[:, j, pair * 128:(pair + 1) * 128], rhs=self.pT[:, i, pcol:pcol + 128], start=first, stop=not first), rd=[pT_res[i], win_res], wr=[self.ps_res[BN]]))
                            ops.append(lambda qb=qb, pcol=pcol, first=first, i=i: self.op(PE, lambda: nc.tensor.matmul(db[:, qb * 128:(qb + 1) * 128], lhsT=self.ones1[:], rhs=self.pT[:, i, pcol:pcol + 128], start=first, stop=not first), rd=[pT_res[i], self.ones_res], wr=[self.ps_res[BD]]))
                        after_block(ops)
                else:
                    nres = r
                    nq = 512 // r
                    kw = self.kw1 if g == 1 else self.kw2
                    for kc in range(2):
                        if g == 1:
                            nrows = 128
                            kcols = slice(kc * 128, kc * 128 + 128)
                        else:
                            nrows = 128 if kc == 0 else 32
                            kcols = slice(0, 128) if kc == 0 else slice(128, 160)
                        d0 = 128 if kc == 0 else 0
                        dist_ap = self.dist[0:nrows, d0:d0 + nq].unsqueeze(1).to_broadcast([nrows, nres, nq])
                        lhs = [kw[rows, pair, c, kcols] for c in range(nres)]
                        rhs = [self.q[rows, qch, c * nq:(c + 1) * nq] for c in range(nres)]
                        i = score_block(lhs, rhs, 512, dist_ap, kb0 + (5 if g == 1 else 7) + kc, nrows, coef, q_res[qch])
                        ops = []
                        for c in range(nres):
                            if g == 1:
                                vl = self.vw1[:, kc, c, pair * 128:(pair + 1) * 128]
                            elif kc == 0:
                                vl = self.vw2a[:, c, pair * 128:(pair + 1) * 128]
                            else:
                                vl = self.vw2b[0:32, c, pair * 128:(pair + 1) * 128]
                            ops.append(lambda vl=vl, c=c, nq=nq, nrows=nrows, i=i, kc=kc: self.op(PE, lambda: nc.tensor.matmul(nb[:, c * nq:(c + 1) * nq], lhsT=vl, rhs=self.pT[0:nrows, i, c * nq:(c + 1) * nq], start=(kc == 0 and c == 0), stop=(kc == 1)), rd=[pT_res[i], win_res], wr=[self.ps_res[BN]]))
                        ops.append(lambda nrows=nrows, i=i, kc=kc: self.op(PE, lambda: nc.tensor.matmul(db, lhsT=self.ones1[0:nrows, :], rhs=self.pT[0:nrows, i, :], start=(kc == 0), stop=(kc == 1)), rd=[pT_res[i], self.ones_res], wr=[self.ps_res[BD]]))
                        after_block(ops)
                flush()
                if g == 0:
                    self.op(DVE, lambda: nc.vector.tensor_copy(out=self.numacc[rows, pair, :], in_=nb[rows, :]), rd=[self.ps_res[BN]], wr=[num_res])
                    self.op(ACT, lambda: nc.scalar.copy(out=self.denacc[rows, pair, :], in_=db[rows, :]), rd=[self.ps_res[BD]], wr=[den_res])
                else:
                    na = self.numacc[rows, pair, :].rearrange("p (i r) -> p r i", r=r)
                    da = self.denacc[rows, pair, :].rearrange("p (i r) -> p r i", r=r)
                    nps = nb[rows, :].rearrange("p (r i) -> p r i", r=r)
                    dps = db[rows, :].rearrange("p (r i) -> p r i", r=r)
                    self.op(DVE, lambda: nc.vector.tensor_tensor(out=na, in0=na, in1=nps, op=ALU.add), rd=[self.ps_res[BN], num_res], wr=[num_res])
                    self.op(DVE, lambda: nc.vector.tensor_tensor(out=da, in0=da, in1=dps, op=ALU.add), rd=[self.ps_res[BD], den_res], wr=[den_res])
            self.dump(f"denacc{g}", self.denacc[:], den_res, m)
            self.dump(f"numacc{g}", self.numacc[:], num_res, m)
        self.dump("numacc", self.numacc[:], num_res, m)
        for pair in range(2):
            self.op(DVE, lambda: nc.vector.tensor_scalar(out=self.denacc[:, pair, :], in0=self.denacc[:, pair, :], scalar1=1e-30, scalar2=None, op0=ALU.max), rd=[den_res], wr=[den_res])
            self.op(DVE, lambda: nc.vector.reciprocal(out=self.denacc[:, pair, :], in_=self.denacc[:, pair, :]), rd=[den_res], wr=[den_res])
            self.op(DVE, lambda: nc.vector.tensor_tensor(out=self.abc[:, 8 + pair, :], in0=self.numacc[:, pair, :], in1=self.denacc[:, pair, :], op=ALU.mult), rd=[num_res, den_res], wr=[abc_res[8 + pair]] + self.diag_res)


def make_pp(inp, layers):
    cols = []
    for l in layers:
        a = np.zeros((128, NPP), np.float32)
        a[:, P_N1G:P_N1G + 16] = inp["norm1_g"][l].reshape(16, 128).T
        a[:, P_N2G:P_N2G + 16] = inp["norm2_g"][l].reshape(16, 128).T
        a[:, P_FG:P_FG + 16] = inp["final_g"].reshape(16, 128).T
        a[:, P_BG:P_BG + 48] = inp["b_gate"][l].reshape(48, 128).T
        a[:, P_CAW:P_CAW + 124] = inp["conv_a_w"][l].reshape(31, 4, 128).transpose(2, 0, 1).reshape(128, 124)
        a[:, P_CAB:P_CAB + 4] = inp["conv_a_b"][l].reshape(4, 128).T
        a[:, P_LNG:P_LNG + 4] = inp["ln_a_g"][l].reshape(4, 128).T
        a[:, P_LNB:P_LNB + 4] = inp["ln_a_b"][l].reshape(4, 128).T
        a[:, P_CBW:P_CBW + 12] = inp["conv_b_w"][l].reshape(3, 4, 128).transpose(2, 0, 1).reshape(128, 12)
        cols.append(a)
    return np.ascontiguousarray(np.concatenate(cols, axis=1))


def make_dist():
    p = np.arange(128)[:, None].astype(np.float64)
    f = np.arange(256)[None, :].astype(np.float64)
    d = np.zeros((128, 256), np.float64)
    fb = f[:, :128]
    rel = p + 64 - fb
    d[:, :128] = np.where(p <= fb, -np.abs(rel), -1e6)
    rel = p - 64 - fb
    d[:, 128:] = np.where(p >= fb, -np.abs(rel), -1e6)
    return d.astype(np.float32)


def make_kb(core, L):
    s0 = (core % 4) * NTOK
    NT = (NTOK + 2 * HALO * L) // T
    kbm = np.zeros((128, NT * NKB), np.float32)
    p = np.arange(128)

    def val(u):
        a = s0 - HALO * L + u
        return np.where((a >= 0) & (a < S), 0.0, -30000.0).astype(np.float32)
    for m in range(NT):
        t0 = m * T
        for j in range(5):
            kbm[:, m * NKB + j] = val(t0 - 64 + 128 * j + p)
        i1 = t0 // 4
        kbm[:, m * NKB + 5] = val(4 * (i1 - 64 + p))
        kbm[:, m * NKB + 6] = val(4 * (i1 + 64 + p))
        i2 = t0 // 16
        kbm[:, m * NKB + 7] = val(16 * (i2 - 64 + p))
        kbm[:, m * NKB + 8] = val(16 * (i2 + 64 + p))
    return kbm


def make_vmask(core, L):
    s0 = (core % 4) * NTOK
    nloc = NTOK + 2 * HALO * L
    a = s0 - HALO * L + np.arange(nloc)
    v = ((a >= 0) & (a < S)).astype(np.float32)
    return np.ascontiguousarray(np.broadcast_to(v[None, :], (128, nloc)))


def slab_T(XT, core, L):
    b, s0 = core // 4, (core % 4) * NTOK
    h = HALO * L
    nloc = NTOK + 2 * h
    out = np.zeros((D, nloc), np.float32)
    lo, hi = max(0, s0 - h), min(S, s0 + NTOK + h)
    out[:, lo - (s0 - h):hi - (s0 - h)] = XT[b][:, lo:hi]
    return out


_PROG_CACHE = {}


def get_prog():
    if "fused" not in _PROG_CACHE:
        _PROG_CACHE["fused"] = Prog(n_layers=DEPTH, final=True)
    return _PROG_CACHE["fused"]


def kernel(**inp):
    inp = {k: np.asarray(v) for k, v in inp.items()}
    x = inp["x"]
    XT = np.ascontiguousarray(x.transpose(0, 2, 1))
    dist = make_dist()
    prog = get_prog()
    pp = make_pp(inp, list(range(DEPTH)))
    n1g = np.ascontiguousarray(np.concatenate([inp["norm1_g"][l].reshape(16, 128).T for l in range(DEPTH)], axis=1)).astype(np.float32)
    wmaps = {f"{wn}{l}": np.ascontiguousarray(inp[wn][l]) for wn in WNAMES for l in range(DEPTH)}
    in_maps = []
    for c in range(NCORE):
        d = dict(wmaps)
        d["xT"] = slab_T(XT, c, DEPTH)
        d["pp"] = pp
        d["kb"] = make_kb(c, DEPTH)
        d["vmask"] = make_vmask(c, DEPTH)
        d["dist"] = dist
        d["n1g"] = n1g
        d["ident"] = np.eye(128, dtype=np.float32)
        in_maps.append(d)
    res = run_bass_kernel_spmd(prog.nc, in_maps, core_ids=list(range(NCORE)))
    YT = np.empty((NBATCH, D, S), np.float32)
    for c in range(NCORE):
        b, s0 = c // 4, (c % 4) * NTOK
        YT[b][:, s0:s0 + NTOK] = res.results[c]["yo"]
    return np.ascontiguousarray(YT.transpose(0, 2, 1)).astype(np.float32)
```

```python
import numpy as np
import ml_dtypes
import concourse.bass as bass
import concourse.mybir as mybir
from concourse.bass_utils import run_bass_kernel_spmd

F32 = mybir.dt.float32
BF16 = mybir.dt.bfloat16
AF = mybir.ActivationFunctionType
ALU = mybir.AluOpType

D = 2048
S = 16384
NBATCH = 2
DEPTH = 4
NCORE = 8
NTOK = 4096
HALO = 1024
NLOC = NTOK + 2 * HALO
T = 512
HC = 16
TW = T + 2 * HC
NIN = 11008
DFF = 5632
OQ, OKK, OV, OG = 2560, 3328, 4096, 4864
RS = (1, 4, 16)
N_HEADS = 12
P_N1G, P_N2G, P_FG, P_BG, P_CAW, P_CAB, P_LNG, P_LNB, P_CBW, NPP = 0, 16, 32, 48, 96, 220, 224, 228, 232, 244
NKB = 9

WNAMES = ("w_in", "w_a_out", "w_b_out", "w_c_out", "w_o", "w_ffn_gate", "w_ffn_up", "w_ffn_down")
WSHAPES = {"w_in": (D, NIN), "w_a_out": (512, D), "w_b_out": (512, D), "w_c_out": (256, D),
           "w_o": (D, D), "w_ffn_gate": (D, DFF), "w_ffn_up": (D, DFF), "w_ffn_down": (DFF, D)}


def alibi_slopes(n):
    return (2.0 ** (-8.0 * np.arange(1, n + 1) / n)).astype(np.float64)


class Res:
    __slots__ = ("name", "w", "r", "dsem", "dcount")

    def __init__(self, name):
        self.name = name
        self.w = None
        self.r = {}
        self.dsem = None
        self.dcount = 0


class Eng:
    def __init__(self, name, h, sem, is_pe=False):
        self.name = name
        self.h = h
        self.sem = sem
        self.is_pe = is_pe
        self.n = 0
        self.seen = {}

    def wait(self, tok):
        if tok is None:
            return
        sem, val = tok
        k = id(sem)
        if self.seen.get(k, 0) >= val:
            return
        self.h.wait_ge(sem, val)
        self.seen[k] = val


class Prog:
    def __init__(self, n_layers=1, final=True, dbg_tile=None, stop_after=None, max_main=None, max_pre=None):
        self.L = n_layers
        self.NLOC = NTOK + 2 * HALO * n_layers
        self.NT = self.NLOC // T
        self.max_main = max_main
        self.max_pre = max_pre
        self.final = final
        self.dbg_tile = dbg_tile
        self.stop_after = stop_after
        self.dbg_outs = []
        self.nc = bass.Bass("TRN2", target_bir_lowering=False)
        self._build()

    def op(self, E, fn, rd=(), wr=()):
        own = E.sem
        for r in rd:
            t = r.w
            if t is not None and not (E.is_pe and t[0] is own):
                E.wait(t)
        for w in wr:
            t = w.w
            if t is not None and not (E.is_pe and t[0] is own):
                E.wait(t)
            for t in w.r.values():
                if not (E.is_pe and t[0] is own):
                    E.wait(t)
        ins = fn()
        E.n += 1
        ins.then_inc(E.sem, 1)
        tok = (E.sem, E.n)
        for w in wr:
            w.w = tok
            w.r = {}
        for r in rd:
            r.r[E.name] = tok
        return ins

    def dma(self, E, out_ap, in_ap, dst, src=None):
        if dst.dsem is None:
            dst.dsem = self.new_sem("d_" + dst.name)
        if src is not None and src.w is not None:
            E.wait(src.w)
        if dst.w is not None and dst.w[0] is not dst.dsem:
            E.wait(dst.w)
        for t in dst.r.values():
            E.wait(t)
        ins = E.h.dma_start(out=out_ap, in_=in_ap)
        dst.dcount += 16
        ins.then_inc(dst.dsem, 16)
        tok = (dst.dsem, dst.dcount)
        dst.w = tok
        dst.r = {}
        if src is not None:
            src.r["dma_" + dst.name] = tok
        self.dma_res[dst.name] = dst

    def new_sem(self, name):
        return self.stack.enter_context(self.nc.semaphore(name))

    def barrier(self, reset=(), with_sp=False):
        engs = [self.PE, self.ACT, self.DVE, self.POOL]
        for E in engs + ([self.SP] if with_sp else []):
            for Fg in engs:
                if Fg is not E and Fg.n > 0:
                    E.wait((Fg.sem, Fg.n))
            for r in self.dma_res.values():
                if r.name.startswith("slab") or r.name.startswith("wb_") or r.name.startswith("kvd") or r.name.startswith("xs"):
                    continue
                if r.dcount > 0:
                    E.wait((r.dsem, r.dcount))
        for r in reset:
            r.w = None
            r.r = {}

    def sb(self, name, shape, dt):
        return self.stack.enter_context(self.nc.sbuf_tensor(name, list(shape), dt))

    def carve(self, off, shape, dt):
        nfree = int(np.prod(shape[1:]))
        esz = 4 if dt == F32 else 2
        assert off % 4 == 0
        a = self.ov[:, off // 4: off // 4 + (nfree * esz + 3) // 4]
        if dt != F32:
            a = a.bitcast(dt)
        a = a[:, 0:nfree]
        if len(shape) == 3:
            a = a.rearrange("p (a b) -> p a b", a=shape[1])
        elif len(shape) == 4:
            a = a.rearrange("p (a b c) -> p a b c", a=shape[1], b=shape[2])
        return a

    def dump(self, name, ap, res, tile):
        if self.dbg_tile is None or tile != self.dbg_tile:
            return
        shape = list(ap.shape)
        o = self.nc.dram_tensor("dbg_" + name, shape, ap.dtype, kind="ExternalOutput").ap()
        r = Res("dbg_" + name)
        self.dma(self.POOL, o, ap, r, res)
        self.dbg_outs.append(("dbg_" + name, r))

    def _build(self):
        import contextlib
        nc = self.nc
        L = self.L
        with contextlib.ExitStack() as stack:
            self.stack = stack
            self.dma_res = {}
            NLOC = self.NLOC
            self.xT = nc.dram_tensor("xT", [D, NLOC], F32, kind="ExternalInput").ap()
            self.xs = {0: self.xT}
            self.xs_res = {0: None}
            for l in range(1, L):
                self.xs[l] = nc.dram_tensor(f"xs{l}", [D, NLOC], F32, kind="Internal").ap()
                self.xs_res[l] = Res(f"xs{l}")
            self.vm_d = nc.dram_tensor("vmask", [128, NLOC], F32, kind="ExternalInput").ap()
            self.wf = {}
            self.wb = {}
            self.wb_res = {}
            for l in range(L):
                for wn in WNAMES:
                    sh = list(WSHAPES[wn])
                    self.wf[(l, wn)] = nc.dram_tensor(f"{wn}{l}", sh, F32, kind="ExternalInput").ap()
                    self.wb[(l, wn)] = nc.dram_tensor(f"wb_{wn}{l}", sh, BF16, kind="Internal").ap()
                    if wn == WNAMES[0]:
                        self.wb_layer_res = getattr(self, "wb_layer_res", {})
                        self.wb_layer_res[l] = Res(f"wb_L{l}")
                    self.wb_res[(l, wn)] = self.wb_layer_res[l]
            self.pp_d = nc.dram_tensor("pp", [128, L * NPP], F32, kind="ExternalInput").ap()
            self.kb_d = nc.dram_tensor("kb", [128, self.NT * NKB], F32, kind="ExternalInput").ap()
            self.dist_d = nc.dram_tensor("dist", [128, 256], F32, kind="ExternalInput").ap()
            self.xo_res = Res("xo")
            if not self.final:
                self.xo = nc.dram_tensor("xo", [D, NTOK], F32, kind="ExternalOutput").ap()
            if self.final:
                self.yo = nc.dram_tensor("yo", [D, NTOK], F32, kind="ExternalOutput").ap()
                self.yo_res = Res("yo")
            self.kd2 = [[nc.dram_tensor(f"kd{par}_{g}", [256, RS[g], self.NLOC // RS[g]], BF16, kind="Internal").ap() for g in range(3)] for par in range(2)]
            self.vd2 = [[nc.dram_tensor(f"vd{par}_{g}", [self.NLOC, 256], BF16, kind="Internal").ap() for g in range(3)] for par in range(2)]
            self.kvd_res2 = [Res("kvd0"), Res("kvd1")]
            self.rs_d = [nc.dram_tensor(f"rs{par}", [1, self.NLOC], F32, kind="Internal").ap() for par in range(2)]
            self.rs_res = [Res("kvdrs0"), Res("kvdrs1")]
            self.kst_res = [Res("kst0"), Res("kst1")]
            self.vst_res = [Res("vst0"), Res("vst1")]
            self.n1g_d = nc.dram_tensor("n1g", [128, L * 16], F32, kind="ExternalInput").ap()
            self.ident_d = nc.dram_tensor("ident", [128, 128], F32, kind="ExternalInput").ap()
            self.win_res = Res("win")
            self.slabs = self.sb("slabs", [128, 3 * 8192], BF16)
            self.slab_res = [Res(f"slab{i}") for i in range(3)]
            self.slab_i = 0
            self.xt = self.sb("xt", [128, 16, TW], F32)
            self.xt_res = Res("xt")
            self.hT = self.sb("hT", [128, 16, TW], BF16)
            self.hT_res = [Res(f"hT{c}") for c in range(16)]
            self.pp = self.sb("pp_sb", [128, NPP], F32)
            self.kb = self.sb("kb_sb", [128, self.NT * NKB], F32)
            self.vm = self.sb("vm_sb", [128, TW], F32)
            self.vm_res = Res("vm")
            self.dist = self.sb("dist_sb", [128, 256], F32)
            self.n1g = self.sb("n1g_sb", [128, L * 16], F32)
            self.ident = self.sb("ident_sb", [128, 128], F32)
            self.const_res = Res("consts")
            self.onesD = self.sb("onesD", [128, 128], BF16)
            self.ones5 = self.sb("ones5", [128, 128], BF16)
            self.ones1 = self.sb("ones1", [128, 128], BF16)
            self.rstd = self.sb("rstd", [128, TW], F32)
            self.rstd_res = Res("rstd")
            self.sq = self.sb("sq", [128, 2, TW], BF16)
            self.sq_res = [Res("sq0"), Res("sq1")]
            R13 = 46080
            R2 = 54272
            self.ov = self.sb("ov", [128, (R13 + R2) // 4], F32)
            o = 0
            self.sg = self.carve(o, [128, 4, TW], BF16); o += 4352
            self.gc = self.carve(o, [128, 4, TW], BF16); o += 4352
            self.glu = self.carve(o, [128, 4, TW], BF16); o += 4352
            self.gcu = self.carve(o, [128, 4, TW], BF16); o += 4352
            self.gb = self.carve(o, [128, 4, T], BF16); o += 4096
            self.q = self.carve(o, [128, 6, T], BF16); o += 6144
            self.a1 = self.carve(o, [128, 4, T], F32); o += 8192
            self.abc = self.carve(o, [128, 10, T], BF16); o += 10240
            assert o == R13
            self.asq = self.carve(8704, [128, 4, T], BF16)
            self.diag = self.carve(R13 - 2048, [128, 8, 128], BF16)
            self.diag_res = [Res(f"diag{i}") for i in range(8)]
            self.diag_i = 0
            self.fT = self.carve(0, [128, 44, T], BF16)
            self.yfin = self.carve(0, [128, 16, T], F32)
            self.lnm2 = self.carve(0, [128, T], F32)
            self.lnrs = self.carve(2048, [128, T], F32)
            self.a1b = self.carve(4352, [128, 4, T], BF16)
            o = R13
            self.kw0 = self.carve(o, [128, 2, 640], BF16); o += 2560
            self.kw1 = self.carve(o, [128, 2, 4, 256], BF16); o += 4096
            self.kw2 = self.carve(o, [128, 2, 16, 160], BF16); o += 10240
            self.vw0 = self.carve(o, [128, 5, 256], BF16); o += 2560
            self.vw1 = self.carve(o, [128, 2, 4, 256], BF16); o += 4096
            self.vw2a = self.carve(o, [128, 16, 256], BF16); o += 8192
            self.vw2b = self.carve(o, [128, 16, 256], BF16); o += 8192
            self.sS = self.carve(o, [128, 2, 512], F32); o += 4096
            self.pT = self.carve(o, [128, 2, 512], BF16); o += 2048
            self.numacc = self.carve(o, [128, 2, 512], F32); o += 4096
            self.denacc = self.carve(o, [128, 2, 512], F32); o += 4096
            assert o == R13 + R2
            self.x2 = self.carve(R13, [128, 16, T], F32)
            self.x2_res = Res("x2")
            self.h2 = self.carve(R13 + 32768, [128, 16, T], BF16)
            self.h2_res = [Res(f"h2_{c}") for c in range(16)]
            self.vm2 = self.carve(R13 + 32768 + 16384, [128, T], F32)
            self.vm2_res = Res("vm2")
            o = R13
            self.sgate = self.carve(o, [128, 4, T], F32); o += 8192
            self.tmp = self.carve(o, [128, 2, T], F32); o += 4096
            self.macc = self.carve(o, [128, 4, T], F32); o += 8192
            self.mT = self.carve(o, [128, 16, T], BF16); o += 16384
            o = 0
            self.kst = self.carve(o, [128, 2, 6, T], BF16); o += 12288
            self.vst = self.carve(o, [128, 2, 4, 768], BF16); o += 12288
            self.ps = stack.enter_context(nc.psum_tensor("ps", [128, 4096], F32))
            self.ps_res = [Res(f"ps{b}") for b in range(8)]
            self.bank_i = 0
            self.PE = Eng("pe", nc.tensor, self.new_sem("s_pe"), is_pe=True)
            self.ACT = Eng("act", nc.scalar, self.new_sem("s_act"))
            self.DVE = Eng("dve", nc.vector, self.new_sem("s_dve"))
            self.POOL = Eng("pool", nc.gpsimd, self.new_sem("s_pool"))
            self.SP = Eng("sp", nc.sync, self.new_sem("s_sp"))
            self._emit()
            for r in [self.xo_res] + ([self.yo_res] if self.final else []) + [r for _, r in self.dbg_outs]:
                if r.dcount:
                    self.POOL.wait((r.dsem, r.dcount))
            self.barrier()

    def bank(self, b):
        return self.ps[:, b * 512:(b + 1) * 512]

    def next_bank(self):
        b = self.bank_i
        self.bank_i = (self.bank_i + 1) % 4
        return b

    def _emit(self):
        nc = self.nc
        P = self.POOL
        self.dma(P, self.kb[:], self.kb_d, self.const_res)
        self.dma(P, self.dist[:], self.dist_d, self.const_res)
        self.dma(P, self.n1g[:], self.n1g_d, self.const_res)
        self.dma(P, self.ident[:], self.ident_d, self.const_res)
        cres = Res("ones")
        self.op(self.DVE, lambda: nc.vector.memset(self.onesD[:], 1.0 / D), wr=[cres])
        self.op(self.DVE, lambda: nc.vector.memset(self.ones5[:], 1.0 / 512), wr=[cres])
        self.op(self.DVE, lambda: nc.vector.memset(self.ones1[:], 1.0), wr=[cres])
        self.ones_res = cres
        self.cast_weights(0)
        for l in range(self.L):
            self.layer(l)

    def cast_weights(self, l):
        P = self.POOL
        for wn in WNAMES:
            K, N = WSHAPES[wn]
            src = self.wf[(l, wn)]
            dst = self.wb[(l, wn)]
            res = self.wb_res[(l, wn)]
            rows = 512 if K >= 512 else K
            if l == 0 and wn == "w_in":
                self.wb_kv0_res = Res("wb_kv0")
                for k0 in range(0, K, rows):
                    self.dma(P, dst[k0:k0 + rows, OKK:OV + 768], src[k0:k0 + rows, OKK:OV + 768], self.wb_kv0_res)
                for k0 in range(0, K, rows):
                    self.dma(P, dst[k0:k0 + rows, 0:OKK], src[k0:k0 + rows, 0:OKK], res)
                    self.dma(P, dst[k0:k0 + rows, OV + 768:N], src[k0:k0 + rows, OV + 768:N], res)
                continue
            for k0 in range(0, K, rows):
                self.dma(P, dst[k0:k0 + rows, :], src[k0:k0 + rows, :], res)

    def load_slab(self, parts):
        i = self.slab_i
        self.slab_i = (i + 1) % 3
        sres = self.slab_res[i]
        base = i * 8192
        views = []
        off = 0
        for (wres, w, k0, kc, n0, nw) in parts:
            v = self.slabs[:, base + off: base + off + kc * nw].rearrange("p (c n) -> p c n", c=kc)
            src = w[k0 * 128:(k0 + kc) * 128, n0:n0 + nw].rearrange("(c p) n -> p c n", p=128)
            self.dma(self.SP, v, src, sres, wres)
            views.append(v)
            off += kc * nw
        assert off <= 8192
        return views, sres

    def rmsnorm(self, l, gcol, c0, ncol, out_bf16=True, out_ap=None, out_res=None, mask=False, gsrc=None, xb=None, hb=None, vmb=None, rs_load=None, rs_store=None):
        nc = self.nc
        xt, xt_res = xb if xb is not None else (self.xt, self.xt_res)
        hT, hT_res = hb if hb is not None else (self.hT, self.hT_res)
        vm, vm_res = vmb if vmb is not None else (self.vm, self.vm_res)
        if rs_load is not None:
            self.dma(self.ACT, self.rstd[:, c0:c0 + ncol], rs_load[0].partition_broadcast(128), self.rstd_res, rs_load[1])
        segs = [(c0, ncol)] if ncol <= 512 else [(c0, ncol // 2), (c0 + ncol // 2, ncol - ncol // 2)]
        banks = [self.next_bank() for _ in segs] if rs_load is None else []
        for c in range(16 if rs_load is None else 0):
            sres = self.sq_res[c % 2]
            if c % 2 == 0:
                self.op(self.ACT, lambda c=c: nc.scalar.activation(out=self.sq[:, c % 2, c0:c0 + ncol], in_=xt[:, c, c0:c0 + ncol], func=AF.Square),
                        rd=[xt_res], wr=[sres])
            else:
                self.op(self.DVE, lambda c=c: nc.vector.tensor_tensor(out=self.sq[:, c % 2, c0:c0 + ncol], in0=xt[:, c, c0:c0 + ncol], in1=xt[:, c, c0:c0 + ncol], op=ALU.mult),
                        rd=[xt_res], wr=[sres])
            for (s0, sn), b in zip(segs, banks):
                self.op(self.PE, lambda c=c, s0=s0, sn=sn, b=b: nc.tensor.matmul(self.bank(b)[:, 0:sn], lhsT=self.onesD[:], rhs=self.sq[:, c % 2, s0:s0 + sn], start=(c == 0), stop=(c == 15)),
                        rd=[sres, self.ones_res], wr=[self.ps_res[b]])
        for (s0, sn), b in zip(segs, banks):
            self.op(self.ACT, lambda s0=s0, sn=sn, b=b: nc.scalar.activation(out=self.rstd[:, s0:s0 + sn], in_=self.bank(b)[:, 0:sn], func=AF.Sqrt, bias=self.eps6[:, 0:1], scale=1.0),
                    rd=[self.ps_res[b]], wr=[self.rstd_res])
        if rs_load is None:
            self.op(self.DVE, lambda: nc.vector.reciprocal(out=self.rstd[:, c0:c0 + ncol], in_=self.rstd[:, c0:c0 + ncol]), rd=[self.rstd_res], wr=[self.rstd_res])
        if mask and rs_load is None:
            self.op(self.DVE, lambda: nc.vector.tensor_tensor(out=self.rstd[:, c0:c0 + ncol], in0=self.rstd[:, c0:c0 + ncol], in1=vm[:, c0:c0 + ncol], op=ALU.mult), rd=[self.rstd_res, vm_res], wr=[self.rstd_res])
        if rs_store is not None:
            self.dma(self.POOL, rs_store[0], self.rstd[0:1, c0:c0 + ncol], rs_store[1], self.rstd_res)
        for c in range(16):
            if out_ap is None:
                o = hT[:, c, c0:c0 + ncol]
                ores = hT_res[c]
            else:
                o = out_ap[:, c, :]
                ores = out_res
            g = self.pp[:, gcol + c: gcol + c + 1] if gsrc is None else gsrc[:, c:c + 1]
            self.op(self.DVE, lambda o=o, c=c, g=g: nc.vector.scalar_tensor_tensor(out=o, in0=xt[:, c, c0:c0 + ncol], scalar=g, in1=self.rstd[:, c0:c0 + ncol], op0=ALU.mult, op1=ALU.mult),
                    rd=[xt_res, self.rstd_res, self.const_res], wr=[ores])

    def proj16(self, slabv, sres, ncols, src_fn, src_res, segs, evac):
        nc = self.nc
        for j in range(ncols // 128):
            for si, (s0, sn) in enumerate(segs):
                b = self.next_bank()
                for c in range(16):
                    self.op(self.PE, lambda c=c, j=j, s0=s0, sn=sn, b=b: nc.tensor.matmul(self.bank(b)[:, 0:sn], lhsT=slabv[:, c, j * 128:(j + 1) * 128], rhs=src_fn(c, s0, sn), start=(c == 0), stop=(c == 15)),
                            rd=[sres, src_res[c]], wr=[self.ps_res[b]])
                evac(j, si, b)

    def layer(self, l):
        nc = self.nc
        if not hasattr(self, "eps6"):
            self.eps6 = self.sb("eps6", [128, 2], F32)
            self.eps_res = Res("eps")
            self.op(self.DVE, lambda: nc.vector.memset(self.eps6[:, 0:1], 1e-6), wr=[self.eps_res])
            self.op(self.DVE, lambda: nc.vector.memset(self.eps6[:, 1:2], 1e-5), wr=[self.eps_res])
            self.barrier()
        self.barrier()
        self.dma(self.POOL, self.pp[:], self.pp_d[:, l * NPP:(l + 1) * NPP], self.const_res)
        if l == 0:
            self.prepass(l)
            self.barrier()
        if self.dbg_tile == "pre":
            for g in range(3):
                self.dump(f"kd{g}", self.kd[g], self.kvd_res, "pre")
                self.dump(f"vd{g}", self.vd[g], self.kvd_res, "pre")
        if self.stop_after == "prepass":
            return
        if l + 1 < self.L:
            self.cast_weights(l + 1)
        mt = list(range(2 * (l + 1), self.NT - 2 * (l + 1)))
        if self.max_main is not None:
            mt = mt[:self.max_main]
        for p in mt:
            self.main_tile(l, p)

    def prepass(self, l):
        nc = self.nc
        P = self.POOL
        self.kd, self.vd, self.kvd_res = self.kd2[l % 2], self.vd2[l % 2], self.kvd_res2[l % 2]
        wres = self.wb_kv0_res if l == 0 else self.wb_res[(l, "w_in")]
        w = self.wb[(l, "w_in")]
        wk = self.slabs[:, 0:12288].rearrange("p (c n) -> p c n", c=16)
        wv = self.slabs[:, 12288:24576].rearrange("p (c n) -> p c n", c=16)
        for sr in self.slab_res:
            if sr.w is not None:
                self.SP.wait(sr.w)
            for t in sr.r.values():
                self.SP.wait(t)
        self.dma(self.SP, wk, w[:, OKK:OKK + 768].rearrange("(c p) n -> p c n", p=128), self.slab_res[0], wres)
        self.dma(self.SP, wv, w[:, OV:OV + 768].rearrange("(c p) n -> p c n", p=128), self.slab_res[2], wres)
        self.slab_i = 0
        kst_res = self.kst_res
        vst_res = self.vst_res
        pts = list(range(2 * l, self.NT - 2 * l))
        if self.max_pre is not None:
            pts = pts[:self.max_pre]
        bufs = [((self.xt, self.xt_res), (self.hT, self.hT_res), (self.vm, self.vm_res)),
                ((self.x2, self.x2_res), (self.h2, self.h2_res), (self.vm2, self.vm2_res))]

        def front(i):
            p = pts[i]
            u0 = p * T
            (xb, xr), hb_, (vb, vr) = bufs[i % 2]
            self.dma(self.SP, xb[:, 0:8, 0:T], self.xs[l][0:1024, u0:u0 + T].rearrange("(c p) n -> p c n", p=128), xr, self.xs_res[l])
            self.dma(self.ACT, xb[:, 8:16, 0:T], self.xs[l][1024:2048, u0:u0 + T].rearrange("(c p) n -> p c n", p=128), xr, self.xs_res[l])
            self.dma(self.ACT, vb[:, 0:T], self.vm_d[:, u0:u0 + T], vr)
            self.rmsnorm(l, P_N1G, 0, T, mask=True, xb=(xb, xr), hb=hb_, vmb=(vb, vr), rs_store=(self.rs_d[l % 2][0:1, u0:u0 + T], self.rs_res[l % 2]))
        if pts:
            front(0)
        for i, p in enumerate(pts):
            u0 = p * T
            pb = p % 2
            if i + 1 < len(pts):
                front(i + 1)
            hT, hT_res = bufs[i % 2][1]
            for ch in range(6):
                g = ch // 2
                r = RS[g]
                b = self.next_bank()
                for c in range(16):
                    self.op(self.PE, lambda c=c, ch=ch, b=b: nc.tensor.matmul(self.bank(b), lhsT=wk[:, c, ch * 128:(ch + 1) * 128], rhs=hT[:, c, 0:T], start=(c == 0), stop=(c == 15)),
                            rd=self.slab_res + [hT_res[c]], wr=[self.ps_res[b]])
                o = self.kst[:, pb, ch, :].rearrange("p (r i) -> p r i", r=r)
                i_ = self.bank(b).rearrange("p (i r) -> p r i", r=r)
                E = self.ACT if ch % 2 == 0 else self.DVE
                if E is self.ACT:
                    self.op(E, lambda o=o, i_=i_: nc.scalar.copy(out=o, in_=i_), rd=[self.ps_res[b]], wr=[kst_res[pb]])
                else:
                    self.op(E, lambda o=o, i_=i_: nc.vector.tensor_copy(out=o, in_=i_), rd=[self.ps_res[b]], wr=[kst_res[pb]])
            for blk in range(4):
                for half in range(2):
                    b = self.next_bank()
                    for c in range(16):
                        self.op(self.PE, lambda c=c, blk=blk, half=half, b=b: nc.tensor.matmul(self.bank(b)[:, 0:384], lhsT=hT[:, c, blk * 128:(blk + 1) * 128], rhs=wv[:, c, half * 384:(half + 1) * 384], start=(c == 0), stop=(c == 15)),
                                rd=self.slab_res + [hT_res[c]], wr=[self.ps_res[b]])
                    o = self.vst[:, pb, blk, half * 384:(half + 1) * 384]
                    if half == 0:
                        self.op(self.ACT, lambda o=o, b=b: nc.scalar.copy(out=o, in_=self.bank(b)[:, 0:384]), rd=[self.ps_res[b]], wr=[vst_res[pb]])
                    else:
                        self.op(self.DVE, lambda o=o, b=b: nc.vector.tensor_copy(out=o, in_=self.bank(b)[:, 0:384]), rd=[self.ps_res[b]], wr=[vst_res[pb]])
            for g in range(3):
                r = RS[g]
                for chl in range(2):
                    dst = self.kd[g][chl * 128:(chl + 1) * 128, :, u0 // r:(u0 + T) // r]
                    srcv = self.kst[:, pb, 2 * g + chl, :].rearrange("p (r i) -> p r i", r=r)
                    self.dma(P, dst, srcv, self.kvd_res, kst_res[pb])
                dst = self.vd[g][u0:u0 + T, :].rearrange("(b p) n -> p b n", p=128)
                self.dma(P, dst, self.vst[:, pb, :, g * 256:(g + 1) * 256], self.kvd_res, vst_res[pb])

    def main_tile(self, l, m):
        nc = self.nc
        P = self.POOL
        PE, ACT, DVE = self.PE, self.ACT, self.DVE
        ppo = 0
        t0 = m * T
        self.kd, self.vd, self.kvd_res = self.kd2[l % 2], self.vd2[l % 2], self.kvd_res2[l % 2]
        w_in = self.wb[(l, "w_in")]
        w_in_res = self.wb_res[(l, "w_in")]
        sg_res = [Res(f"sg{j}") for j in range(4)]
        gc_res = [Res(f"gc{j}") for j in range(4)]
        glu_res = [Res(f"glu{j}") for j in range(4)]
        gcu_res = [Res(f"gcu{j}") for j in range(4)]
        gb_res = [Res(f"gb{j}") for j in range(4)]
        q_res = [Res(f"q{j}") for j in range(6)]
        a1_res = [Res(f"a1{j}") for j in range(4)]
        asq_res = [Res(f"asq{j}") for j in range(4)]
        a1b_res = [Res(f"a1b{j}") for j in range(4)]
        abc_res = [Res(f"abc{j}") for j in range(10)]
        win_res = self.win_res
        self.barrier(with_sp=True)
        P = self.SP
        self.dma(self.SP, self.xt[:, 0:8, :], self.xs[l][0:1024, t0 - HC:t0 + T + HC].rearrange("(c p) n -> p c n", p=128), self.xt_res, self.xs_res[l])
        self.dma(self.ACT, self.xt[:, 8:16, :], self.xs[l][1024:2048, t0 - HC:t0 + T + HC].rearrange("(c p) n -> p c n", p=128), self.xt_res, self.xs_res[l])
        self.dma(self.ACT, self.vm[:, :], self.vm_d[:, t0 - HC:t0 + T + HC], self.vm_res)
        P = self.POOL
        i1 = t0 // 4
        i2 = t0 // 16
        self.dma(P, self.kw0[:], self.kd[0][:, 0, t0 - 64:t0 + 576].rearrange("(c p) n -> p c n", p=128), win_res, self.kvd_res)
        for chl in range(2):
            self.dma(P, self.kw1[:, chl], self.kd[1][chl * 128:(chl + 1) * 128, :, i1 - 64:i1 + 192], win_res, self.kvd_res)
            self.dma(P, self.kw2[:, chl], self.kd[2][chl * 128:(chl + 1) * 128, :, i2 - 64:i2 + 96], win_res, self.kvd_res)
        self.dma(P, self.vw0[:], self.vd[0][t0 - 64:t0 + 576, :].rearrange("(j p) n -> p j n", p=128), win_res, self.kvd_res)
        self.dma(P, self.vw1[:], self.vd[1][4 * (i1 - 64):4 * (i1 + 192), :].rearrange("(k p c) n -> p k c n", p=128, c=4), win_res, self.kvd_res)
        self.dma(P, self.vw2a[:], self.vd[2][16 * (i2 - 64):16 * (i2 + 64), :].rearrange("(p c) n -> p c n", c=16), win_res, self.kvd_res)
        self.dma(P, self.vw2b[0:32], self.vd[2][16 * (i2 + 64):16 * (i2 + 96), :].rearrange("(p c) n -> p c n", c=16), win_res, self.kvd_res)
        P = self.POOL
        self.rmsnorm(l, P_N1G, 0, TW, mask=True, rs_load=(self.rs_d[l % 2][0:1, t0 - HC:t0 + T + HC], self.rs_res[l % 2]))
        for E in (self.ACT, self.DVE, self.POOL, self.PE):
            for rr in self.kst_res + self.vst_res:
                for tk in rr.r.values():
                    E.wait(tk)
        self.dump("hT", self.hT[:], self.hT_res[15], m)
        segW = [(0, TW // 2), (TW // 2, TW // 2)]
        segT = [(HC, T)]
        src = lambda c, s0, sn: self.hT[:, c, s0:s0 + sn]

        def slab_in(n0, nw):
            v, r = self.load_slab([(w_in_res, w_in, 0, 16, n0, nw)])
            return v[0], r

        v, r = slab_in(512, 512)
        self.proj16(v, r, 512, src, self.hT_res, segW,
                    lambda j, si, b: self.op(ACT, lambda: nc.scalar.activation(out=self.sg[:, j, segW[si][0]:segW[si][0] + segW[si][1]], in_=self.bank(b)[:, 0:segW[si][1]], func=AF.Sigmoid), rd=[self.ps_res[b]], wr=[sg_res[j]]))
        v, r = slab_in(0, 512)
        self.proj16(v, r, 512, src, self.hT_res, segW,
                    lambda j, si, b: self.op(DVE, lambda: nc.vector.tensor_tensor(out=self.glu[:, j, segW[si][0]:segW[si][0] + segW[si][1]], in0=self.bank(b)[:, 0:segW[si][1]], in1=self.sg[:, j, segW[si][0]:segW[si][0] + segW[si][1]], op=ALU.mult), rd=[self.ps_res[b], sg_res[j]], wr=[glu_res[j]]))
        for j in range(4):
            b = self.next_bank()
            for k in range(31):
                slot = self.diag_i % 8
                self.diag_i += 1
                wcol = self.pp[:, ppo + P_CAW + k * 4 + j: ppo + P_CAW + k * 4 + j + 1]
                self.op(DVE, lambda: nc.vector.tensor_scalar(out=self.diag[:, slot, :], in0=self.ident[:], scalar1=wcol, scalar2=None, op0=ALU.mult), rd=[self.const_res], wr=[self.diag_res[slot]])
                self.op(PE, lambda: nc.tensor.matmul(self.bank(b), lhsT=self.diag[:, slot, :], rhs=self.glu[:, j, k + 1:k + 1 + T], start=(k == 0), stop=(k == 30)), rd=[self.diag_res[slot], glu_res[j]], wr=[self.ps_res[b]])
            bcol = self.pp[:, ppo + P_CAB + j: ppo + P_CAB + j + 1]
            self.op(ACT, lambda: nc.scalar.activation(out=self.a1[:, j, :], in_=self.bank(b), func=AF.Identity, bias=bcol, scale=1.0), rd=[self.ps_res[b], self.const_res], wr=[a1_res[j]])
        v, r = slab_in(1024, 512)
        self.proj16(v, r, 512, src, self.hT_res, segT,
                    lambda j, si, b: self.op(ACT, lambda: nc.scalar.copy(out=self.gb[:, j, :], in_=self.bank(b)), rd=[self.ps_res[b]], wr=[gb_res[j]]))
        bm = self.next_bank()
        bq = self.next_bank()
        for j in range(4):
            self.op(ACT, lambda: nc.scalar.activation(out=self.asq[:, j, :], in_=self.a1[:, j, :], func=AF.Square), rd=[a1_res[j]], wr=[asq_res[j]] + glu_res)
            self.op(ACT, lambda: nc.scalar.copy(out=self.a1b[:, j, :], in_=self.a1[:, j, :]), rd=[a1_res[j]], wr=[a1b_res[j]])
        for j in range(4):
            self.op(PE, lambda: nc.tensor.matmul(self.bank(bm), lhsT=self.ones5[:], rhs=self.a1b[:, j, :], start=(j == 0), stop=(j == 3)), rd=[a1b_res[j], self.ones_res], wr=[self.ps_res[bm]])
        for j in range(4):
            self.op(PE, lambda: nc.tensor.matmul(self.bank(bq), lhsT=self.ones5[:], rhs=self.asq[:, j, :], start=(j == 0), stop=(j == 3)), rd=[asq_res[j], self.ones_res], wr=[self.ps_res[bq]])
        ln_res = Res("lnm2")
        lr_res = Res("lnrs")
        self.op(ACT, lambda: nc.scalar.activation(out=self.lnm2, in_=self.bank(bm), func=AF.Square), rd=[self.ps_res[bm]], wr=[ln_res] + sg_res)
        self.op(DVE, lambda: nc.vector.tensor_tensor(out=self.lnrs, in0=self.bank(bq), in1=self.lnm2, op=ALU.subtract), rd=[self.ps_res[bq], ln_res], wr=[lr_res] + sg_res)
        self.op(ACT, lambda: nc.scalar.activation(out=self.lnrs, in_=self.lnrs, func=AF.Sqrt, bias=self.eps6[:, 1:2], scale=1.0), rd=[lr_res, self.eps_res], wr=[lr_res])
        self.op(DVE, lambda: nc.vector.reciprocal(out=self.lnrs, in_=self.lnrs), rd=[lr_res], wr=[lr_res])
        for j in range(4):
            self.op(DVE, lambda: nc.vector.tensor_tensor(out=self.a1[:, j, :], in0=self.a1[:, j, :], in1=self.bank(bm), op=ALU.subtract), rd=[a1_res[j], self.ps_res[bm]], wr=[a1_res[j]])
            self.op(DVE, lambda: nc.vector.tensor_tensor(out=self.a1[:, j, :], in0=self.a1[:, j, :], in1=self.lnrs, op=ALU.mult), rd=[a1_res[j], lr_res], wr=[a1_res[j]])
            gcol = self.pp[:, ppo + P_LNG + j: ppo + P_LNG + j + 1]
            bcol = self.pp[:, ppo + P_LNB + j: ppo + P_LNB + j + 1]
            self.op(ACT, lambda: nc.scalar.activation(out=self.abc[:, j, :], in_=self.a1[:, j, :], func=AF.Silu, bias=bcol, scale=gcol), rd=[a1_res[j], self.const_res], wr=[abc_res[j]])
        v, r = slab_in(1536, 512)
        self.proj16(v, r, 512, src, self.hT_res, segW,
                    lambda j, si, b: self.op(ACT, lambda: nc.scalar.copy(out=self.gc[:, j, segW[si][0]:segW[si][0] + segW[si][1]], in_=self.bank(b)[:, 0:segW[si][1]]), rd=[self.ps_res[b]], wr=[gc_res[j]] + a1b_res))
        v, r = slab_in(2048, 512)

        def u_evac(j, si, b):
            self.op(ACT, lambda: nc.scalar.copy(out=self.sg[:, j, segW[si][0]:segW[si][0] + segW[si][1]], in_=self.bank(b)[:, 0:segW[si][1]]), rd=[self.ps_res[b]], wr=[sg_res[j], ln_res, lr_res])
            if si == 1:
                self.op(self.POOL, lambda: nc.gpsimd.tensor_tensor(out=self.gcu[:, j, :], in0=self.sg[:, j, :], in1=self.gc[:, j, :], op=ALU.mult), rd=[sg_res[j], gc_res[j]], wr=[gcu_res[j]])
        self.proj16(v, r, 512, src, self.hT_res, segW, u_evac)

        for j in range(4):
            for k in range(3):
                wcol = self.pp[:, ppo + P_CBW + k * 4 + j: ppo + P_CBW + k * 4 + j + 1]
                xin = self.gcu[:, j, k + HC - 1:k + HC - 1 + T]
                if k == 0:
                    self.op(DVE, lambda: nc.vector.tensor_scalar(out=self.a1[:, j, :], in0=xin, scalar1=wcol, scalar2=None, op0=ALU.mult), rd=[gcu_res[j], self.const_res], wr=[a1_res[j]])
                else:
                    self.op(DVE, lambda: nc.vector.scalar_tensor_tensor(out=self.a1[:, j, :], in0=xin, scalar=wcol, in1=self.a1[:, j, :], op0=ALU.mult, op1=ALU.add), rd=[gcu_res[j], self.const_res, a1_res[j]], wr=[a1_res[j]])
            self.op(DVE, lambda: nc.vector.tensor_tensor(out=self.abc[:, 4 + j, :], in0=self.a1[:, j, :], in1=self.gb[:, j, :], op=ALU.mult), rd=[a1_res[j], gb_res[j]], wr=[abc_res[4 + j]])
        def q_evac(ch0):
            def f(j, si, b):
                ch = ch0 + j
                r_ = RS[ch // 2]
                o = self.q[:, ch, :].rearrange("p (r i) -> p r i", r=r_)
                i_ = self.bank(b).rearrange("p (i r) -> p r i", r=r_)
                self.op(ACT, lambda: nc.scalar.copy(out=o, in_=i_), rd=[self.ps_res[b]], wr=[q_res[ch]])
            return f
        v, r = slab_in(OQ, 512)
        self.proj16(v, r, 512, src, self.hT_res, segT, q_evac(0))
        v, r = slab_in(OQ + 512, 256)
        self.proj16(v, r, 256, src, self.hT_res, segT, q_evac(4))
        self.dump("glu", self.glu[:], glu_res[3], m)
        self.dump("gcu", self.gcu[:], gcu_res[3], m)
        self.dump("q", self.q[:], q_res[5], m)
        if self.stop_after == "stage1":
            return

        self.dump("aconv", self.a1[:], a1_res[3], m)
        self.dump("abT", self.abc[:, 0:8, :], abc_res[7], m)
        if self.stop_after == "branches":
            return
        self.attention(m, q_res, win_res, abc_res)
        self.dump("cT", self.abc[:, 8:10, :], abc_res[9], m)
        if self.stop_after == "attn":
            return
        for E in (ACT, DVE, self.POOL):
            for tk in win_res.r.values():
                E.wait(tk)
        sgate_res = [Res(f"sgate{j}") for j in range(4)]
        tmp_res = [Res("tmp0"), Res("tmp1")]
        macc_res = [Res(f"macc{j}") for j in range(4)]
        mT_res = [Res(f"mT{j}") for j in range(16)]
        wo = {k: (self.wb_res[(l, k)], self.wb[(l, k)]) for k in WNAMES}
        ti = 0
        for jb in range(4):
            for br in range(3):
                v, r = slab_in(OG + br * D + jb * 512, 512)

                def gate_evac(j, si, b, br=br, jb=jb):
                    bcol = self.pp[:, ppo + P_BG + br * 16 + jb * 4 + j: ppo + P_BG + br * 16 + jb * 4 + j + 1]
                    self.op(ACT, lambda: nc.scalar.activation(out=self.sgate[:, j, :], in_=self.bank(b), func=AF.Sigmoid, bias=bcol, scale=1.0), rd=[self.ps_res[b], self.const_res], wr=[sgate_res[j]])
                self.proj16(v, r, 512, src, self.hT_res, segT, gate_evac)
                kcs = (4, 4, 2)[br]
                base = (0, 4, 8)[br]
                wbn = ("w_a_out", "w_b_out", "w_c_out")[br]
                vb_, rbr = self.load_slab([(wo[wbn][0], wo[wbn][1], 0, kcs, jb * 512, 512)])
                vbr = {br: vb_[0]}
                for j in range(4):
                    b = self.next_bank()
                    for c in range(kcs):
                        self.op(PE, lambda: nc.tensor.matmul(self.bank(b), lhsT=vbr[br][:, c, j * 128:(j + 1) * 128], rhs=self.abc[:, base + c, :], start=(c == 0), stop=(c == kcs - 1)), rd=[rbr, abc_res[base + c]], wr=[self.ps_res[b]])
                    if br == 0:
                        self.op(DVE, lambda: nc.vector.tensor_tensor(out=self.macc[:, j, :], in0=self.bank(b), in1=self.sgate[:, j, :], op=ALU.mult), rd=[self.ps_res[b], sgate_res[j]], wr=[macc_res[j]])
                    else:
                        tt = ti % 2
                        ti += 1
                        self.op(DVE, lambda: nc.vector.tensor_tensor(out=self.tmp[:, tt, :], in0=self.bank(b), in1=self.sgate[:, j, :], op=ALU.mult), rd=[self.ps_res[b], sgate_res[j]], wr=[tmp_res[tt]])
                        if br == 1:
                            self.op(self.POOL, lambda: nc.gpsimd.tensor_tensor(out=self.macc[:, j, :], in0=self.macc[:, j, :], in1=self.tmp[:, tt, :], op=ALU.add), rd=[macc_res[j], tmp_res[tt]], wr=[macc_res[j]])
                        else:
                            self.op(self.POOL, lambda: nc.gpsimd.tensor_tensor(out=self.mT[:, jb * 4 + j, :], in0=self.macc[:, j, :], in1=self.tmp[:, tt, :], op=ALU.add), rd=[macc_res[j], tmp_res[tt]], wr=[mT_res[jb * 4 + j]])
        self.dump("mT", self.mT[:], mT_res[15], m)
        if self.stop_after == "stage2":
            return
        msrc = lambda c, s0, sn: self.mT[:, c, :]
        for jb in range(4):
            v, r = self.load_slab([(wo["w_o"][0], wo["w_o"][1], 0, 16, jb * 512, 512)])
            self.proj16(v[0], r, 512, msrc, mT_res, [(0, T)],
                        lambda j, si, b, jb=jb: self.op(DVE, lambda: nc.vector.tensor_tensor(out=self.xt[:, jb * 4 + j, HC:HC + T], in0=self.xt[:, jb * 4 + j, HC:HC + T], in1=self.bank(b), op=ALU.add), rd=[self.ps_res[b], self.xt_res], wr=[self.xt_res]))
        self.dump("x1", self.xt[:, :, HC:HC + T], self.xt_res, m)
        if self.stop_after == "wo":
            return
        self.barrier()
        self.rmsnorm(l, P_N2G, HC, T)
        fT_res = [Res(f"fT{j}") for j in range(44)]
        sgate_res = [Res(f"sgateF{j}") for j in range(4)]
        wg_r, wg = wo["w_ffn_gate"]
        wu_r, wu = wo["w_ffn_up"]
        wd_r, wd = wo["w_ffn_down"]
        for fb in range(11):
            v, r = self.load_slab([(wg_r, wg, 0, 16, fb * 512, 512)])
            self.proj16(v[0], r, 512, src, self.hT_res, segT,
                        lambda j, si, b: self.op(ACT, lambda: nc.scalar.activation(out=self.sgate[:, j, :], in_=self.bank(b), func=AF.Silu), rd=[self.ps_res[b]], wr=[sgate_res[j]]))
            v, r = self.load_slab([(wu_r, wu, 0, 16, fb * 512, 512)])
            self.proj16(v[0], r, 512, src, self.hT_res, segT,
                        lambda j, si, b, fb=fb: self.op(DVE, lambda: nc.vector.tensor_tensor(out=self.fT[:, fb * 4 + j, :], in0=self.bank(b), in1=self.sgate[:, j, :], op=ALU.mult), rd=[self.ps_res[b], sgate_res[j]], wr=[fT_res[fb * 4 + j]]))
        for jb in range(4):
            for kq in range(4):
                v, r = self.load_slab([(wd_r, wd, kq * 11, 11, jb * 512, 512)])
                for j in range(4):
                    for c in range(11):
                        self.op(PE, lambda: nc.tensor.matmul(self.bank(j), lhsT=v[0][:, c, j * 128:(j + 1) * 128], rhs=self.fT[:, kq * 11 + c, :], start=(kq == 0 and c == 0), stop=(kq == 3 and c == 10)), rd=[r, fT_res[kq * 11 + c]], wr=[self.ps_res[j]])
            for j in range(4):
                self.op(DVE, lambda: nc.vector.tensor_tensor(out=self.xt[:, jb * 4 + j, HC:HC + T], in0=self.xt[:, jb * 4 + j, HC:HC + T], in1=self.bank(j), op=ALU.add), rd=[self.ps_res[j], self.xt_res], wr=[self.xt_res])
        self.bank_i = 0
        mo = m - 2 * self.L
        if l + 1 < self.L:
            self.dma(P, self.xs[l + 1][:, t0:t0 + T].rearrange("(c p) n -> p c n", p=128), self.xt[:, :, HC:HC + T], self.xs_res[l + 1], self.xt_res)
        elif not self.final:
            self.dma(P, self.xo[:, mo * T:(mo + 1) * T].rearrange("(c p) n -> p c n", p=128), self.xt[:, :, HC:HC + T], self.xo_res, self.xt_res)
        if l + 1 < self.L:
            self.kv_next(l + 1, m)
        if self.final and l == self.L - 1:
            self.barrier()
            yres = Res("yfin")
            self.rmsnorm(l, P_FG, HC, T, out_ap=self.yfin, out_res=yres)
            self.dma(P, self.yo[:, mo * T:(mo + 1) * T].rearrange("(c p) n -> p c n", p=128), self.yfin[:], self.yo_res, yres)

    def kv_next(self, ln, p):
        nc = self.nc
        P = self.POOL
        PE, ACT, DVE = self.PE, self.ACT, self.DVE
        kd, vd, kvres = self.kd2[ln % 2], self.vd2[ln % 2], self.kvd_res2[ln % 2]
        w = self.wb[(ln, "w_in")]
        wres = self.wb_res[(ln, "w_in")]
        u0 = p * T
        pb = p % 2
        self.rmsnorm(ln, 0, HC, T, mask=True, gsrc=self.n1g[:, ln * 16:(ln + 1) * 16], rs_store=(self.rs_d[ln % 2][0:1, u0:u0 + T], self.rs_res[ln % 2]))
        src = lambda c, s0, sn: self.hT[:, c, s0:s0 + sn]

        def k_evac(ch0):
            def f(j, si, b):
                ch = ch0 + j
                r_ = RS[ch // 2]
                o = self.kst[:, pb, ch, :].rearrange("p (r i) -> p r i", r=r_)
                i_ = self.bank(b).rearrange("p (i r) -> p r i", r=r_)
                if j % 2 == 0:
                    self.op(ACT, lambda: nc.scalar.copy(out=o, in_=i_), rd=[self.ps_res[b]], wr=[self.kst_res[pb]])
                else:
                    self.op(DVE, lambda: nc.vector.tensor_copy(out=o, in_=i_), rd=[self.ps_res[b]], wr=[self.kst_res[pb]])
            return f
        v, r = self.load_slab([(wres, w, 0, 16, OKK, 512)])
        self.proj16(v[0], r, 512, src, self.hT_res, [(HC, T)], k_evac(0))
        v, r = self.load_slab([(wres, w, 0, 16, OKK + 512, 256)])
        self.proj16(v[0], r, 256, src, self.hT_res, [(HC, T)], k_evac(4))
        for half in range(2):
            v, r = self.load_slab([(wres, w, 0, 16, OV + half * 384, 384)])
            for blk in range(4):
                b = self.next_bank()
                for c in range(16):
                    self.op(PE, lambda: nc.tensor.matmul(self.bank(b)[:, 0:384], lhsT=self.hT[:, c, HC + blk * 128:HC + (blk + 1) * 128], rhs=v[0][:, c, :], start=(c == 0), stop=(c == 15)),
                            rd=[r, self.hT_res[c]], wr=[self.ps_res[b]])
                o = self.vst[:, pb, blk, half * 384:(half + 1) * 384]
                if blk % 2 == 0:
                    self.op(ACT, lambda: nc.scalar.copy(out=o, in_=self.bank(b)[:, 0:384]), rd=[self.ps_res[b]], wr=[self.vst_res[pb]])
                else:
                    self.op(DVE, lambda: nc.vector.tensor_copy(out=o, in_=self.bank(b)[:, 0:384]), rd=[self.ps_res[b]], wr=[self.vst_res[pb]])
        for g in range(3):
            r_ = RS[g]
            for chl in range(2):
                dst = kd[g][chl * 128:(chl + 1) * 128, :, u0 // r_:(u0 + T) // r_]
                srcv = self.kst[:, pb, 2 * g + chl, :].rearrange("p (r i) -> p r i", r=r_)
                self.dma(P, dst, srcv, kvres, self.kst_res[pb])
            dst = vd[g][u0:u0 + T, :].rearrange("(b p) n -> p b n", p=128)
            self.dma(P, dst, self.vst[:, pb, :, g * 256:(g + 1) * 256], kvres, self.vst_res[pb])

    def attention(self, m, q_res, win_res, abc_res):
        nc = self.nc
        PE, ACT, DVE = self.PE, self.ACT, self.DVE
        slopes = alibi_slopes(N_HEADS)
        num_res = Res("numacc")
        den_res = Res("denacc")
        sS_res = [Res("sS0"), Res("sS1")]
        pT_res = [Res("pT0"), Res("pT1")]
        BN, BD = 6, 7
        sb_i = [0]
        pending = []

        def flush():
            for f in pending:
                f()
            pending.clear()

        def after_block(ops):
            flush()
            pending.extend(ops)

        def score_block(lhs_list, rhs_list, ncols, dist_ap, kbcol, nrows, coef, qres):
            i = sb_i[0] % 2
            sb_i[0] += 1
            b = 4 + i
            col = 0
            for lh, rh in zip(lhs_list, rhs_list):
                n = rh.shape[-1]
                self.op(PE, lambda lh=lh, rh=rh, col=col, n=n: nc.tensor.matmul(self.bank(b)[0:nrows, col:col + n], lhsT=lh, rhs=rh, start=True, stop=True), rd=[win_res, qres], wr=[self.ps_res[b]])
                col += n
            assert col == ncols
            ps_ap = self.bank(b)[0:nrows, 0:ncols]
            ss = self.sS[0:nrows, i, 0:ncols]
            if len(dist_ap.shape) == 3:
                ps_ap = ps_ap.rearrange("p (a b) -> p a b", a=dist_ap.shape[1])
                ss3 = ss.rearrange("p (a b) -> p a b", a=dist_ap.shape[1])
            else:
                ss3 = ss
            self.op(DVE, lambda: nc.vector.scalar_tensor_tensor(out=ss3, in0=dist_ap, scalar=float(coef), in1=ps_ap, op0=ALU.mult, op1=ALU.add), rd=[self.ps_res[b], self.const_res], wr=[sS_res[i]])
            self.op(ACT, lambda: nc.scalar.activation(out=self.pT[0:nrows, i, 0:ncols], in_=ss, func=AF.Exp, bias=self.kb[0:nrows, kbcol:kbcol + 1], scale=0.125), rd=[sS_res[i], self.const_res], wr=[pT_res[i]])
            return i

        for g in range(3):
            r = RS[g]
            for hl in range(4):
                h = 4 * g + hl
                pair, par = hl // 2, hl % 2
                rows = slice(par * 64, par * 64 + 64)
                qch = 2 * g + pair
                coef = 8.0 * slopes[h] * r
                kb0 = m * NKB
                nb, db = self.bank(BN), self.bank(BD)
                if g == 0:
                    for j in range(5):
                        qc0, qc1 = max(0, 128 * (j - 1)), min(512, 128 * (j + 1))
                        d0 = 128 if j == 0 else 0
                        n = qc1 - qc0
                        i = score_block([self.kw0[rows, pair, 128 * j:128 * j + 128]], [self.q[rows, qch, qc0:qc1]], n, self.dist[:, d0:d0 + n], kb0 + j, 128, coef, q_res[qch])
                        if hl == 0 and j == 1:
                            self.dump("sS", self.sS[:, i, 0:256], sS_res[i], m)
                            self.dump("pT", self.pT[:, i, 0:256], pT_res[i], m)
                            self.dump("kw0", self.kw0[:], win_res, m)
                            self.dump("vw0", self.vw0[:], win_res, m)
                            self.dump("vw1", self.vw1[:], win_res, m)
                            self.dump("vw2a", self.vw2a[:], win_res, m)
                            self.dump("vw2b", self.vw2b[0:32], win_res, m)
                        ops = []
                        for qb in range(qc0 // 128, qc1 // 128):
                            pcol = qb * 128 - qc0
                            first = (j == 0)
                            ops.append(lambda qb=qb, pcol=pcol, first=first, j=j, i=i, pair=pair: self.op(PE, lambda: nc.tensor.matmul(nb[:, qb * 128:(qb + 1) * 128], lhsT=self.vw0[:, j, pair * 128:(pair + 1) * 128], rhs=self.pT[:, i, pcol:pcol + 128], start=first, stop=not first), rd=[pT_res[i], win_res], wr=[self.ps_res[BN]]))
                            ops.append(lambda qb=qb, pcol=pcol, first=first, i=i: self.op(PE, lambda: nc.tensor.matmul(db[:, qb * 128:(qb + 1) * 128], lhsT=self.ones1[:], rhs=self.pT[:, i, pcol:pcol + 128], start=first, stop=not first), rd=[pT_res[i], self.ones_res], wr=[self.ps_res[BD]]))
                        after_block(ops)
                else:
                    nres = r
                    nq = 512 // r
                    kw = self.kw1 if g == 1 else self.kw2
                    for kc in range(2):
                        if g == 1:
                            nrows = 128
                            kcols = slice(kc * 128, kc * 128 + 128)
                        else:
                            nrows = 128 if kc == 0 else 32
                            kcols = slice(0, 128) if kc == 0 else slice(128, 160)
                        d0 = 128 if kc == 0 else 0
                        dist_ap = self.dist[0:nrows, d0:d0 + nq].unsqueeze(1).to_broadcast([nrows, nres, nq])
                        lhs = [kw[rows, pair, c, kcols] for c in range(nres)]
                        rhs = [self.q[rows, qch, c * nq:(c + 1) * nq] for c in range(nres)]
                        i = score_block(lhs, rhs, 512, dist_ap, kb0 + (5 if g == 1 else 7) + kc, nrows, coef, q_res[qch])
                        ops = []
                        for c in range(nres):
                            if g == 1:
                                vl = self.vw1[:, kc, c, pair * 128:(pair + 1) * 128]
                            elif kc == 0:
                                vl = self.vw2a[:, c, pair * 128:(pair + 1) * 128]
                            else:
                                vl = self.vw2b[0:32, c, pair * 128:(pair + 1) * 128]
                            ops.append(lambda vl=vl, c=c, nq=nq, nrows=nrows, i=i, kc=kc: self.op(PE, lambda: nc.tensor.matmul(nb[:, c * nq:(c + 1) * nq], lhsT=vl, rhs=self.pT[0:nrows, i, c * nq:(c + 1) * nq], start=(kc == 0 and c == 0), stop=(kc == 1)), rd=[pT_res[i], win_res], wr=[self.ps_res[BN]]))
                        ops.append(lambda nrows=nrows, i=i, kc=kc: self.op(PE, lambda: nc.tensor.matmul(db, lhsT=self.ones1[0:nrows, :], rhs=self.pT[0:nrows, i, :], start=(kc == 0), stop=(kc == 1)), rd=[pT_res[i], self.ones_res], wr=[self.ps_res[BD]]))
                        after_block(ops)
                flush()
                if g == 0:
                    self.op(DVE, lambda: nc.vector.tensor_copy(out=self.numacc[rows, pair, :], in_=nb[rows, :]), rd=[self.ps_res[BN]], wr=[num_res])
                    self.op(ACT, lambda: nc.scalar.copy(out=self.denacc[rows, pair, :], in_=db[rows, :]), rd=[self.ps_res[BD]], wr=[den_res])
                else:
                    na = self.numacc[rows, pair, :].rearrange("p (i r) -> p r i", r=r)
                    da = self.denacc[rows, pair, :].rearrange("p (i r) -> p r i", r=r)
                    nps = nb[rows, :].rearrange("p (r i) -> p r i", r=r)
                    dps = db[rows, :].rearrange("p (r i) -> p r i", r=r)
                    self.op(DVE, lambda: nc.vector.tensor_tensor(out=na, in0=na, in1=nps, op=ALU.add), rd=[self.ps_res[BN], num_res], wr=[num_res])
                    self.op(DVE, lambda: nc.vector.tensor_tensor(out=da, in0=da, in1=dps, op=ALU.add), rd=[self.ps_res[BD], den_res], wr=[den_res])
            self.dump(f"denacc{g}", self.denacc[:], den_res, m)
            self.dump(f"numacc{g}", self.numacc[:], num_res, m)
        self.dump("numacc", self.numacc[:], num_res, m)
        for pair in range(2):
            self.op(DVE, lambda: nc.vector.tensor_scalar(out=self.denacc[:, pair, :], in0=self.denacc[:, pair, :], scalar1=1e-30, scalar2=None, op0=ALU.max), rd=[den_res], wr=[den_res])
            self.op(DVE, lambda: nc.vector.reciprocal(out=self.denacc[:, pair, :], in_=self.denacc[:, pair, :]), rd=[den_res], wr=[den_res])
            self.op(DVE, lambda: nc.vector.tensor_tensor(out=self.abc[:, 8 + pair, :], in0=self.numacc[:, pair, :], in1=self.denacc[:, pair, :], op=ALU.mult), rd=[num_res, den_res], wr=[abc_res[8 + pair]] + self.diag_res)


def make_pp(inp, layers):
    cols = []
    for l in layers:
        a = np.zeros((128, NPP), np.float32)
        a[:, P_N1G:P_N1G + 16] = inp["norm1_g"][l].reshape(16, 128).T
        a[:, P_N2G:P_N2G + 16] = inp["norm2_g"][l].reshape(16, 128).T
        a[:, P_FG:P_FG + 16] = inp["final_g"].reshape(16, 128).T
        a[:, P_BG:P_BG + 48] = inp["b_gate"][l].reshape(48, 128).T
        a[:, P_CAW:P_CAW + 124] = inp["conv_a_w"][l].reshape(31, 4, 128).transpose(2, 0, 1).reshape(128, 124)
        a[:, P_CAB:P_CAB + 4] = inp["conv_a_b"][l].reshape(4, 128).T
        a[:, P_LNG:P_LNG + 4] = inp["ln_a_g"][l].reshape(4, 128).T
        a[:, P_LNB:P_LNB + 4] = inp["ln_a_b"][l].reshape(4, 128).T
        a[:, P_CBW:P_CBW + 12] = inp["conv_b_w"][l].reshape(3, 4, 128).transpose(2, 0, 1).reshape(128, 12)
        cols.append(a)
    return np.ascontiguousarray(np.concatenate(cols, axis=1))


def make_dist():
    p = np.arange(128)[:, None].astype(np.float64)
    f = np.arange(256)[None, :].astype(np.float64)
    d = np.zeros((128, 256), np.float64)
    fb = f[:, :128]
    rel = p + 64 - fb
    d[:, :128] = np.where(p <= fb, -np.abs(rel), -1e6)
    rel = p - 64 - fb
    d[:, 128:] = np.where(p >= fb, -np.abs(rel), -1e6)
    return d.astype(np.float32)


def make_kb(core, L):
    s0 = (core % 4) * NTOK
    NT = (NTOK + 2 * HALO * L) // T
    kbm = np.zeros((128, NT * NKB), np.float32)
    p = np.arange(128)

    def val(u):
        a = s0 - HALO * L + u
        return np.where((a >= 0) & (a < S), 0.0, -30000.0).astype(np.float32)
    for m in range(NT):
        t0 = m * T
        for j in range(5):
            kbm[:, m * NKB + j] = val(t0 - 64 + 128 * j + p)
        i1 = t0 // 4
        kbm[:, m * NKB + 5] = val(4 * (i1 - 64 + p))
        kbm[:, m * NKB + 6] = val(4 * (i1 + 64 + p))
        i2 = t0 // 16
        kbm[:, m * NKB + 7] = val(16 * (i2 - 64 + p))
        kbm[:, m * NKB + 8] = val(16 * (i2 + 64 + p))
    return kbm


def make_vmask(core, L):
    s0 = (core % 4) * NTOK
    nloc = NTOK + 2 * HALO * L
    a = s0 - HALO * L + np.arange(nloc)
    v = ((a >= 0) & (a < S)).astype(np.float32)
    return np.ascontiguousarray(np.broadcast_to(v[None, :], (128, nloc)))


def slab_T(XT, core, L):
    b, s0 = core // 4, (core % 4) * NTOK
    h = HALO * L
    nloc = NTOK + 2 * h
    out = np.zeros((D, nloc), np.float32)
    lo, hi = max(0, s0 - h), min(S, s0 + NTOK + h)
    out[:, lo - (s0 - h):hi - (s0 - h)] = XT[b][:, lo:hi]
    return out


_PROG_CACHE = {}


def get_prog():
    if "fused" not in _PROG_CACHE:
        _PROG_CACHE["fused"] = Prog(n_layers=DEPTH, final=True)
    return _PROG_CACHE["fused"]


def kernel(**inp):
    inp = {k: np.asarray(v) for k, v in inp.items()}
    x = inp["x"]
    XT = np.ascontiguousarray(x.transpose(0, 2, 1))
    dist = make_dist()
    prog = get_prog()
    pp = make_pp(inp, list(range(DEPTH)))
    n1g = np.ascontiguousarray(np.concatenate([inp["norm1_g"][l].reshape(16, 128).T for l in range(DEPTH)], axis=1)).astype(np.float32)
    wmaps = {f"{wn}{l}": np.ascontiguousarray(inp[wn][l]) for wn in WNAMES for l in range(DEPTH)}
    in_maps = []
    for c in range(NCORE):
        d = dict(wmaps)
        d["xT"] = slab_T(XT, c, DEPTH)
        d["pp"] = pp
        d["kb"] = make_kb(c, DEPTH)
        d["vmask"] = make_vmask(c, DEPTH)
        d["dist"] = dist
        d["n1g"] = n1g
        d["ident"] = np.eye(128, dtype=np.float32)
        in_maps.append(d)
    res = run_bass_kernel_spmd(prog.nc, in_maps, core_ids=list(range(NCORE)))
    YT = np.empty((NBATCH, D, S), np.float32)
    for c in range(NCORE):
        b, s0 = c // 4, (c % 4) * NTOK
        YT[b][:, s0:s0 + NTOK] = res.results[c]["yo"]
    return np.ascontiguousarray(YT.transpose(0, 2, 1)).astype(np.float32)
```

```python
import numpy as np
import ml_dtypes
import concourse.bass as bass
import concourse.mybir as mybir
from concourse.bass_utils import run_bass_kernel_spmd

F32 = mybir.dt.float32
BF16 = mybir.dt.bfloat16
AF = mybir.ActivationFunctionType
ALU = mybir.AluOpType

D = 2048
S = 16384
NBATCH = 2
DEPTH = 4
NCORE = 8
NTOK = 4096
HALO = 1024
NLOC = NTOK + 2 * HALO
T = 512
HC = 16
TW = T + 2 * HC
NIN = 11008
DFF = 5632
OQ, OKK, OV, OG = 2560, 3328, 4096, 4864
RS = (1, 4, 16)
N_HEADS = 12
P_N1G, P_N2G, P_FG, P_BG, P_CAW, P_CAB, P_LNG, P_LNB, P_CBW, NPP = 0, 16, 32, 48, 96, 220, 224, 228, 232, 244
NKB = 9

WNAMES = ("w_in", "w_a_out", "w_b_out", "w_c_out", "w_o", "w_ffn_gate", "w_ffn_up", "w_ffn_down")
WSHAPES = {"w_in": (D, NIN), "w_a_out": (512, D), "w_b_out": (512, D), "w_c_out": (256, D),
           "w_o": (D, D), "w_ffn_gate": (D, DFF), "w_ffn_up": (D, DFF), "w_ffn_down": (DFF, D)}


def alibi_slopes(n):
    return (2.0 ** (-8.0 * np.arange(1, n + 1) / n)).astype(np.float64)


class Res:
    __slots__ = ("name", "w", "r", "dsem", "dcount")

    def __init__(self, name):
        self.name = name
        self.w = None
        self.r = {}
        self.dsem = None
        self.dcount = 0


class Eng:
    def __init__(self, name, h, sem, is_pe=False):
        self.name = name
        self.h = h
        self.sem = sem
        self.is_pe = is_pe
        self.n = 0
        self.seen = {}

    def wait(self, tok):
        if tok is None:
            return
        sem, val = tok
        k = id(sem)
        if self.seen.get(k, 0) >= val:
            return
        self.h.wait_ge(sem, val)
        self.seen[k] = val


class Prog:
    def __init__(self, n_layers=1, final=True, dbg_tile=None, stop_after=None, max_main=None, max_pre=None):
        self.L = n_layers
        self.NLOC = NTOK + 2 * HALO * n_layers
        self.NT = self.NLOC // T
        self.max_main = max_main
        self.max_pre = max_pre
        self.final = final
        self.dbg_tile = dbg_tile
        self.stop_after = stop_after
        self.dbg_outs = []
        self.nc = bass.Bass("TRN2", target_bir_lowering=False)
        self._build()

    def op(self, E, fn, rd=(), wr=()):
        own = E.sem
        for r in rd:
            t = r.w
            if t is not None and not (E.is_pe and t[0] is own):
                E.wait(t)
        for w in wr:
            t = w.w
            if t is not None and not (E.is_pe and t[0] is own):
                E.wait(t)
            for t in w.r.values():
                if not (E.is_pe and t[0] is own):
                    E.wait(t)
        ins = fn()
        E.n += 1
        ins.then_inc(E.sem, 1)
        tok = (E.sem, E.n)
        for w in wr:
            w.w = tok
            w.r = {}
        for r in rd:
            r.r[E.name] = tok
        return ins

    def dma(self, E, out_ap, in_ap, dst, src=None):
        if dst.dsem is None:
            dst.dsem = self.new_sem("d_" + dst.name)
        if src is not None and src.w is not None:
            E.wait(src.w)
        if dst.w is not None and dst.w[0] is not dst.dsem:
            E.wait(dst.w)
        for t in dst.r.values():
            E.wait(t)
        ins = E.h.dma_start(out=out_ap, in_=in_ap)
        dst.dcount += 16
        ins.then_inc(dst.dsem, 16)
        tok = (dst.dsem, dst.dcount)
        dst.w = tok
        dst.r = {}
        if src is not None:
            src.r["dma_" + dst.name] = tok
        self.dma_res[dst.name] = dst

    def new_sem(self, name):
        return self.stack.enter_context(self.nc.semaphore(name))

    def barrier(self, reset=(), with_sp=False):
        engs = [self.PE, self.ACT, self.DVE, self.POOL]
        for E in engs + ([self.SP] if with_sp else []):
            for Fg in engs:
                if Fg is not E and Fg.n > 0:
                    E.wait((Fg.sem, Fg.n))
            for r in self.dma_res.values():
                if r.name.startswith("slab") or r.name.startswith("wb_") or r.name.startswith("kvd") or r.name.startswith("xs"):
                    continue
                if r.dcount > 0:
                    E.wait((r.dsem, r.dcount))
        for r in reset:
            r.w = None
            r.r = {}

    def sb(self, name, shape, dt):
        return self.stack.enter_context(self.nc.sbuf_tensor(name, list(shape), dt))

    def carve(self, off, shape, dt):
        nfree = int(np.prod(shape[1:]))
        esz = 4 if dt == F32 else 2
        assert off % 4 == 0
        a = self.ov[:, off // 4: off // 4 + (nfree * esz + 3) // 4]
        if dt != F32:
            a = a.bitcast(dt)
        a = a[:, 0:nfree]
        if len(shape) == 3:
            a = a.rearrange("p (a b) -> p a b", a=shape[1])
        elif len(shape) == 4:
            a = a.rearrange("p (a b c) -> p a b c", a=shape[1], b=shape[2])
        return a

    def dump(self, name, ap, res, tile):
        if self.dbg_tile is None or tile != self.dbg_tile:
            return
        shape = list(ap.shape)
        o = self.nc.dram_tensor("dbg_" + name, shape, ap.dtype, kind="ExternalOutput").ap()
        r = Res("dbg_" + name)
        self.dma(self.POOL, o, ap, r, res)
        self.dbg_outs.append(("dbg_" + name, r))

    def _build(self):
        import contextlib
        nc = self.nc
        L = self.L
        with contextlib.ExitStack() as stack:
            self.stack = stack
            self.dma_res = {}
            NLOC = self.NLOC
            self.xT = nc.dram_tensor("xT", [D, NLOC], F32, kind="ExternalInput").ap()
            self.xs = {0: self.xT}
            self.xs_res = {0: None}
            for l in range(1, L):
                self.xs[l] = nc.dram_tensor(f"xs{l}", [D, NLOC], F32, kind="Internal").ap()
                self.xs_res[l] = Res(f"xs{l}")
            self.vm_d = nc.dram_tensor("vmask", [128, NLOC], F32, kind="ExternalInput").ap()
            self.wf = {}
            self.wb = {}
            self.wb_res = {}
            for l in range(L):
                for wn in WNAMES:
                    sh = list(WSHAPES[wn])
                    self.wf[(l, wn)] = nc.dram_tensor(f"{wn}{l}", sh, F32, kind="ExternalInput").ap()
                    self.wb[(l, wn)] = nc.dram_tensor(f"wb_{wn}{l}", sh, BF16, kind="Internal").ap()
                    if wn == WNAMES[0]:
                        self.wb_layer_res = getattr(self, "wb_layer_res", {})
                        self.wb_layer_res[l] = Res(f"wb_L{l}")
                    self.wb_res[(l, wn)] = self.wb_layer_res[l]
            self.pp_d = nc.dram_tensor("pp", [128, L * NPP], F32, kind="ExternalInput").ap()
            self.kb_d = nc.dram_tensor("kb", [128, self.NT * NKB], F32, kind="ExternalInput").ap()
            self.dist_d = nc.dram_tensor("dist", [128, 256], F32, kind="ExternalInput").ap()
            self.xo_res = Res("xo")
            if not self.final:
                self.xo = nc.dram_tensor("xo", [D, NTOK], F32, kind="ExternalOutput").ap()
            if self.final:
                self.yo = nc.dram_tensor("yo", [D, NTOK], F32, kind="ExternalOutput").ap()
                self.yo_res = Res("yo")
            self.kd2 = [[nc.dram_tensor(f"kd{par}_{g}", [256, RS[g], self.NLOC // RS[g]], BF16, kind="Internal").ap() for g in range(3)] for par in range(2)]
            self.vd2 = [[nc.dram_tensor(f"vd{par}_{g}", [self.NLOC, 256], BF16, kind="Internal").ap() for g in range(3)] for par in range(2)]
            self.kvd_res2 = [Res("kvd0"), Res("kvd1")]
            self.rs_d = [nc.dram_tensor(f"rs{par}", [1, self.NLOC], F32, kind="Internal").ap() for par in range(2)]
            self.rs_res = [Res("kvdrs0"), Res("kvdrs1")]
            self.kst_res = [Res("kst0"), Res("kst1")]
            self.vst_res = [Res("vst0"), Res("vst1")]
            self.n1g_d = nc.dram_tensor("n1g", [128, L * 16], F32, kind="ExternalInput").ap()
            self.ident_d = nc.dram_tensor("ident", [128, 128], F32, kind="ExternalInput").ap()
            self.win_res = Res("win")
            self.slabs = self.sb("slabs", [128, 3 * 8192], BF16)
            self.slab_res = [Res(f"slab{i}") for i in range(3)]
            self.slab_i = 0
            self.xt = self.sb("xt", [128, 16, TW], F32)
            self.xt_res = Res("xt")
            self.hT = self.sb("hT", [128, 16, TW], BF16)
            self.hT_res = [Res(f"hT{c}") for c in range(16)]
            self.pp = self.sb("pp_sb", [128, NPP], F32)
            self.kb = self.sb("kb_sb", [128, self.NT * NKB], F32)
            self.vm = self.sb("vm_sb", [128, TW], F32)
            self.vm_res = Res("vm")
            self.dist = self.sb("dist_sb", [128, 256], F32)
            self.n1g = self.sb("n1g_sb", [128, L * 16], F32)
            self.ident = self.sb("ident_sb", [128, 128], F32)
            self.const_res = Res("consts")
            self.onesD = self.sb("onesD", [128, 128], BF16)
            self.ones5 = self.sb("ones5", [128, 128], BF16)
            self.ones1 = self.sb("ones1", [128, 128], BF16)
            self.rstd = self.sb("rstd", [128, TW], F32)
            self.rstd_res = Res("rstd")
            self.sq = self.sb("sq", [128, 2, TW], BF16)
            self.sq_res = [Res("sq0"), Res("sq1")]
            R13 = 46080
            R2 = 54272
            self.ov = self.sb("ov", [128, (R13 + R2) // 4], F32)
            o = 0
            self.sg = self.carve(o, [128, 4, TW], BF16); o += 4352
            self.gc = self.carve(o, [128, 4, TW], BF16); o += 4352
            self.glu = self.carve(o, [128, 4, TW], BF16); o += 4352
            self.gcu = self.carve(o, [128, 4, TW], BF16); o += 4352
            self.gb = self.carve(o, [128, 4, T], BF16); o += 4096
            self.q = self.carve(o, [128, 6, T], BF16); o += 6144
            self.a1 = self.carve(o, [128, 4, T], F32); o += 8192
            self.abc = self.carve(o, [128, 10, T], BF16); o += 10240
            assert o == R13
            self.asq = self.carve(8704, [128, 4, T], BF16)
            self.diag = self.carve(R13 - 2048, [128, 8, 128], BF16)
            self.diag_res = [Res(f"diag{i}") for i in range(8)]
            self.diag_i = 0
            self.fT = self.carve(0, [128, 44, T], BF16)
            self.yfin = self.carve(0, [128, 16, T], F32)
            self.lnm2 = self.carve(0, [128, T], F32)
            self.lnrs = self.carve(2048, [128, T], F32)
            self.a1b = self.carve(4352, [128, 4, T], BF16)
            o = R13
            self.kw0 = self.carve(o, [128, 2, 640], BF16); o += 2560
            self.kw1 = self.carve(o, [128, 2, 4, 256], BF16); o += 4096
            self.kw2 = self.carve(o, [128, 2, 16, 160], BF16); o += 10240
            self.vw0 = self.carve(o, [128, 5, 256], BF16); o += 2560
            self.vw1 = self.carve(o, [128, 2, 4, 256], BF16); o += 4096
            self.vw2a = self.carve(o, [128, 16, 256], BF16); o += 8192
            self.vw2b = self.carve(o, [128, 16, 256], BF16); o += 8192
            self.sS = self.carve(o, [128, 2, 512], F32); o += 4096
            self.pT = self.carve(o, [128, 2, 512], BF16); o += 2048
            self.numacc = self.carve(o, [128, 2, 512], F32); o += 4096
            self.denacc = self.carve(o, [128, 2, 512], F32); o += 4096
            assert o == R13 + R2
            self.x2 = self.carve(R13, [128, 16, T], F32)
            self.x2_res = Res("x2")
            self.h2 = self.carve(R13 + 32768, [128, 16, T], BF16)
            self.h2_res = [Res(f"h2_{c}") for c in range(16)]
            self.vm2 = self.carve(R13 + 32768 + 16384, [128, T], F32)
            self.vm2_res = Res("vm2")
            o = R13
            self.sgate = self.carve(o, [128, 4, T], F32); o += 8192
            self.tmp = self.carve(o, [128, 2, T], F32); o += 4096
            self.macc = self.carve(o, [128, 4, T], F32); o += 8192
            self.mT = self.carve(o, [128, 16, T], BF16); o += 16384
            o = 0
            self.kst = self.carve(o, [128, 2, 6, T], BF16); o += 12288
            self.vst = self.carve(o, [128, 2, 4, 768], BF16); o += 12288
            self.ps = stack.enter_context(nc.psum_tensor("ps", [128, 4096], F32))
            self.ps_res = [Res(f"ps{b}") for b in range(8)]
            self.bank_i = 0
            self.PE = Eng("pe", nc.tensor, self.new_sem("s_pe"), is_pe=True)
            self.ACT = Eng("act", nc.scalar, self.new_sem("s_act"))
            self.DVE = Eng("dve", nc.vector, self.new_sem("s_dve"))
            self.POOL = Eng("pool", nc.gpsimd, self.new_sem("s_pool"))
            self.SP = Eng("sp", nc.sync, self.new_sem("s_sp"))
            self._emit()
            for r in [self.xo_res] + ([self.yo_res] if self.final else []) + [r for _, r in self.dbg_outs]:
                if r.dcount:
                    self.POOL.wait((r.dsem, r.dcount))
            self.barrier()

    def bank(self, b):
        return self.ps[:, b * 512:(b + 1) * 512]

    def next_bank(self):
        b = self.bank_i
        self.bank_i = (self.bank_i + 1) % 4
        return b

    def _emit(self):
        nc = self.nc
        P = self.POOL
        self.dma(P, self.kb[:], self.kb_d, self.const_res)
        self.dma(P, self.dist[:], self.dist_d, self.const_res)
        self.dma(P, self.n1g[:], self.n1g_d, self.const_res)
        self.dma(P, self.ident[:], self.ident_d, self.const_res)
        cres = Res("ones")
        self.op(self.DVE, lambda: nc.vector.memset(self.onesD[:], 1.0 / D), wr=[cres])
        self.op(self.DVE, lambda: nc.vector.memset(self.ones5[:], 1.0 / 512), wr=[cres])
        self.op(self.DVE, lambda: nc.vector.memset(self.ones1[:], 1.0), wr=[cres])
        self.ones_res = cres
        self.cast_weights(0)
        for l in range(self.L):
            self.layer(l)

    def cast_weights(self, l):
        P = self.POOL
        for wn in WNAMES:
            K, N = WSHAPES[wn]
            src = self.wf[(l, wn)]
            dst = self.wb[(l, wn)]
            res = self.wb_res[(l, wn)]
            rows = 512 if K >= 512 else K
            if l == 0 and wn == "w_in":
                self.wb_kv0_res = Res("wb_kv0")
                for k0 in range(0, K, rows):
                    self.dma(P, dst[k0:k0 + rows, OKK:OV + 768], src[k0:k0 + rows, OKK:OV + 768], self.wb_kv0_res)
                for k0 in range(0, K, rows):
                    self.dma(P, dst[k0:k0 + rows, 0:OKK], src[k0:k0 + rows, 0:OKK], res)
                    self.dma(P, dst[k0:k0 + rows, OV + 768:N], src[k0:k0 + rows, OV + 768:N], res)
                continue
            for k0 in range(0, K, rows):
                self.dma(P, dst[k0:k0 + rows, :], src[k0:k0 + rows, :], res)

    def load_slab(self, parts):
        i = self.slab_i
        self.slab_i = (i + 1) % 3
        sres = self.slab_res[i]
        base = i * 8192
        views = []
        off = 0
        for (wres, w, k0, kc, n0, nw) in parts:
            v = self.slabs[:, base + off: base + off + kc * nw].rearrange("p (c n) -> p c n", c=kc)
            src = w[k0 * 128:(k0 + kc) * 128, n0:n0 + nw].rearrange("(c p) n -> p c n", p=128)
            self.dma(self.SP, v, src, sres, wres)
            views.append(v)
            off += kc * nw
        assert off <= 8192
        return views, sres

    def rmsnorm(self, l, gcol, c0, ncol, out_bf16=True, out_ap=None, out_res=None, mask=False, gsrc=None, xb=None, hb=None, vmb=None, rs_load=None, rs_store=None):
        nc = self.nc
        xt, xt_res = xb if xb is not None else (self.xt, self.xt_res)
        hT, hT_res = hb if hb is not None else (self.hT, self.hT_res)
        vm, vm_res = vmb if vmb is not None else (self.vm, self.vm_res)
        if rs_load is not None:
            self.dma(self.ACT, self.rstd[:, c0:c0 + ncol], rs_load[0].partition_broadcast(128), self.rstd_res, rs_load[1])
        segs = [(c0, ncol)] if ncol <= 512 else [(c0, ncol // 2), (c0 + ncol // 2, ncol - ncol // 2)]
        banks = [self.next_bank() for _ in segs] if rs_load is None else []
        for c in range(16 if rs_load is None else 0):
            sres = self.sq_res[c % 2]
            if c % 2 == 0:
                self.op(self.ACT, lambda c=c: nc.scalar.activation(out=self.sq[:, c % 2, c0:c0 + ncol], in_=xt[:, c, c0:c0 + ncol], func=AF.Square),
                        rd=[xt_res], wr=[sres])
            else:
                self.op(self.DVE, lambda c=c: nc.vector.tensor_tensor(out=self.sq[:, c % 2, c0:c0 + ncol], in0=xt[:, c, c0:c0 + ncol], in1=xt[:, c, c0:c0 + ncol], op=ALU.mult),
                        rd=[xt_res], wr=[sres])
            for (s0, sn), b in zip(segs, banks):
                self.op(self.PE, lambda c=c, s0=s0, sn=sn, b=b: nc.tensor.matmul(self.bank(b)[:, 0:sn], lhsT=self.onesD[:], rhs=self.sq[:, c % 2, s0:s0 + sn], start=(c == 0), stop=(c == 15)),
                        rd=[sres, self.ones_res], wr=[self.ps_res[b]])
        for (s0, sn), b in zip(segs, banks):
            self.op(self.ACT, lambda s0=s0, sn=sn, b=b: nc.scalar.activation(out=self.rstd[:, s0:s0 + sn], in_=self.bank(b)[:, 0:sn], func=AF.Sqrt, bias=self.eps6[:, 0:1], scale=1.0),
                    rd=[self.ps_res[b]], wr=[self.rstd_res])
        if rs_load is None:
            self.op(self.DVE, lambda: nc.vector.reciprocal(out=self.rstd[:, c0:c0 + ncol], in_=self.rstd[:, c0:c0 + ncol]), rd=[self.rstd_res], wr=[self.rstd_res])
        if mask and rs_load is None:
            self.op(self.DVE, lambda: nc.vector.tensor_tensor(out=self.rstd[:, c0:c0 + ncol], in0=self.rstd[:, c0:c0 + ncol], in1=vm[:, c0:c0 + ncol], op=ALU.mult), rd=[self.rstd_res, vm_res], wr=[self.rstd_res])
        if rs_store is not None:
            self.dma(self.POOL, rs_store[0], self.rstd[0:1, c0:c0 + ncol], rs_store[1], self.rstd_res)
        for c in range(16):
            if out_ap is None:
                o = hT[:, c, c0:c0 + ncol]
                ores = hT_res[c]
            else:
                o = out_ap[:, c, :]
                ores = out_res
            g = self.pp[:, gcol + c: gcol + c + 1] if gsrc is None else gsrc[:, c:c + 1]
            self.op(self.DVE, lambda o=o, c=c, g=g: nc.vector.scalar_tensor_tensor(out=o, in0=xt[:, c, c0:c0 + ncol], scalar=g, in1=self.rstd[:, c0:c0 + ncol], op0=ALU.mult, op1=ALU.mult),
                    rd=[xt_res, self.rstd_res, self.const_res], wr=[ores])

    def proj16(self, slabv, sres, ncols, src_fn, src_res, segs, evac):
        nc = self.nc
        for j in range(ncols // 128):
            for si, (s0, sn) in enumerate(segs):
                b = self.next_bank()
                for c in range(16):
                    self.op(self.PE, lambda c=c, j=j, s0=s0, sn=sn, b=b: nc.tensor.matmul(self.bank(b)[:, 0:sn], lhsT=slabv[:, c, j * 128:(j + 1) * 128], rhs=src_fn(c, s0, sn), start=(c == 0), stop=(c == 15)),
                            rd=[sres, src_res[c]], wr=[self.ps_res[b]])
                evac(j, si, b)

    def layer(self, l):
        nc = self.nc
        if not hasattr(self, "eps6"):
            self.eps6 = self.sb("eps6", [128, 2], F32)
            self.eps_res = Res("eps")
            self.op(self.DVE, lambda: nc.vector.memset(self.eps6[:, 0:1], 1e-6), wr=[self.eps_res])
            self.op(self.DVE, lambda: nc.vector.memset(self.eps6[:, 1:2], 1e-5), wr=[self.eps_res])
            self.barrier()
        self.barrier()
        self.dma(self.POOL, self.pp[:], self.pp_d[:, l * NPP:(l + 1) * NPP], self.const_res)
        if l == 0:
            self.prepass(l)
            self.barrier()
        if self.dbg_tile == "pre":
            for g in range(3):
                self.dump(f"kd{g}", self.kd[g], self.kvd_res, "pre")
                self.dump(f"vd{g}", self.vd[g], self.kvd_res, "pre")
        if self.stop_after == "prepass":
            return
        if l + 1 < self.L:
            self.cast_weights(l + 1)
        mt = list(range(2 * (l + 1), self.NT - 2 * (l + 1)))
        if self.max_main is not None:
            mt = mt[:self.max_main]
        for p in mt:
            self.main_tile(l, p)

    def prepass(self, l):
        nc = self.nc
        P = self.POOL
        self.kd, self.vd, self.kvd_res = self.kd2[l % 2], self.vd2[l % 2], self.kvd_res2[l % 2]
        wres = self.wb_kv0_res if l == 0 else self.wb_res[(l, "w_in")]
        w = self.wb[(l, "w_in")]
        wk = self.slabs[:, 0:12288].rearrange("p (c n) -> p c n", c=16)
        wv = self.slabs[:, 12288:24576].rearrange("p (c n) -> p c n", c=16)
        for sr in self.slab_res:
            if sr.w is not None:
                self.SP.wait(sr.w)
            for t in sr.r.values():
                self.SP.wait(t)
        self.dma(self.SP, wk, w[:, OKK:OKK + 768].rearrange("(c p) n -> p c n", p=128), self.slab_res[0], wres)
        self.dma(self.SP, wv, w[:, OV:OV + 768].rearrange("(c p) n -> p c n", p=128), self.slab_res[2], wres)
        self.slab_i = 0
        kst_res = self.kst_res
        vst_res = self.vst_res
        pts = list(range(2 * l, self.NT - 2 * l))
        if self.max_pre is not None:
            pts = pts[:self.max_pre]
        bufs = [((self.xt, self.xt_res), (self.hT, self.hT_res), (self.vm, self.vm_res)),
                ((self.x2, self.x2_res), (self.h2, self.h2_res), (self.vm2, self.vm2_res))]

        def front(i):
            p = pts[i]
            u0 = p * T
            (xb, xr), hb_, (vb, vr) = bufs[i % 2]
            self.dma(self.SP, xb[:, 0:8, 0:T], self.xs[l][0:1024, u0:u0 + T].rearrange("(c p) n -> p c n", p=128), xr, self.xs_res[l])
            self.dma(self.ACT, xb[:, 8:16, 0:T], self.xs[l][1024:2048, u0:u0 + T].rearrange("(c p) n -> p c n", p=128), xr, self.xs_res[l])
            self.dma(self.ACT, vb[:, 0:T], self.vm_d[:, u0:u0 + T], vr)
            self.rmsnorm(l, P_N1G, 0, T, mask=True, xb=(xb, xr), hb=hb_, vmb=(vb, vr), rs_store=(self.rs_d[l % 2][0:1, u0:u0 + T], self.rs_res[l % 2]))
        if pts:
            front(0)
        for i, p in enumerate(pts):
            u0 = p * T
            pb = p % 2
            if i + 1 < len(pts):
                front(i + 1)
            hT, hT_res = bufs[i % 2][1]
            for ch in range(6):
                g = ch // 2
                r = RS[g]
                b = self.next_bank()
                for c in range(16):
                    self.op(self.PE, lambda c=c, ch=ch, b=b: nc.tensor.matmul(self.bank(b), lhsT=wk[:, c, ch * 128:(ch + 1) * 128], rhs=hT[:, c, 0:T], start=(c == 0), stop=(c == 15)),
                            rd=self.slab_res + [hT_res[c]], wr=[self.ps_res[b]])
                o = self.kst[:, pb, ch, :].rearrange("p (r i) -> p r i", r=r)
                i_ = self.bank(b).rearrange("p (i r) -> p r i", r=r)
                E = self.ACT if ch % 2 == 0 else self.DVE
                if E is self.ACT:
                    self.op(E, lambda o=o, i_=i_: nc.scalar.copy(out=o, in_=i_), rd=[self.ps_res[b]], wr=[kst_res[pb]])
                else:
                    self.op(E, lambda o=o, i_=i_: nc.vector.tensor_copy(out=o, in_=i_), rd=[self.ps_res[b]], wr=[kst_res[pb]])
            for blk in range(4):
                for half in range(2):
                    b = self.next_bank()
                    for c in range(16):
                        self.op(self.PE, lambda c=c, blk=blk, half=half, b=b: nc.tensor.matmul(self.bank(b)[:, 0:384], lhsT=hT[:, c, blk * 128:(blk + 1) * 128], rhs=wv[:, c, half * 384:(half + 1) * 384], start=(c == 0), stop=(c == 15)),
                                rd=self.slab_res + [hT_res[c]], wr=[self.ps_res[b]])
                    o = self.vst[:, pb, blk, half * 384:(half + 1) * 384]
                    if half == 0:
                        self.op(self.ACT, lambda o=o, b=b: nc.scalar.copy(out=o, in_=self.bank(b)[:, 0:384]), rd=[self.ps_res[b]], wr=[vst_res[pb]])
                    else:
                        self.op(self.DVE, lambda o=o, b=b: nc.vector.tensor_copy(out=o, in_=self.bank(b)[:, 0:384]), rd=[self.ps_res[b]], wr=[vst_res[pb]])
            for g in range(3):
                r = RS[g]
                for chl in range(2):
                    dst = self.kd[g][chl * 128:(chl + 1) * 128, :, u0 // r:(u0 + T) // r]
                    srcv = self.kst[:, pb, 2 * g + chl, :].rearrange("p (r i) -> p r i", r=r)
                    self.dma(P, dst, srcv, self.kvd_res, kst_res[pb])
                dst = self.vd[g][u0:u0 + T, :].rearrange("(b p) n -> p b n", p=128)
                self.dma(P, dst, self.vst[:, pb, :, g * 256:(g + 1) * 256], self.kvd_res, vst_res[pb])

    def main_tile(self, l, m):
        nc = self.nc
        P = self.POOL
        PE, ACT, DVE = self.PE, self.ACT, self.DVE
        ppo = 0
        t0 = m * T
        self.kd, self.vd, self.kvd_res = self.kd2[l % 2], self.vd2[l % 2], self.kvd_res2[l % 2]
        w_in = self.wb[(l, "w_in")]
        w_in_res = self.wb_res[(l, "w_in")]
        sg_res = [Res(f"sg{j}") for j in range(4)]
        gc_res = [Res(f"gc{j}") for j in range(4)]
        glu_res = [Res(f"glu{j}") for j in range(4)]
        gcu_res = [Res(f"gcu{j}") for j in range(4)]
        gb_res = [Res(f"gb{j}") for j in range(4)]
        q_res = [Res(f"q{j}") for j in range(6)]
        a1_res = [Res(f"a1{j}") for j in range(4)]
        asq_res = [Res(f"asq{j}") for j in range(4)]
        a1b_res = [Res(f"a1b{j}") for j in range(4)]
        abc_res = [Res(f"abc{j}") for j in range(10)]
        win_res = self.win_res
        self.barrier(with_sp=True)
        P = self.SP
        self.dma(self.SP, self.xt[:, 0:8, :], self.xs[l][0:1024, t0 - HC:t0 + T + HC].rearrange("(c p) n -> p c n", p=128), self.xt_res, self.xs_res[l])
        self.dma(self.ACT, self.xt[:, 8:16, :], self.xs[l][1024:2048, t0 - HC:t0 + T + HC].rearrange("(c p) n -> p c n", p=128), self.xt_res, self.xs_res[l])
        self.dma(self.ACT, self.vm[:, :], self.vm_d[:, t0 - HC:t0 + T + HC], self.vm_res)
        def emit_windows():
            P = self.POOL
            i1 = t0 // 4
            i2 = t0 // 16
            self.dma(P, self.kw0[:], self.kd[0][:, 0, t0 - 64:t0 + 576].rearrange("(c p) n -> p c n", p=128), win_res, self.kvd_res)
            for chl in range(2):
                self.dma(P, self.kw1[:, chl], self.kd[1][chl * 128:(chl + 1) * 128, :, i1 - 64:i1 + 192], win_res, self.kvd_res)
                self.dma(P, self.kw2[:, chl], self.kd[2][chl * 128:(chl + 1) * 128, :, i2 - 64:i2 + 96], win_res, self.kvd_res)
            self.dma(P, self.vw0[:], self.vd[0][t0 - 64:t0 + 576, :].rearrange("(j p) n -> p j n", p=128), win_res, self.kvd_res)
            self.dma(P, self.vw1[:], self.vd[1][4 * (i1 - 64):4 * (i1 + 192), :].rearrange("(k p c) n -> p k c n", p=128, c=4), win_res, self.kvd_res)
            self.dma(P, self.vw2a[:], self.vd[2][16 * (i2 - 64):16 * (i2 + 64), :].rearrange("(p c) n -> p c n", c=16), win_res, self.kvd_res)
            self.dma(P, self.vw2b[0:32], self.vd[2][16 * (i2 + 64):16 * (i2 + 96), :].rearrange("(p c) n -> p c n", c=16), win_res, self.kvd_res)
        P = self.POOL
        self.rmsnorm(l, P_N1G, 0, TW, mask=True, rs_load=(self.rs_d[l % 2][0:1, t0 - HC:t0 + T + HC], self.rs_res[l % 2]))
        for E in (self.ACT, self.DVE, self.POOL, self.PE):
            for rr in self.kst_res + self.vst_res:
                for tk in rr.r.values():
                    E.wait(tk)
        self.dump("hT", self.hT[:], self.hT_res[15], m)
        segW = [(0, TW // 2), (TW // 2, TW // 2)]
        segT = [(HC, T)]
        src = lambda c, s0, sn: self.hT[:, c, s0:s0 + sn]

        def slab_in(n0, nw):
            v, r = self.load_slab([(w_in_res, w_in, 0, 16, n0, nw)])
            return v[0], r

        v, r = slab_in(512, 512)
        self.proj16(v, r, 512, src, self.hT_res, segW,
                    lambda j, si, b: self.op(ACT, lambda: nc.scalar.activation(out=self.sg[:, j, segW[si][0]:segW[si][0] + segW[si][1]], in_=self.bank(b)[:, 0:segW[si][1]], func=AF.Sigmoid), rd=[self.ps_res[b]], wr=[sg_res[j]]))
        self.POOL.wait((PE.sem, PE.n))
        emit_windows()
        v, r = slab_in(0, 512)
        self.proj16(v, r, 512, src, self.hT_res, segW,
                    lambda j, si, b: self.op(DVE, lambda: nc.vector.tensor_tensor(out=self.glu[:, j, segW[si][0]:segW[si][0] + segW[si][1]], in0=self.bank(b)[:, 0:segW[si][1]], in1=self.sg[:, j, segW[si][0]:segW[si][0] + segW[si][1]], op=ALU.mult), rd=[self.ps_res[b], sg_res[j]], wr=[glu_res[j]]))
        for j in range(4):
            b = self.next_bank()
            for k in range(31):
                slot = self.diag_i % 8
                self.diag_i += 1
                wcol = self.pp[:, ppo + P_CAW + k * 4 + j: ppo + P_CAW + k * 4 + j + 1]
                self.op(DVE, lambda: nc.vector.tensor_scalar(out=self.diag[:, slot, :], in0=self.ident[:], scalar1=wcol, scalar2=None, op0=ALU.mult), rd=[self.const_res], wr=[self.diag_res[slot]])
                self.op(PE, lambda: nc.tensor.matmul(self.bank(b), lhsT=self.diag[:, slot, :], rhs=self.glu[:, j, k + 1:k + 1 + T], start=(k == 0), stop=(k == 30)), rd=[self.diag_res[slot], glu_res[j]], wr=[self.ps_res[b]])
            bcol = self.pp[:, ppo + P_CAB + j: ppo + P_CAB + j + 1]
            self.op(ACT, lambda: nc.scalar.activation(out=self.a1[:, j, :], in_=self.bank(b), func=AF.Identity, bias=bcol, scale=1.0), rd=[self.ps_res[b], self.const_res], wr=[a1_res[j]])
        v, r = slab_in(1024, 512)
        self.proj16(v, r, 512, src, self.hT_res, segT,
                    lambda j, si, b: self.op(ACT, lambda: nc.scalar.copy(out=self.gb[:, j, :], in_=self.bank(b)), rd=[self.ps_res[b]], wr=[gb_res[j]]))
        bm = self.next_bank()
        bq = self.next_bank()
        for j in range(4):
            self.op(ACT, lambda: nc.scalar.activation(out=self.asq[:, j, :], in_=self.a1[:, j, :], func=AF.Square), rd=[a1_res[j]], wr=[asq_res[j]] + glu_res)
            self.op(ACT, lambda: nc.scalar.copy(out=self.a1b[:, j, :], in_=self.a1[:, j, :]), rd=[a1_res[j]], wr=[a1b_res[j]])
        for j in range(4):
            self.op(PE, lambda: nc.tensor.matmul(self.bank(bm), lhsT=self.ones5[:], rhs=self.a1b[:, j, :], start=(j == 0), stop=(j == 3)), rd=[a1b_res[j], self.ones_res], wr=[self.ps_res[bm]])
        for j in range(4):
            self.op(PE, lambda: nc.tensor.matmul(self.bank(bq), lhsT=self.ones5[:], rhs=self.asq[:, j, :], start=(j == 0), stop=(j == 3)), rd=[asq_res[j], self.ones_res], wr=[self.ps_res[bq]])
        ln_res = Res("lnm2")
        lr_res = Res("lnrs")
        self.op(ACT, lambda: nc.scalar.activation(out=self.lnm2, in_=self.bank(bm), func=AF.Square), rd=[self.ps_res[bm]], wr=[ln_res] + sg_res)
        self.op(DVE, lambda: nc.vector.tensor_tensor(out=self.lnrs, in0=self.bank(bq), in1=self.lnm2, op=ALU.subtract), rd=[self.ps_res[bq], ln_res], wr=[lr_res] + sg_res)
        self.op(ACT, lambda: nc.scalar.activation(out=self.lnrs, in_=self.lnrs, func=AF.Sqrt, bias=self.eps6[:, 1:2], scale=1.0), rd=[lr_res, self.eps_res], wr=[lr_res])
        self.op(DVE, lambda: nc.vector.reciprocal(out=self.lnrs, in_=self.lnrs), rd=[lr_res], wr=[lr_res])
        for j in range(4):
            self.op(DVE, lambda: nc.vector.tensor_tensor(out=self.a1[:, j, :], in0=self.a1[:, j, :], in1=self.bank(bm), op=ALU.subtract), rd=[a1_res[j], self.ps_res[bm]], wr=[a1_res[j]])
            self.op(DVE, lambda: nc.vector.tensor_tensor(out=self.a1[:, j, :], in0=self.a1[:, j, :], in1=self.lnrs, op=ALU.mult), rd=[a1_res[j], lr_res], wr=[a1_res[j]])
            gcol = self.pp[:, ppo + P_LNG + j: ppo + P_LNG + j + 1]
            bcol = self.pp[:, ppo + P_LNB + j: ppo + P_LNB + j + 1]
            self.op(ACT, lambda: nc.scalar.activation(out=self.abc[:, j, :], in_=self.a1[:, j, :], func=AF.Silu, bias=bcol, scale=gcol), rd=[a1_res[j], self.const_res], wr=[abc_res[j]])
        v, r = slab_in(1536, 512)
        self.proj16(v, r, 512, src, self.hT_res, segW,
                    lambda j, si, b: self.op(ACT, lambda: nc.scalar.copy(out=self.gc[:, j, segW[si][0]:segW[si][0] + segW[si][1]], in_=self.bank(b)[:, 0:segW[si][1]]), rd=[self.ps_res[b]], wr=[gc_res[j]] + a1b_res))
        v, r = slab_in(2048, 512)

        def u_evac(j, si, b):
            self.op(ACT, lambda: nc.scalar.copy(out=self.sg[:, j, segW[si][0]:segW[si][0] + segW[si][1]], in_=self.bank(b)[:, 0:segW[si][1]]), rd=[self.ps_res[b]], wr=[sg_res[j], ln_res, lr_res])
            if si == 1:
                self.op(self.POOL, lambda: nc.gpsimd.tensor_tensor(out=self.gcu[:, j, :], in0=self.sg[:, j, :], in1=self.gc[:, j, :], op=ALU.mult), rd=[sg_res[j], gc_res[j]], wr=[gcu_res[j]])
        self.proj16(v, r, 512, src, self.hT_res, segW, u_evac)

        for j in range(4):
            for k in range(3):
                wcol = self.pp[:, ppo + P_CBW + k * 4 + j: ppo + P_CBW + k * 4 + j + 1]
                xin = self.gcu[:, j, k + HC - 1:k + HC - 1 + T]
                if k == 0:
                    self.op(DVE, lambda: nc.vector.tensor_scalar(out=self.a1[:, j, :], in0=xin, scalar1=wcol, scalar2=None, op0=ALU.mult), rd=[gcu_res[j], self.const_res], wr=[a1_res[j]])
                else:
                    self.op(DVE, lambda: nc.vector.scalar_tensor_tensor(out=self.a1[:, j, :], in0=xin, scalar=wcol, in1=self.a1[:, j, :], op0=ALU.mult, op1=ALU.add), rd=[gcu_res[j], self.const_res, a1_res[j]], wr=[a1_res[j]])
            self.op(DVE, lambda: nc.vector.tensor_tensor(out=self.abc[:, 4 + j, :], in0=self.a1[:, j, :], in1=self.gb[:, j, :], op=ALU.mult), rd=[a1_res[j], gb_res[j]], wr=[abc_res[4 + j]])
        def q_evac(ch0):
            def f(j, si, b):
                ch = ch0 + j
                r_ = RS[ch // 2]
                o = self.q[:, ch, :].rearrange("p (r i) -> p r i", r=r_)
                i_ = self.bank(b).rearrange("p (i r) -> p r i", r=r_)
                self.op(ACT, lambda: nc.scalar.copy(out=o, in_=i_), rd=[self.ps_res[b]], wr=[q_res[ch]])
            return f
        v, r = slab_in(OQ, 512)
        self.proj16(v, r, 512, src, self.hT_res, segT, q_evac(0))
        v, r = slab_in(OQ + 512, 256)
        self.proj16(v, r, 256, src, self.hT_res, segT, q_evac(4))
        self.dump("glu", self.glu[:], glu_res[3], m)
        self.dump("gcu", self.gcu[:], gcu_res[3], m)
        self.dump("q", self.q[:], q_res[5], m)
        if self.stop_after == "stage1":
            return

        self.dump("aconv", self.a1[:], a1_res[3], m)
        self.dump("abT", self.abc[:, 0:8, :], abc_res[7], m)
        if self.stop_after == "branches":
            return
        self.attention(m, q_res, win_res, abc_res)
        self.dump("cT", self.abc[:, 8:10, :], abc_res[9], m)
        if self.stop_after == "attn":
            return
        for E in (ACT, DVE, self.POOL):
            for tk in win_res.r.values():
                E.wait(tk)
        sgate_res = [Res(f"sgate{j}") for j in range(4)]
        tmp_res = [Res("tmp0"), Res("tmp1")]
        macc_res = [Res(f"macc{j}") for j in range(4)]
        mT_res = [Res(f"mT{j}") for j in range(16)]
        wo = {k: (self.wb_res[(l, k)], self.wb[(l, k)]) for k in WNAMES}
        ti = 0
        for jb in range(4):
            for br in range(3):
                v, r = slab_in(OG + br * D + jb * 512, 512)

                def gate_evac(j, si, b, br=br, jb=jb):
                    bcol = self.pp[:, ppo + P_BG + br * 16 + jb * 4 + j: ppo + P_BG + br * 16 + jb * 4 + j + 1]
                    self.op(ACT, lambda: nc.scalar.activation(out=self.sgate[:, j, :], in_=self.bank(b), func=AF.Sigmoid, bias=bcol, scale=1.0), rd=[self.ps_res[b], self.const_res], wr=[sgate_res[j]])
                self.proj16(v, r, 512, src, self.hT_res, segT, gate_evac)
                kcs = (4, 4, 2)[br]
                base = (0, 4, 8)[br]
                wbn = ("w_a_out", "w_b_out", "w_c_out")[br]
                vb_, rbr = self.load_slab([(wo[wbn][0], wo[wbn][1], 0, kcs, jb * 512, 512)])
                vbr = {br: vb_[0]}
                for j in range(4):
                    b = self.next_bank()
                    for c in range(kcs):
                        self.op(PE, lambda: nc.tensor.matmul(self.bank(b), lhsT=vbr[br][:, c, j * 128:(j + 1) * 128], rhs=self.abc[:, base + c, :], start=(c == 0), stop=(c == kcs - 1)), rd=[rbr, abc_res[base + c]], wr=[self.ps_res[b]])
                    if br == 0:
                        self.op(DVE, lambda: nc.vector.tensor_tensor(out=self.macc[:, j, :], in0=self.bank(b), in1=self.sgate[:, j, :], op=ALU.mult), rd=[self.ps_res[b], sgate_res[j]], wr=[macc_res[j]])
                    else:
                        tt = ti % 2
                        ti += 1
                        self.op(DVE, lambda: nc.vector.tensor_tensor(out=self.tmp[:, tt, :], in0=self.bank(b), in1=self.sgate[:, j, :], op=ALU.mult), rd=[self.ps_res[b], sgate_res[j]], wr=[tmp_res[tt]])
                        if br == 1:
                            self.op(self.POOL, lambda: nc.gpsimd.tensor_tensor(out=self.macc[:, j, :], in0=self.macc[:, j, :], in1=self.tmp[:, tt, :], op=ALU.add), rd=[macc_res[j], tmp_res[tt]], wr=[macc_res[j]])
                        else:
                            self.op(self.POOL, lambda: nc.gpsimd.tensor_tensor(out=self.mT[:, jb * 4 + j, :], in0=self.macc[:, j, :], in1=self.tmp[:, tt, :], op=ALU.add), rd=[macc_res[j], tmp_res[tt]], wr=[mT_res[jb * 4 + j]])
        self.dump("mT", self.mT[:], mT_res[15], m)
        if self.stop_after == "stage2":
            return
        msrc = lambda c, s0, sn: self.mT[:, c, :]
        for jb in range(4):
            v, r = self.load_slab([(wo["w_o"][0], wo["w_o"][1], 0, 16, jb * 512, 512)])
            self.proj16(v[0], r, 512, msrc, mT_res, [(0, T)],
                        lambda j, si, b, jb=jb: self.op(DVE, lambda: nc.vector.tensor_tensor(out=self.xt[:, jb * 4 + j, HC:HC + T], in0=self.xt[:, jb * 4 + j, HC:HC + T], in1=self.bank(b), op=ALU.add), rd=[self.ps_res[b], self.xt_res], wr=[self.xt_res]))
        self.dump("x1", self.xt[:, :, HC:HC + T], self.xt_res, m)
        if self.stop_after == "wo":
            return
        self.barrier()
        self.rmsnorm(l, P_N2G, HC, T)
        fT_res = [Res(f"fT{j}") for j in range(44)]
        sgate_res = [Res(f"sgateF{j}") for j in range(4)]
        wg_r, wg = wo["w_ffn_gate"]
        wu_r, wu = wo["w_ffn_up"]
        wd_r, wd = wo["w_ffn_down"]
        for fb in range(11):
            v, r = self.load_slab([(wg_r, wg, 0, 16, fb * 512, 512)])
            self.proj16(v[0], r, 512, src, self.hT_res, segT,
                        lambda j, si, b: self.op(ACT, lambda: nc.scalar.activation(out=self.sgate[:, j, :], in_=self.bank(b), func=AF.Silu), rd=[self.ps_res[b]], wr=[sgate_res[j]]))
            v, r = self.load_slab([(wu_r, wu, 0, 16, fb * 512, 512)])
            self.proj16(v[0], r, 512, src, self.hT_res, segT,
                        lambda j, si, b, fb=fb: self.op(DVE, lambda: nc.vector.tensor_tensor(out=self.fT[:, fb * 4 + j, :], in0=self.bank(b), in1=self.sgate[:, j, :], op=ALU.mult), rd=[self.ps_res[b], sgate_res[j]], wr=[fT_res[fb * 4 + j]]))
        for jb in range(4):
            for kq in range(4):
                v, r = self.load_slab([(wd_r, wd, kq * 11, 11, jb * 512, 512)])
                for j in range(4):
                    for c in range(11):
                        self.op(PE, lambda: nc.tensor.matmul(self.bank(j), lhsT=v[0][:, c, j * 128:(j + 1) * 128], rhs=self.fT[:, kq * 11 + c, :], start=(kq == 0 and c == 0), stop=(kq == 3 and c == 10)), rd=[r, fT_res[kq * 11 + c]], wr=[self.ps_res[j]])
            for j in range(4):
                self.op(DVE, lambda: nc.vector.tensor_tensor(out=self.xt[:, jb * 4 + j, HC:HC + T], in0=self.xt[:, jb * 4 + j, HC:HC + T], in1=self.bank(j), op=ALU.add), rd=[self.ps_res[j], self.xt_res], wr=[self.xt_res])
        self.bank_i = 0
        mo = m - 2 * self.L
        if l + 1 < self.L:
            self.dma(P, self.xs[l + 1][:, t0:t0 + T].rearrange("(c p) n -> p c n", p=128), self.xt[:, :, HC:HC + T], self.xs_res[l + 1], self.xt_res)
        elif not self.final:
            self.dma(P, self.xo[:, mo * T:(mo + 1) * T].rearrange("(c p) n -> p c n", p=128), self.xt[:, :, HC:HC + T], self.xo_res, self.xt_res)
        if l + 1 < self.L:
            self.kv_next(l + 1, m)
        if self.final and l == self.L - 1:
            self.barrier()
            yres = Res("yfin")
            self.rmsnorm(l, P_FG, HC, T, out_ap=self.yfin, out_res=yres)
            self.dma(P, self.yo[:, mo * T:(mo + 1) * T].rearrange("(c p) n -> p c n", p=128), self.yfin[:], self.yo_res, yres)

    def kv_next(self, ln, p):
        nc = self.nc
        P = self.POOL
        PE, ACT, DVE = self.PE, self.ACT, self.DVE
        kd, vd, kvres = self.kd2[ln % 2], self.vd2[ln % 2], self.kvd_res2[ln % 2]
        w = self.wb[(ln, "w_in")]
        wres = self.wb_res[(ln, "w_in")]
        u0 = p * T
        pb = p % 2
        self.rmsnorm(ln, 0, HC, T, mask=True, gsrc=self.n1g[:, ln * 16:(ln + 1) * 16], rs_store=(self.rs_d[ln % 2][0:1, u0:u0 + T], self.rs_res[ln % 2]))
        src = lambda c, s0, sn: self.hT[:, c, s0:s0 + sn]

        def k_evac(ch0):
            def f(j, si, b):
                ch = ch0 + j
                r_ = RS[ch // 2]
                o = self.kst[:, pb, ch, :].rearrange("p (r i) -> p r i", r=r_)
                i_ = self.bank(b).rearrange("p (i r) -> p r i", r=r_)
                if j % 2 == 0:
                    self.op(ACT, lambda: nc.scalar.copy(out=o, in_=i_), rd=[self.ps_res[b]], wr=[self.kst_res[pb]])
                else:
                    self.op(DVE, lambda: nc.vector.tensor_copy(out=o, in_=i_), rd=[self.ps_res[b]], wr=[self.kst_res[pb]])
            return f
        v, r = self.load_slab([(wres, w, 0, 16, OKK, 512)])
        self.proj16(v[0], r, 512, src, self.hT_res, [(HC, T)], k_evac(0))
        v, r = self.load_slab([(wres, w, 0, 16, OKK + 512, 256)])
        self.proj16(v[0], r, 256, src, self.hT_res, [(HC, T)], k_evac(4))
        for half in range(2):
            v, r = self.load_slab([(wres, w, 0, 16, OV + half * 384, 384)])
            for blk in range(4):
                b = self.next_bank()
                for c in range(16):
                    self.op(PE, lambda: nc.tensor.matmul(self.bank(b)[:, 0:384], lhsT=self.hT[:, c, HC + blk * 128:HC + (blk + 1) * 128], rhs=v[0][:, c, :], start=(c == 0), stop=(c == 15)),
                            rd=[r, self.hT_res[c]], wr=[self.ps_res[b]])
                o = self.vst[:, pb, blk, half * 384:(half + 1) * 384]
                if blk % 2 == 0:
                    self.op(ACT, lambda: nc.scalar.copy(out=o, in_=self.bank(b)[:, 0:384]), rd=[self.ps_res[b]], wr=[self.vst_res[pb]])
                else:
                    self.op(DVE, lambda: nc.vector.tensor_copy(out=o, in_=self.bank(b)[:, 0:384]), rd=[self.ps_res[b]], wr=[self.vst_res[pb]])
        for g in range(3):
            r_ = RS[g]
            for chl in range(2):
                dst = kd[g][chl * 128:(chl + 1) * 128, :, u0 // r_:(u0 + T) // r_]
                srcv = self.kst[:, pb, 2 * g + chl, :].rearrange("p (r i) -> p r i", r=r_)
                self.dma(P, dst, srcv, kvres, self.kst_res[pb])
            dst = vd[g][u0:u0 + T, :].rearrange("(b p) n -> p b n", p=128)
            self.dma(P, dst, self.vst[:, pb, :, g * 256:(g + 1) * 256], kvres, self.vst_res[pb])

    def attention(self, m, q_res, win_res, abc_res):
        nc = self.nc
        PE, ACT, DVE = self.PE, self.ACT, self.DVE
        slopes = alibi_slopes(N_HEADS)
        num_res = Res("numacc")
        den_res = Res("denacc")
        sS_res = [Res("sS0"), Res("sS1")]
        pT_res = [Res("pT0"), Res("pT1")]
        BN, BD = 6, 7
        sb_i = [0]
        pending = []

        def flush():
            for f in pending:
                f()
            pending.clear()

        def after_block(ops):
            flush()
            pending.extend(ops)

        def score_block(lhs_list, rhs_list, ncols, dist_ap, kbcol, nrows, coef, qres):
            i = sb_i[0] % 2
            sb_i[0] += 1
            b = 4 + i
            col = 0
            for lh, rh in zip(lhs_list, rhs_list):
                n = rh.shape[-1]
                self.op(PE, lambda lh=lh, rh=rh, col=col, n=n: nc.tensor.matmul(self.bank(b)[0:nrows, col:col + n], lhsT=lh, rhs=rh, start=True, stop=True), rd=[win_res, qres], wr=[self.ps_res[b]])
                col += n
            assert col == ncols
            ps_ap = self.bank(b)[0:nrows, 0:ncols]
            ss = self.sS[0:nrows, i, 0:ncols]
            if len(dist_ap.shape) == 3:
                ps_ap = ps_ap.rearrange("p (a b) -> p a b", a=dist_ap.shape[1])
                ss3 = ss.rearrange("p (a b) -> p a b", a=dist_ap.shape[1])
            else:
                ss3 = ss
            self.op(DVE, lambda: nc.vector.scalar_tensor_tensor(out=ss3, in0=dist_ap, scalar=float(coef), in1=ps_ap, op0=ALU.mult, op1=ALU.add), rd=[self.ps_res[b], self.const_res], wr=[sS_res[i]])
            self.op(ACT, lambda: nc.scalar.activation(out=self.pT[0:nrows, i, 0:ncols], in_=ss, func=AF.Exp, bias=self.kb[0:nrows, kbcol:kbcol + 1], scale=0.125), rd=[sS_res[i], self.const_res], wr=[pT_res[i]])
            return i

        for g in range(3):
            r = RS[g]
            for hl in range(4):
                h = 4 * g + hl
                pair, par = hl // 2, hl % 2
                rows = slice(par * 64, par * 64 + 64)
                qch = 2 * g + pair
                coef = 8.0 * slopes[h] * r
                kb0 = m * NKB
                nb, db = self.bank(BN), self.bank(BD)
                if g == 0:
                    for j in range(5):
                        qc0, qc1 = max(0, 128 * (j - 1)), min(512, 128 * (j + 1))
                        d0 = 128 if j == 0 else 0
                        n = qc1 - qc0
                        i = score_block([self.kw0[rows, pair, 128 * j:128 * j + 128]], [self.q[rows, qch, qc0:qc1]], n, self.dist[:, d0:d0 + n], kb0 + j, 128, coef, q_res[qch])
                        if hl == 0 and j == 1:
                            self.dump("sS", self.sS[:, i, 0:256], sS_res[i], m)
                            self.dump("pT", self.pT[:, i, 0:256], pT_res[i], m)
                            self.dump("kw0", self.kw0[:], win_res, m)
                            self.dump("vw0", self.vw0[:], win_res, m)
                            self.dump("vw1", self.vw1[:], win_res, m)
                            self.dump("vw2a", self.vw2a[:], win_res, m)
                            self.dump("vw2b", self.vw2b[0:32], win_res, m)
                        ops = []
                        for qb in range(qc0 // 128, qc1 // 128):
                            pcol = qb * 128 - qc0
                            first = (j == 0)
                            ops.append(lambda qb=qb, pcol=pcol, first=first, j=j, i=i, pair=pair: self.op(PE, lambda: nc.tensor.matmul(nb[:, qb * 128:(qb + 1) * 128], lhsT=self.vw0[:, j, pair * 128:(pair + 1) * 128], rhs=self.pT[:, i, pcol:pcol + 128], start=first, stop=not first), rd=[pT_res[i], win_res], wr=[self.ps_res[BN]]))
                            ops.append(lambda qb=qb, pcol=pcol, first=first, i=i: self.op(PE, lambda: nc.tensor.matmul(db[:, qb * 128:(qb + 1) * 128], lhsT=self.ones1[:], rhs=self.pT[:, i, pcol:pcol + 128], start=first, stop=not first), rd=[pT_res[i], self.ones_res], wr=[self.ps_res[BD]]))
                        after_block(ops)
                else:
                    nres = r
                    nq = 512 // r
                    kw = self.kw1 if g == 1 else self.kw2
                    for kc in range(2):
                        if g == 1:
                            nrows = 128
                            kcols = slice(kc * 128, kc * 128 + 128)
                        else:
                            nrows = 128 if kc == 0 else 32
                            kcols = slice(0, 128) if kc == 0 else slice(128, 160)
                        d0 = 128 if kc == 0 else 0
                        dist_ap = self.dist[0:nrows, d0:d0 + nq].unsqueeze(1).to_broadcast([nrows, nres, nq])
                        lhs = [kw[rows, pair, c, kcols] for c in range(nres)]
                        rhs = [self.q[rows, qch, c * nq:(c + 1) * nq] for c in range(nres)]
                        i = score_block(lhs, rhs, 512, dist_ap, kb0 + (5 if g == 1 else 7) + kc, nrows, coef, q_res[qch])
                        ops = []
                        for c in range(nres):
                            if g == 1:
                                vl = self.vw1[:, kc, c, pair * 128:(pair + 1) * 128]
                            elif kc == 0:
                                vl = self.vw2a[:, c, pair * 128:(pair + 1) * 128]
                            else:
                                vl = self.vw2b[0:32, c, pair * 128:(pair + 1) * 128]
                            ops.append(lambda vl=vl, c=c, nq=nq, nrows=nrows, i=i, kc=kc: self.op(PE, lambda: nc.tensor.matmul(nb[:, c * nq:(c + 1) * nq], lhsT=vl, rhs=self.pT[0:nrows, i, c * nq:(c + 1) * nq], start=(kc == 0 and c == 0), stop=(kc == 1)), rd=[pT_res[i], win_res], wr=[self.ps_res[BN]]))
                        ops.append(lambda nrows=nrows, i=i, kc=kc: self.op(PE, lambda: nc.tensor.matmul(db, lhsT=self.ones1[0:nrows, :], rhs=self.pT[0:nrows, i, :], start=(kc == 0), stop=(kc == 1)), rd=[pT_res[i], self.ones_res], wr=[self.ps_res[BD]]))
                        after_block(ops)
                flush()
                if g == 0:
                    self.op(DVE, lambda: nc.vector.tensor_copy(out=self.numacc[rows, pair, :], in_=nb[rows, :]), rd=[self.ps_res[BN]], wr=[num_res])
                    self.op(ACT, lambda: nc.scalar.copy(out=self.denacc[rows, pair, :], in_=db[rows, :]), rd=[self.ps_res[BD]], wr=[den_res])
                else:
                    na = self.numacc[rows, pair, :].rearrange("p (i r) -> p r i", r=r)
                    da = self.denacc[rows, pair, :].rearrange("p (i r) -> p r i", r=r)
                    nps = nb[rows, :].rearrange("p (r i) -> p r i", r=r)
                    dps = db[rows, :].rearrange("p (r i) -> p r i", r=r)
                    self.op(DVE, lambda: nc.vector.tensor_tensor(out=na, in0=na, in1=nps, op=ALU.add), rd=[self.ps_res[BN], num_res], wr=[num_res])
                    self.op(DVE, lambda: nc.vector.tensor_tensor(out=da, in0=da, in1=dps, op=ALU.add), rd=[self.ps_res[BD], den_res], wr=[den_res])
            self.dump(f"denacc{g}", self.denacc[:], den_res, m)
            self.dump(f"numacc{g}", self.numacc[:], num_res, m)
        self.dump("numacc", self.numacc[:], num_res, m)
        for pair in range(2):
            self.op(DVE, lambda: nc.vector.tensor_scalar(out=self.denacc[:, pair, :], in0=self.denacc[:, pair, :], scalar1=1e-30, scalar2=None, op0=ALU.max), rd=[den_res], wr=[den_res])
            self.op(DVE, lambda: nc.vector.reciprocal(out=self.denacc[:, pair, :], in_=self.denacc[:, pair, :]), rd=[den_res], wr=[den_res])
            self.op(DVE, lambda: nc.vector.tensor_tensor(out=self.abc[:, 8 + pair, :], in0=self.numacc[:, pair, :], in1=self.denacc[:, pair, :], op=ALU.mult), rd=[num_res, den_res], wr=[abc_res[8 + pair]] + self.diag_res)


def make_pp(inp, layers):
    cols = []
    for l in layers:
        a = np.zeros((128, NPP), np.float32)
        a[:, P_N1G:P_N1G + 16] = inp["norm1_g"][l].reshape(16, 128).T
        a[:, P_N2G:P_N2G + 16] = inp["norm2_g"][l].reshape(16, 128).T
        a[:, P_FG:P_FG + 16] = inp["final_g"].reshape(16, 128).T
        a[:, P_BG:P_BG + 48] = inp["b_gate"][l].reshape(48, 128).T
        a[:, P_CAW:P_CAW + 124] = inp["conv_a_w"][l].reshape(31, 4, 128).transpose(2, 0, 1).reshape(128, 124)
        a[:, P_CAB:P_CAB + 4] = inp["conv_a_b"][l].reshape(4, 128).T
        a[:, P_LNG:P_LNG + 4] = inp["ln_a_g"][l].reshape(4, 128).T
        a[:, P_LNB:P_LNB + 4] = inp["ln_a_b"][l].reshape(4, 128).T
        a[:, P_CBW:P_CBW + 12] = inp["conv_b_w"][l].reshape(3, 4, 128).transpose(2, 0, 1).reshape(128, 12)
        cols.append(a)
    return np.ascontiguousarray(np.concatenate(cols, axis=1))


def make_dist():
    p = np.arange(128)[:, None].astype(np.float64)
    f = np.arange(256)[None, :].astype(np.float64)
    d = np.zeros((128, 256), np.float64)
    fb = f[:, :128]
    rel = p + 64 - fb
    d[:, :128] = np.where(p <= fb, -np.abs(rel), -1e6)
    rel = p - 64 - fb
    d[:, 128:] = np.where(p >= fb, -np.abs(rel), -1e6)
    return d.astype(np.float32)


def make_kb(core, L):
    s0 = (core % 4) * NTOK
    NT = (NTOK + 2 * HALO * L) // T
    kbm = np.zeros((128, NT * NKB), np.float32)
    p = np.arange(128)

    def val(u):
        a = s0 - HALO * L + u
        return np.where((a >= 0) & (a < S), 0.0, -30000.0).astype(np.float32)
    for m in range(NT):
        t0 = m * T
        for j in range(5):
            kbm[:, m * NKB + j] = val(t0 - 64 + 128 * j + p)
        i1 = t0 // 4
        kbm[:, m * NKB + 5] = val(4 * (i1 - 64 + p))
        kbm[:, m * NKB + 6] = val(4 * (i1 + 64 + p))
        i2 = t0 // 16
        kbm[:, m * NKB + 7] = val(16 * (i2 - 64 + p))
        kbm[:, m * NKB + 8] = val(16 * (i2 + 64 + p))
    return kbm


def make_vmask(core, L):
    s0 = (core % 4) * NTOK
    nloc = NTOK + 2 * HALO * L
    a = s0 - HALO * L + np.arange(nloc)
    v = ((a >= 0) & (a < S)).astype(np.float32)
    return np.ascontiguousarray(np.broadcast_to(v[None, :], (128, nloc)))


def slab_T(XT, core, L):
    b, s0 = core // 4, (core % 4) * NTOK
    h = HALO * L
    nloc = NTOK + 2 * h
    out = np.zeros((D, nloc), np.float32)
    lo, hi = max(0, s0 - h), min(S, s0 + NTOK + h)
    out[:, lo - (s0 - h):hi - (s0 - h)] = XT[b][:, lo:hi]
    return out


_PROG_CACHE = {}


def get_prog():
    if "fused" not in _PROG_CACHE:
        _PROG_CACHE["fused"] = Prog(n_layers=DEPTH, final=True)
    return _PROG_CACHE["fused"]


def kernel(**inp):
    inp = {k: np.asarray(v) for k, v in inp.items()}
    x = inp["x"]
    XT = np.ascontiguousarray(x.transpose(0, 2, 1))
    dist = make_dist()
    prog = get_prog()
    pp = make_pp(inp, list(range(DEPTH)))
    n1g = np.ascontiguousarray(np.concatenate([inp["norm1_g"][l].reshape(16, 128).T for l in range(DEPTH)], axis=1)).astype(np.float32)
    wmaps = {f"{wn}{l}": np.ascontiguousarray(inp[wn][l]) for wn in WNAMES for l in range(DEPTH)}
    in_maps = []
    for c in range(NCORE):
        d = dict(wmaps)
        d["xT"] = slab_T(XT, c, DEPTH)
        d["pp"] = pp
        d["kb"] = make_kb(c, DEPTH)
        d["vmask"] = make_vmask(c, DEPTH)
        d["dist"] = dist
        d["n1g"] = n1g
        d["ident"] = np.eye(128, dtype=np.float32)
        in_maps.append(d)
    res = run_bass_kernel_spmd(prog.nc, in_maps, core_ids=list(range(NCORE)))
    YT = np.empty((NBATCH, D, S), np.float32)
    for c in range(NCORE):
        b, s0 = c // 4, (c % 4) * NTOK
        YT[b][:, s0:s0 + NTOK] = res.results[c]["yo"]
    return np.ascontiguousarray(YT.transpose(0, 2, 1)).astype(np.float32)
```
